# Optimizing a Trainium2 kernel written in Bass

```python
import jax, jax.numpy as jnp
from jax import lax
import numpy as np

D_MODEL = 2048
BATCH = 4
SEQ = 4096
DEPTH = 2

N_MIXERS = 2
N_LAYERS_A = (DEPTH + 1) // 2
N_LAYERS_B = DEPTH // 2

H_A = 8
QK_W = D_MODEL // 2
V_W = D_MODEL
DQK = QK_W // H_A
DV = V_W // H_A
CHUNK = 128
IN_A = 2 * QK_W + 2 * V_W + 2 * H_A

D_RNN = D_MODEL
N_BLOCKS = 8
BLOCK_W = D_RNN // N_BLOCKS
CONV_W = 4
RG_C = 8.0

D_FF = 4 * D_MODEL

EPS = 1e-6

kernel_name = "hybrid_mlstm_rglru_adaln_trunk"


def rms_norm(x):
    xf = x.astype(jnp.float32)
    return (xf * lax.rsqrt(jnp.mean(xf * xf, axis=-1, keepdims=True) + EPS)).astype(x.dtype)


def ada_modulate(x, c, w, b):
    mod = jax.nn.silu(c) @ w + b
    shift, scale, gate = jnp.split(mod, 3, axis=-1)
    h = rms_norm(x) * (1.0 + scale[:, None, :]) + shift[:, None, :]
    return h, gate[:, None, :]


def mlstm_chunkwise(q, k, v, li, lf):
    B, H, S, _ = q.shape
    nc = S // CHUNK

    def chunks(t):
        return jnp.moveaxis(t.reshape(B, H, nc, CHUNK, *t.shape[3:]), 2, 0)

    tri = jnp.tril(jnp.ones((CHUNK, CHUNK), dtype=bool))

    def step(carry, xs):
        C, n, m = carry
        qc, kc, vc, lic, lfc = xs
        b = jnp.cumsum(lfc, axis=-1)
        d = b[..., :, None] - b[..., None, :] + lic[..., None, :]
        d = jnp.where(tri, d, -jnp.inf)
        inter = b + m[..., None]
        m_t = jnp.maximum(inter, jnp.max(d, axis=-1))
        w_inter = jnp.exp(inter - m_t)
        p = jnp.exp(d - m_t[..., None]) * jnp.einsum("bhtd,bhsd->bhts", qc, kc)
        num = (w_inter[..., None] * jnp.einsum("bhtd,bhde->bhte", qc, C)
               + jnp.einsum("bhts,bhse->bhte", p, vc))
        den = w_inter * jnp.einsum("bhtd,bhd->bht", qc, n) + jnp.sum(p, axis=-1)
        h = num / jnp.maximum(jnp.abs(den), jnp.exp(-m_t))[..., None]
        b_last = b[..., -1]
        g = b_last[..., None] - b + lic
        m_new = jnp.maximum(b_last + m, jnp.max(g, axis=-1))
        wk = jnp.exp(g - m_new[..., None])
        decay = jnp.exp(b_last + m - m_new)
        C_new = decay[..., None, None] * C + jnp.einsum("bhs,bhsd,bhse->bhde", wk, kc, vc)
        n_new = decay[..., None] * n + jnp.einsum("bhs,bhsd->bhd", wk, kc)
        return (C_new, n_new, m_new), h

    init = (jnp.zeros((B, H, DQK, DV), jnp.float32),
            jnp.zeros((B, H, DQK), jnp.float32),
            jnp.zeros((B, H), jnp.float32))
    _, hs = lax.scan(step, init, (chunks(q), chunks(k), chunks(v), chunks(li), chunks(lf)))
    return jnp.moveaxis(hs, 0, 2).reshape(B, H, S, DV)


def mlstm_mixer(h, w_in, b_gate, norm_g, w_out):
    B, S, _ = h.shape
    proj = h @ w_in
    q, k, v, o, ig, fg = jnp.split(
        proj, [QK_W, 2 * QK_W, 2 * QK_W + V_W, 2 * QK_W + 2 * V_W, 2 * QK_W + 2 * V_W + H_A], axis=-1)
    q = q.reshape(B, S, H_A, DQK).transpose(0, 2, 1, 3).astype(jnp.float32) * (DQK ** -0.5)
    k = k.reshape(B, S, H_A, DQK).transpose(0, 2, 1, 3).astype(jnp.float32)
    v = v.reshape(B, S, H_A, DV).transpose(0, 2, 1, 3).astype(jnp.float32)
    li = (ig.astype(jnp.float32) + b_gate[0].astype(jnp.float32)).transpose(0, 2, 1)
    lf = jax.nn.log_sigmoid(fg.astype(jnp.float32) + b_gate[1].astype(jnp.float32)).transpose(0, 2, 1)
    hh = mlstm_chunkwise(q, k, v, li, lf).transpose(0, 2, 1, 3)
    hh = hh * lax.rsqrt(jnp.mean(hh * hh, axis=-1, keepdims=True) + EPS)
    hh = hh.reshape(B, S, V_W) * norm_g.astype(jnp.float32)
    return (hh.astype(h.dtype) * jax.nn.sigmoid(o)) @ w_out


def _lin_combine(e1, e2):
    a1, b1 = e1
    a2, b2 = e2
    return a1 * a2, a2 * b1 + b2


def rglru_mixer(h, w_in, conv_w, conv_b, w_ra, b_ra, w_ri, b_ri, lam, w_out):
    B, S, _ = h.shape
    xb, gb = jnp.split(h @ w_in, 2, axis=-1)
    xb = lax.conv_general_dilated(
        xb, conv_w[:, None, :], window_strides=(1,), padding=[(CONV_W - 1, 0)],
        dimension_numbers=("NWC", "WIO", "NWC"), feature_group_count=D_RNN) + conv_b
    xf = xb.astype(jnp.float32)
    xblk = xf.reshape(B, S, N_BLOCKS, BLOCK_W)
    r = jax.nn.sigmoid(jnp.einsum("bsnc,ncd->bsnd", xblk, w_ra.astype(jnp.float32)).reshape(B, S, D_RNN)
                       + b_ra.astype(jnp.float32))
    i = jax.nn.sigmoid(jnp.einsum("bsnc,ncd->bsnd", xblk, w_ri.astype(jnp.float32)).reshape(B, S, D_RNN)
                       + b_ri.astype(jnp.float32))
    log_a = -RG_C * r * jax.nn.softplus(-lam.astype(jnp.float32))
    a = jnp.exp(log_a)
    u = jnp.sqrt(-jnp.expm1(2.0 * log_a)) * (i * xf)
    _, hs = lax.associative_scan(_lin_combine, (a, u), axis=1)
    y = hs.astype(h.dtype) * jax.nn.gelu(gb)
    return y @ w_out


def sq_relu_mlp(h, w1, w2):
    return jnp.square(jax.nn.relu(h @ w1)) @ w2


def setup_inputs(seed: int = 0) -> dict:
    key = jax.random.key(seed)
    ks = jax.random.split(key, 20)
    f32 = jnp.float32
    nrm = lambda k, shape, scale: jax.random.normal(k, shape, f32) * scale
    s_rg = jax.random.uniform(ks[16], (N_LAYERS_B, D_RNN), f32, 0.9, 0.999) ** (1.0 / RG_C)
    return {
        "x": nrm(ks[0], (BATCH, SEQ, D_MODEL), 1.0),
        "c": nrm(ks[1], (BATCH, D_MODEL), 1.0),
        "ada_w": nrm(ks[2], (DEPTH, 2, D_MODEL, 3 * D_MODEL), 0.5 * D_MODEL ** -0.5),
        "ada_b": nrm(ks[3], (DEPTH, 2, 3 * D_MODEL), 0.02),
        "a_w_in": nrm(ks[4], (N_LAYERS_A, D_MODEL, IN_A), D_MODEL ** -0.5),
        "a_b_gate": jnp.stack([
            nrm(ks[5], (N_LAYERS_A, H_A), 0.1),
            3.0 + 3.0 * jax.random.uniform(ks[6], (N_LAYERS_A, H_A), f32)], axis=1),
        "a_norm_g": 1.0 + nrm(ks[7], (N_LAYERS_A, V_W), 0.02),
        "a_w_out": nrm(ks[8], (N_LAYERS_A, V_W, D_MODEL), V_W ** -0.5),
        "b_w_in": nrm(ks[9], (N_LAYERS_B, D_MODEL, 2 * D_RNN), D_MODEL ** -0.5),
        "b_conv_w": nrm(ks[10], (N_LAYERS_B, CONV_W, D_RNN), CONV_W ** -0.5),
        "b_conv_b": nrm(ks[11], (N_LAYERS_B, D_RNN), 0.02),
        "b_w_ra": nrm(ks[12], (N_LAYERS_B, N_BLOCKS, BLOCK_W, BLOCK_W), BLOCK_W ** -0.5),
        "b_b_ra": nrm(ks[13], (N_LAYERS_B, D_RNN), 0.02),
        "b_w_ri": nrm(ks[14], (N_LAYERS_B, N_BLOCKS, BLOCK_W, BLOCK_W), BLOCK_W ** -0.5),
        "b_b_ri": nrm(ks[15], (N_LAYERS_B, D_RNN), 0.02),
        "b_lam": jnp.log(s_rg / (1.0 - s_rg)),
        "b_w_out": nrm(ks[17], (N_LAYERS_B, D_RNN, D_MODEL), D_RNN ** -0.5),
        "mlp_w1": nrm(ks[18], (DEPTH, D_MODEL, D_FF), D_MODEL ** -0.5),
        "mlp_w2": nrm(ks[19], (DEPTH, D_FF, D_MODEL), D_FF ** -0.5),
        "final_g": 1.0 + nrm(jax.random.fold_in(key, 99), (D_MODEL,), 0.02),
    }


def reference(x, c, ada_w, ada_b, a_w_in, a_b_gate, a_norm_g, a_w_out,
              b_w_in, b_conv_w, b_conv_b, b_w_ra, b_b_ra, b_w_ri, b_b_ri, b_lam, b_w_out,
              mlp_w1, mlp_w2, final_g):
    for layer in range(DEPTH):
        slot = layer // N_MIXERS
        h, gate = ada_modulate(x, c, ada_w[layer, 0], ada_b[layer, 0])
        if layer % N_MIXERS == 0:
            y = mlstm_mixer(h, a_w_in[slot], a_b_gate[slot], a_norm_g[slot], a_w_out[slot])
        else:
            y = rglru_mixer(h, b_w_in[slot], b_conv_w[slot], b_conv_b[slot], b_w_ra[slot], b_b_ra[slot],
                            b_w_ri[slot], b_b_ri[slot], b_lam[slot], b_w_out[slot])
        x = x + gate * y
        h, gate = ada_modulate(x, c, ada_w[layer, 1], ada_b[layer, 1])
        x = x + gate * sq_relu_mlp(h, mlp_w1[layer], mlp_w2[layer])
    return rms_norm(x) * final_g
```

```python
import numpy as np
import concourse.bass as bass
import concourse.mybir as mybir
from concourse.bass_utils import run_bass_kernel_spmd

F32 = mybir.dt.float32
BF16 = mybir.dt.bfloat16
ALU = mybir.AluOpType
AF = mybir.ActivationFunctionType

D = 2048
SEQ = 4096
NB = 4
KC = 16
T = 512
DFF = 8192
IN_A = 6160
EPS = 1e-6
NCORES = 4
WDEPTH = 2


class Res:
    __slots__ = ("name", "w", "r")

    def __init__(self, name):
        self.name = name
        self.w = None
        self.r = {}

    def add_r(self, ev):
        k = id(ev[0])
        if k not in self.r or self.r[k][1] < ev[1]:
            self.r[k] = ev


class Eng:
    def __init__(self, nc, name, h):
        self.name = name
        self.h = h
        self.sem = nc.alloc_semaphore(name="s_" + name)
        self.count = 0
        self.known = {}


class FW:
    def __init__(self, nc):
        self.nc = nc
        self.pe = Eng(nc, "pe", nc.tensor)
        self.act = Eng(nc, "act", nc.scalar)
        self.dve = Eng(nc, "dve", nc.vector)
        self.pool = Eng(nc, "pool", nc.gpsimd)
        self.sp = Eng(nc, "sp", nc.sync)
        self.dsems = {}

    def _wait(self, eng, evs):
        best = {}
        for ev in evs:
            if ev is None:
                continue
            sem, val, owner = ev
            if owner is eng:
                continue
            k = id(sem)
            if k not in best or best[k][1] < val:
                best[k] = ev
        for k, (sem, val, owner) in best.items():
            if eng.known.get(k, 0) >= val:
                continue
            eng.h.wait_ge(sem, val)
            eng.known[k] = val

    @staticmethod
    def _deps(reads, writes):
        evs = []
        for r in reads:
            evs.append(r.w)
        for w in writes:
            evs.append(w.w)
            evs.extend(w.r.values())
        return evs

    @staticmethod
    def _record(ev, reads, writes):
        for r in reads:
            r.add_r(ev)
        for w in writes:
            w.w = ev
            w.r = {}

    def op(self, eng, fn, reads=(), writes=()):
        self._wait(eng, self._deps(reads, writes))
        ins = fn()
        eng.count += 1
        ins.then_inc(eng.sem, 1)
        self._record((eng.sem, eng.count, eng), reads, writes)

    def mm(self, fns, reads=(), writes=()):
        eng = self.pe
        self._wait(eng, self._deps(reads, writes))
        ins = None
        for f in fns:
            ins = f()
        eng.count += 1
        ins.then_inc(eng.sem, 1)
        self._record((eng.sem, eng.count, eng), reads, writes)

    def dma(self, q, out, in_, key, reads=(), writes=()):
        if key not in self.dsems:
            self.dsems[key] = [self.nc.alloc_semaphore(name="d_" + key), 0]
        ds = self.dsems[key]
        self._wait(q, self._deps(reads, writes))
        q.h.dma_start(out=out, in_=in_).then_inc(ds[0], 16)
        ds[1] += 16
        ev = (ds[0], ds[1], None)
        self._record(ev, reads, writes)
        return ev


def _prune(res_list):
    pass


def build(nblk=SEQ // T, dbg=None):
    nc = bass.Bass("TRN2", target_bir_lowering=False)
    fw = FW(nc)
    PE, ACT, DVE, POOL, SP = fw.pe, fw.act, fw.dve, fw.pool, fw.sp
    mm = nc.tensor.matmul
    _zb = []

    def act_(out, in_, func, bias=None, scale=1.0):
        if func == AF.Copy:
            assert bias is None
            return nc.scalar.activation(out, in_, func, scale=scale)
        if bias is None:
            bias = _zb[0][:, 0:1]
        elif isinstance(bias, float):
            raise ValueError("float bias")
        return nc.scalar.activation(out, in_, func, bias=bias, scale=scale)

    def din(name, shape):
        return nc.dram_tensor(name, shape, F32, kind="ExternalInput").ap()

    xT = din("xT", [D, SEQ])
    cT = din("cT", [128, KC])
    ada_w = din("ada_w", [4, D, 3 * D])
    ada_b = din("ada_b", [128, 4 * 48])
    a_w_in = din("a_w_in", [D, IN_A])
    a_bg = din("a_bg", [128, 64])
    a_ng = din("a_ng", [128, KC])
    a_w_out = din("a_w_out", [D, D])
    b_w_in = din("b_w_in", [D, 2 * D])
    b_cw = din("b_cw", [128, KC * 4])
    b_vec = din("b_vec", [128, 4 * KC])
    b_w_ra = din("b_w_ra", [8, 256, 256])
    b_w_ri = din("b_w_ri", [8, 256, 256])
    b_w_out = din("b_w_out", [D, D])
    mlp_w1 = din("mlp_w1", [2, D, DFF])
    mlp_w2 = din("mlp_w2", [2, DFF, D])
    fin_g = din("fin_g", [128, KC])
    triu = din("triu", [128, 128])
    outT = nc.dram_tensor("outT", [D, SEQ], F32, kind="ExternalOutput").ap()

    def sb(name, shape, dt=F32):
        return nc.alloc_sbuf_tensor(name, shape, dt), Res(name)

    U32, rU32 = sb("U32", [128, 128])
    U4, rU4 = sb("U4", [128, 4, 128])
    ones_bf, rones = sb("ones_bf", [128, 128], BF16)
    c_sb, rc = sb("c_sb", [128, KC])
    sc_bf, rsc = sb("sc_bf", [128, KC], BF16)
    modv, rmod = sb("modv", [128, 4, 48])
    adab, radab = sb("adab", [128, 4, 48])
    bg_sb, rbg = sb("bg_sb", [128, 4, 16])
    ng_sb, rng_ = sb("ng_sb", [128, KC])
    cw_sb, rcw = sb("cw_sb", [128, KC, 4])
    bv_sb, rbv = sb("bv_sb", [128, 4, KC])
    m8, rm8 = sb("m8", [128, KC])
    m16, rm16 = sb("m16", [128, KC])
    fg_sb, rfg = sb("fg_sb", [128, KC])
    eps_sb, reps = sb("eps_sb", [128, 1])
    zero_sb, rzero = sb("zero_sb", [128, 1])
    _zb.append(zero_sb)
    one_sb, rone = sb("one_sb", [128, 1])
    hb_sb, rhb = sb("hb_sb", [128, 2, KC])
    m4, rm4 = sb("m4", [128, KC])
    wg_bf, rwg = sb("wg_bf", [128, KC, 16], BF16)
    wra_bf, rwra = sb("wra_bf", [128, 8, 2, 256], BF16)
    wri_bf, rwri = sb("wri_bf", [128, 8, 2, 256], BF16)

    x_sb, rx = sb("x_sb", [128, KC, T])
    hT, rh = sb("hT", [128, KC, T], BF16)
    Cf, rCf = [], []
    for h in range(8):
        a, r = sb(f"Cf{h}", [128, 384])
        Cf.append(a)
        rCf.append(r)
    Cb, rCb = [], []
    for h in range(8):
        a, r = sb(f"Cb{h}", [128, 384], BF16)
        Cb.append(a)
        rCb.append(r)
    eL, reL = sb("eL", [128, 2, 8])
    hst, rhst = sb("hst", [128, KC])
    halo, rhalo = sb("halo", [128, KC, 3])

    wslot = []
    for i in range(WDEPTH):
        wslot.append(sb(f"wslot{i}", [128, KC, 512], BF16))

    SCR_BYTES = 80 * 1024
    scr_used = [0]
    scr = nc.alloc_sbuf_tensor("scr", [128, SCR_BYTES // 4], F32)
    rscr_all = Res("scr_all")

    PS = [nc.alloc_psum_tensor(f"ps{i}", [128, 512], F32) for i in range(8)]
    rPS = [Res(f"ps{i}") for i in range(8)]

    plan = []

    def wtile(src2d, r0, c0, ncols):
        return src2d[r0:r0 + D, c0:c0 + ncols].rearrange("(kc p) n -> p kc n", p=128), ncols

    for cmb in range(4):
        for i in range(12):
            plan.append(wtile(ada_w[cmb], 0, i * 512, 512))
    for _ in range(nblk):
        for i in range(8):
            plan.append(wtile(a_w_in, 0, i * 512, 512))
        for i in range(8, 12):
            plan.append(wtile(a_w_in, 0, i * 512, 512))
        for i in range(4):
            plan.append(wtile(a_w_out, 0, i * 512, 512))
        for lyr in range(2):
            if lyr == 1:
                for i in range(4):
                    plan.append(wtile(b_w_in, 0, i * 512, 512))
                    plan.append(wtile(b_w_in, 0, D + i * 512, 512))
                for i in range(4):
                    plan.append(wtile(b_w_out, 0, i * 512, 512))
            for i in range(16):
                plan.append(wtile(mlp_w1[lyr], 0, i * 512, 512))
            for cg in range(4):
                for kg in range(4):
                    plan.append(wtile(mlp_w2[lyr], kg * D, cg * 512, 512))
    plan = []
    for cmb in range(4):
        for i in range(12):
            plan.append(wtile(ada_w[cmb], 0, i * 512, 512))
    for _ in range(nblk):
        for i in range(12):
            plan.append(wtile(a_w_in, 0, i * 512, 512))
        for i in range(4):
            plan.append(wtile(a_w_out, 0, i * 512, 512))
        for i in range(16):
            plan.append(wtile(mlp_w1[0], 0, i * 512, 512))
        for cg in range(4):
            for kg in range(4):
                plan.append(wtile(mlp_w2[0], kg * D, cg * 512, 512))
        for i in range(4):
            plan.append(wtile(b_w_in, 0, i * 512, 512))
            plan.append(wtile(b_w_in, 0, D + i * 512, 512))
        for i in range(4):
            plan.append(wtile(b_w_out, 0, i * 512, 512))
        for i in range(16):
            plan.append(wtile(mlp_w1[1], 0, i * 512, 512))
        for cg in range(4):
            for kg in range(4):
                plan.append(wtile(mlp_w2[1], kg * D, cg * 512, 512))

    wstate = {"next_load": 0, "next_use": 0}

    def _issue_load():
        i = wstate["next_load"]
        if i >= len(plan):
            return
        src, ncols = plan[i]
        t, r = wslot[i % WDEPTH]
        fw.dma(POOL, t[:, :, 0:ncols], src, f"w{i % WDEPTH}", writes=[r])
        wstate["next_load"] = i + 1

    def wnext():
        i = wstate["next_use"]
        if i == 0:
            for _ in range(WDEPTH):
                _issue_load()
        else:
            _issue_load()
        wstate["next_use"] = i + 1
        return wslot[i % WDEPTH]

    def cload(t, r, src):
        fw.dma(SP, t, src, "const", writes=[r])
        SP.h.wait_ge(fw.dsems["const"][0], fw.dsems["const"][1])
        SP.known[id(fw.dsems["const"][0])] = fw.dsems["const"][1]

    cload(U32[:, :], rU32, triu)
    cload(c_sb[:, :], rc, cT)
    cload(adab[:, :, :], radab, ada_b.rearrange("p (a b) -> p a b", a=4))
    cload(bg_sb[:, :, :], rbg, a_bg.rearrange("p (a b) -> p a b", a=4))
    cload(ng_sb[:, :], rng_, a_ng)
    cload(cw_sb[:, :, :], rcw, b_cw.rearrange("p (a b) -> p a b", a=KC))
    cload(bv_sb[:, :, :], rbv, b_vec.rearrange("p (a b) -> p a b", a=4))
    cload(fg_sb[:, :], rfg, fin_g)
    fw.dma(POOL, wg_bf[:, :, :], a_w_in[:, 6144:6160].rearrange("(kc p) n -> p kc n", p=128), "cw0", writes=[rwg])
    fw.dma(POOL, wra_bf[:, :, :, :], b_w_ra.rearrange("n (kc p) d -> p n kc d", p=128), "cw1", writes=[rwra])
    fw.dma(POOL, wri_bf[:, :, :, :], b_w_ri.rearrange("n (kc p) d -> p n kc d", p=128), "cw2", writes=[rwri])

    for j in range(4):
        fw.op(DVE, lambda j=j: nc.vector.tensor_copy(U4[:, j, :], U32[:, :]), reads=[rU32], writes=[rU4])
    fw.op(DVE, lambda: nc.vector.memset(ones_bf[:, :], 1.0), writes=[rones])
    fw.op(DVE, lambda: nc.vector.memset(eps_sb[:, :], EPS), writes=[reps])
    fw.op(DVE, lambda: nc.vector.memset(zero_sb[:, :], 0.0), writes=[rzero])
    fw.op(DVE, lambda: nc.vector.memset(one_sb[:, :], 1.0), writes=[rone])
    fw.op(DVE, lambda: nc.vector.tensor_scalar(hb_sb[:, :, :], bv_sb[:, 1:3, :], 0.5, None, ALU.mult), reads=[rbv], writes=[rhb])
    for h in range(8):
        fw.op(DVE, lambda h=h: nc.vector.memset(Cf[h][:, :], 0.0), writes=[rCf[h]])
        fw.op(DVE, lambda h=h: nc.vector.memset(Cb[h][:, :], 0.0), writes=[rCb[h]])
    fw.op(DVE, lambda: nc.vector.memset(eL[:, :, :], 1.0), writes=[reL])
    fw.op(DVE, lambda: nc.vector.memset(hst[:, :], 0.0), writes=[rhst])
    fw.op(DVE, lambda: nc.vector.memset(halo[:, :, :], 0.0), writes=[rhalo])
    fw.op(ACT, lambda: act_(sc_bf[:, :], c_sb[:, :], AF.Silu), reads=[rc, rzero, rone, reps], writes=[rsc])
    fw.op(ACT, lambda: act_(m8[:, :], bv_sb[:, 3, :], AF.Exp, scale=-1.0), reads=[rbv], writes=[rm8])
    fw.op(ACT, lambda: act_(m16[:, :], m8[:, :], AF.Ln, bias=one_sb[:, 0:1]), reads=[rm8], writes=[rm16])
    fw.op(DVE, lambda: nc.vector.tensor_scalar(m8[:, :], m16[:, :], -8.0, None, ALU.mult), reads=[rm16], writes=[rm8])
    fw.op(DVE, lambda: nc.vector.tensor_scalar(m4[:, :], m16[:, :], -4.0, None, ALU.mult), reads=[rm16], writes=[rm4])
    fw.op(DVE, lambda: nc.vector.tensor_scalar(m16[:, :], m16[:, :], -16.0, None, ALU.mult), reads=[rm16], writes=[rm16])

    if dbg == "mdbg":
        dbt, rdbt = sb("dbt", [128, 96])
        fw.op(ACT, lambda: act_(dbt[:, 0:16], bv_sb[:, 3, :], AF.Exp, scale=-1.0), reads=[rbv], writes=[rdbt])
        fw.op(ACT, lambda: act_(dbt[:, 16:32], dbt[:, 0:16], AF.Ln, bias=one_sb[:, 0:1]), reads=[rdbt], writes=[rdbt])
        fw.op(ACT, lambda: act_(dbt[:, 32:48], dbt[:, 0:16], AF.Ln, bias=one_sb[:, 0:1]), reads=[rdbt, rone], writes=[rdbt])
        fw.op(DVE, lambda: nc.vector.tensor_scalar(dbt[:, 48:64], dbt[:, 0:16], 1.0, None, ALU.add), reads=[rdbt], writes=[rdbt])
        fw.op(ACT, lambda: act_(dbt[:, 64:80], dbt[:, 48:64], AF.Ln), reads=[rdbt], writes=[rdbt])
        fw.op(DVE, lambda: nc.vector.tensor_copy(dbt[:, 80:96], m16[:, :]), reads=[rm16], writes=[rdbt])
        ev = fw.dma(SP, outT[0:128, 0:96], dbt[:, :], "out", reads=[rdbt])
        fw._wait(SP, [ev])
        return nc
    for cmb in range(4):
        for i in range(12):
            wt, rw = wnext()
            fns = []
            for fc in range(4):
                col = i * 4 + fc
                for kc in range(KC):
                    fns.append(lambda fc=fc, kc=kc, col=col, wt=wt: mm(
                        PS[0][:, col:col + 1], wt[:, kc, fc * 128:(fc + 1) * 128], sc_bf[:, kc:kc + 1],
                        start=(kc == 0), stop=(kc == KC - 1)))
            fw.mm(fns, reads=[rw, rsc], writes=[rPS[0]])
        fw.op(DVE, lambda cmb=cmb: nc.vector.tensor_tensor(modv[:, cmb, :], PS[0][:, 0:48], adab[:, cmb, :], ALU.add),
              reads=[rPS[0], radab], writes=[rmod])
        fw.op(DVE, lambda cmb=cmb: nc.vector.tensor_scalar(modv[:, cmb, 16:32], modv[:, cmb, 16:32], 1.0, None, ALU.add),
              reads=[rmod], writes=[rmod])

    class Scr:
        def __init__(self):
            self.off = 0

        def take(self, name, shape, dt=F32):
            n = int(np.prod(shape[1:]))
            nbytes = n * (4 if dt == F32 else 2)
            nbytes = (nbytes + 31) // 32 * 32
            w0 = self.off // 4
            self.off += nbytes
            assert self.off <= SCR_BYTES, (name, self.off)
            v = scr[:, w0:w0 + nbytes // 4]
            if dt != F32:
                v = v.bitcast(dt)
            v = v[:, 0:n]
            if len(shape) == 3:
                v = v.rearrange("p (a b) -> p a b", a=shape[1])
            elif len(shape) == 4:
                v = v.rearrange("p (a b c) -> p a b c", a=shape[1], b=shape[2])
            return v

    phase_res = []

    def new_phase():
        prev = list(phase_res)
        phase_res.clear()
        return prev

    def adaln(cmb):
        sq = s.take_tmp("sq", [128, 2, T], BF16, nres=2)
        for kc in range(KC):
            fw.op(ACT, lambda kc=kc: act_(sq[0][:, kc % 2, :], x_sb[:, kc, :], AF.Square),
                  reads=[rx], writes=[sq[1][kc % 2]])
            fw.mm([lambda kc=kc: mm(PS[7][:, :], ones_bf[:, :], sq[0][:, kc % 2, :], start=(kc == 0), stop=(kc == KC - 1))],
                  reads=[sq[1][kc % 2], rones], writes=[rPS[7]])
        rstd = s.take_tmp("rstd", [128, T])
        fw.op(ACT, lambda: act_(rstd[0][:, :], PS[7][:, :], AF.Sqrt, bias=eps_sb[:, 0:1], scale=1.0 / D),
              reads=[rPS[7], reps], writes=[rstd[1]])
        fw.op(DVE, lambda: nc.vector.reciprocal(rstd[0][:, :], rstd[0][:, :]),
              reads=[rstd[1]], writes=[rstd[1]])
        tmp = s.take_tmp("adatmp", [128, 2, T], nres=2)
        for kc in range(KC):
            fw.op(DVE, lambda kc=kc: nc.vector.scalar_tensor_tensor(
                tmp[0][:, kc % 2, :], x_sb[:, kc, :], modv[:, cmb, 16 + kc:17 + kc], rstd[0][:, :], ALU.mult, ALU.mult),
                reads=[rx, rmod, rstd[1]], writes=[tmp[1][kc % 2]])
            fw.op(ACT, lambda kc=kc: act_(hT[:, kc, :], tmp[0][:, kc % 2, :], AF.Identity,
                                                          bias=modv[:, cmb, kc:kc + 1]),
                  reads=[tmp[1][kc % 2], rmod], writes=[rh])

    class S:
        def __init__(self):
            self.scr = Scr()
            self.bufs = {}
            self.haz = {}

        def reset(self):
            for (_, res) in self.bufs.values():
                for o in (res if isinstance(res, list) else [res]):
                    evs = list(o.r.values())
                    if o.w is not None:
                        evs.append(o.w)
                    for ev in evs:
                        k = id(ev[0])
                        if k not in self.haz or self.haz[k][1] < ev[1]:
                            self.haz[k] = ev
            self.scr = Scr()
            self.bufs = {}

        def take_tmp(self, name, shape, dt=F32, nres=0):
            if name in self.bufs:
                return self.bufs[name]
            ap = self.scr.take(name, shape, dt)
            res = [Res(name + str(i)) for i in range(nres)] if nres else Res(name)
            for rr in (res if isinstance(res, list) else [res]):
                rr.r = dict(self.haz)
            self.bufs[name] = (ap, res)
            return self.bufs[name]

    s = S()

    def proj_fm(ntiles, evac):
        for i in range(ntiles):
            wt, rw = wnext()
            for fc in range(4):
                bank = (i * 4 + fc) % 4
                fw.mm([lambda kc=kc, fc=fc, wt=wt, bank=bank: mm(
                    PS[bank][:, :], wt[:, kc, fc * 128:(fc + 1) * 128], hT[:, kc, :],
                    start=(kc == 0), stop=(kc == KC - 1)) for kc in range(KC)],
                    reads=[rw, rh], writes=[rPS[bank]])
                evac(i, fc, bank)

    def resid_evac(gate_col0):
        def ev(i, fc, bank):
            f = i * 4 + fc
            fw.op(DVE, lambda: nc.vector.scalar_tensor_tensor(
                x_sb[:, f, :], PS[bank][:, :], modv[:, gate_col0[0], gate_col0[1] + f:gate_col0[1] + f + 1],
                x_sb[:, f, :], ALU.mult, ALU.add), reads=[rPS[bank], rmod, rx], writes=[rx])
        return ev

    def mlp(lyr, cmb):
        s.reset()
        adaln(cmb)
        s.reset()
        aT, raT = s.take_tmp("aT", [128, 64, T], BF16)
        rl = s.take_tmp("relu", [128, 2, T], nres=2)
        rrl = rl[1]
        cnt = [0]

        def ev1(i, fc, bank):
            f = i * 4 + fc
            k = cnt[0] % 2
            cnt[0] += 1
            fw.op(ACT, lambda: act_(rl[0][:, k, :], PS[bank][:, :], AF.Relu),
                  reads=[rPS[bank]], writes=[rrl[k]])
            fw.op(DVE, lambda: nc.vector.tensor_tensor(aT[:, f, :], rl[0][:, k, :], rl[0][:, k, :], ALU.mult),
                  reads=[rrl[k]], writes=[raT])
        proj_fm(16, ev1)
        for cg in range(4):
            for kg in range(4):
                wt, rw = wnext()
                for fc in range(4):
                    fw.mm([lambda kc=kc, fc=fc, wt=wt, kg=kg: mm(
                        PS[fc][:, :], wt[:, kc, fc * 128:(fc + 1) * 128], aT[:, kg * 16 + kc, :],
                        start=(kg == 0 and kc == 0), stop=(kg == 3 and kc == KC - 1)) for kc in range(KC)],
                        reads=[rw, raT], writes=[rPS[fc]])
            for fc in range(4):
                f = cg * 4 + fc
                fw.op(DVE, lambda f=f, fc=fc: nc.vector.scalar_tensor_tensor(
                    x_sb[:, f, :], PS[fc][:, :], modv[:, cmb, 32 + f:33 + f], x_sb[:, f, :], ALU.mult, ALU.add),
                    reads=[rPS[fc], rmod, rx], writes=[rx])

    def mlstm_layer(blk):
        cmb = 0
        s.reset()
        adaln(cmb)
        s.reset()
        qT, rq = s.take_tmp("qT", [128, 8, T], BF16)
        kT, rk = s.take_tmp("kT", [128, 8, T], BF16)
        ktok, rkt = s.take_tmp("ktok", [128, 4, 1024], BF16)
        vaug, rv = s.take_tmp("vaug", [128, 4, 8, 256], BF16)
        yT, ry = s.take_tmp("yT", [128, KC, T], BF16)
        zs, rzs = s.take_tmp("zs", [128, 4, 16])
        nlf, rnlf = s.take_tmp("nlf", [128, 4, 8])
        ecol, recol = s.take_tmp("ecol", [128, 4, 8])
        nlfrep, rnr = s.take_tmp("nlfrep", [128, 8, 128])
        rb, rrb = s.take_tmp("rb", [128, 4, 128])
        vs, rvs = s.take_tmp("vs", [128, 4, 384], BF16)
        MT, rMT = s.take_tmp("MT", [128, 4, 128], BF16)
        den, rden = s.take_tmp("den", [128, 4, 128])
        hh, rhh = s.take_tmp("hh", [128, 2, 512])
        sqh, rsqh = s.take_tmp("sqh", [128, 2, 512], BF16)
        rsd, rrsd = s.take_tmp("rsd", [128, 512])

        for i in range(8):
            wt, rw = wnext()
            if i < 4:
                for fc in range(4):
                    bank = fc
                    fw.mm([lambda kc=kc, fc=fc, wt=wt, bank=bank: mm(
                        PS[bank][:, :], wt[:, kc, fc * 128:(fc + 1) * 128], hT[:, kc, :],
                        start=(kc == 0), stop=(kc == KC - 1)) for kc in range(KC)],
                        reads=[rw, rh], writes=[rPS[bank]])
                    hd = (i % 2) * 4 + fc
                    if i < 2:
                        fw.op(ACT, lambda hd=hd, bank=bank: act_(
                            qT[:, hd, :], PS[bank][:, :], AF.Copy, scale=float(128 ** -0.5)),
                            reads=[rPS[bank]], writes=[rq])
                    else:
                        fw.op(ACT, lambda hd=hd, bank=bank: act_(
                            kT[:, hd, :], PS[bank][:, :], AF.Copy), reads=[rPS[bank]], writes=[rk])
            if i >= 2:
                for tl in range(4):
                    bank = 4 + (tl % 2)
                    fw.mm([lambda kc=kc, tl=tl, wt=wt, bank=bank: mm(
                        PS[bank][:, :], hT[:, kc, tl * 128:(tl + 1) * 128], wt[:, kc, :],
                        start=(kc == 0), stop=(kc == KC - 1)) for kc in range(KC)],
                        reads=[rw, rh], writes=[rPS[bank]])
                    if i < 4:
                        fw.op(DVE, lambda tl=tl, bank=bank, i=i: nc.vector.tensor_copy(
                            ktok[:, tl, (i - 2) * 512:(i - 1) * 512], PS[bank][:, :]),
                            reads=[rPS[bank]], writes=[rkt])
                    else:
                        h0 = (i - 4) * 2
                        fw.op(DVE, lambda tl=tl, bank=bank, h0=h0: nc.vector.tensor_copy(
                            vaug[:, tl, h0:h0 + 2, 0:256], PS[bank][:, :].rearrange("p (a b) -> p a b", a=2)),
                            reads=[rPS[bank]], writes=[rv])
        fw.mm([lambda kc=kc, tl=tl: mm(PS[6][:, tl * 16:(tl + 1) * 16], hT[:, kc, tl * 128:(tl + 1) * 128], wg_bf[:, kc, :],
                                       start=(kc == 0), stop=(kc == KC - 1)) for tl in range(4) for kc in range(KC)],
              reads=[rh, rwg], writes=[rPS[6]])
        fw.op(DVE, lambda: nc.vector.tensor_tensor(zs[:, :, :], PS[6][:, 0:64].rearrange("p (a b) -> p a b", a=4),
                                                   bg_sb[:, :, :], ALU.add), reads=[rPS[6], rbg], writes=[rzs])
        fw.op(ACT, lambda: act_(nlf[:, :, :], zs[:, :, 8:16], AF.Exp, scale=-1.0), reads=[rzs], writes=[rnlf])
        fw.op(ACT, lambda: act_(nlf[:, :, :], nlf[:, :, :], AF.Ln, bias=one_sb[:, 0:1]), reads=[rnlf], writes=[rnlf])
        fw.mm([lambda tl=tl: mm(PS[6][:, 64 + tl * 8:72 + tl * 8], U32[:, :], nlf[:, tl, :], start=True, stop=True)
               for tl in range(4)], reads=[rU32, rnlf], writes=[rPS[6]])
        fw.op(DVE, lambda: nc.vector.tensor_tensor(ecol[:, :, :], PS[6][:, 64:96].rearrange("p (a b) -> p a b", a=4),
                                                   zs[:, :, 0:8], ALU.add), reads=[rPS[6], rzs], writes=[recol])
        fw.op(ACT, lambda: act_(ecol[:, :, :], ecol[:, :, :], AF.Exp), reads=[recol], writes=[recol])

        for tl in range(4):
            ch = blk * 4 + tl
            par = ch % 2
            tsl = slice(tl * 128, (tl + 1) * 128)
            fw.op(DVE, lambda tl=tl: nc.vector.tensor_copy(
                nlfrep[:, :, :], nlf[:, tl, :].unsqueeze(2).broadcast_to([128, 8, 128])), reads=[rnlf], writes=[rnr])
            for hg in range(2):
                hs = [hg * 4 + j for j in range(4)]
                fw.mm([lambda j=j, h=h: mm(PS[7][:, j * 128:(j + 1) * 128], nlfrep[:, h, :], U32[:, :], start=True, stop=True)
                       for j, h in enumerate(hs)], reads=[rnr, rU32], writes=[rPS[7]])
                fw.op(ACT, lambda: act_(rb[:, :, :], PS[7][:, :].rearrange("p (a b) -> p a b", a=4), AF.Exp),
                      reads=[rPS[7]], writes=[rrb])
                fw.op(ACT, lambda hg=hg, par=par: act_(
                    eL[:, par, hg * 4:hg * 4 + 4], PS[7][:, :].rearrange("p (a b) -> p a b", a=4)[:, :, 127], AF.Exp, scale=-1.0),
                    reads=[rPS[7]], writes=[reL])
                fw.op(DVE, lambda tl=tl, hg=hg: nc.vector.tensor_tensor(
                    vs[:, :, 0:256], vaug[:, tl, hg * 4:hg * 4 + 4, :],
                    ecol[:, tl, hg * 4:hg * 4 + 4].unsqueeze(2).broadcast_to([128, 4, 256]), ALU.mult),
                    reads=[rv, recol], writes=[rvs])
                fw.op(DVE, lambda tl=tl, hg=hg: nc.vector.tensor_copy(
                    vs[:, :, 256:384], ecol[:, tl, hg * 4:hg * 4 + 4].unsqueeze(2).broadcast_to([128, 4, 128])),
                    reads=[recol], writes=[rvs])
                fw.mm([lambda j=j, h=h, tsl=tsl: mm(PS[0][:, j * 128:(j + 1) * 128], kT[:, h, tsl], qT[:, h, tsl], start=True, stop=True)
                       for j, h in enumerate(hs)], reads=[rk, rq], writes=[rPS[0]])
                fw.op(DVE, lambda: nc.vector.tensor_tensor(MT[:, :, :], PS[0][:, :].rearrange("p (a b) -> p a b", a=4),
                                                           U4[:, :, :], ALU.mult), reads=[rPS[0], rU4], writes=[rMT])
                for ec in range(3):
                    fns = []
                    for j, h in enumerate(hs):
                        fns.append(lambda j=j, h=h, ec=ec, tsl=tsl: mm(
                            PS[1 + ec][:, j * 128:(j + 1) * 128], Cb[h][:, ec * 128:(ec + 1) * 128], qT[:, h, tsl],
                            start=True, stop=False))
                        fns.append(lambda j=j, h=h, ec=ec: mm(
                            PS[1 + ec][:, j * 128:(j + 1) * 128], vs[:, j, ec * 128:(ec + 1) * 128], MT[:, j, :],
                            start=False, stop=True))
                    fw.mm(fns, reads=[rCb[h] for h in hs] + [rq, rvs, rMT], writes=[rPS[1 + ec]])
                fw.op(ACT, lambda: act_(den[:, :, :], PS[3][:, :].rearrange("p (a b) -> p a b", a=4), AF.Abs),
                      reads=[rPS[3]], writes=[rden])
                fw.op(DVE, lambda: nc.vector.tensor_tensor(den[:, :, :], den[:, :, :], rb[:, :, :], ALU.max),
                      reads=[rden, rrb], writes=[rden])
                fw.op(DVE, lambda: nc.vector.reciprocal(den[:, :, :], den[:, :, :]), reads=[rden], writes=[rden])
                for ec in range(2):
                    fw.op(DVE, lambda ec=ec: nc.vector.tensor_tensor(
                        hh[:, ec, :], PS[1 + ec][:, :], den[:, :, :].rearrange("p a b -> p (a b)"), ALU.mult),
                        reads=[rPS[1 + ec], rden], writes=[rhh])
                    fw.op(ACT, lambda ec=ec: act_(sqh[:, ec, :], hh[:, ec, :], AF.Square),
                          reads=[rhh], writes=[rsqh])
                fw.mm([lambda ec=ec: mm(PS[4][:, :], ones_bf[:, :], sqh[:, ec, :], start=(ec == 0), stop=(ec == 1))
                       for ec in range(2)], reads=[rones, rsqh], writes=[rPS[4]])
                fw.op(ACT, lambda: act_(rsd[:, :], PS[4][:, :], AF.Sqrt, bias=eps_sb[:, 0:1], scale=1.0 / 256),
                      reads=[rPS[4], reps], writes=[rrsd])
                fw.op(DVE, lambda: nc.vector.reciprocal(rsd[:, :], rsd[:, :]),
                      reads=[rrsd], writes=[rrsd])
                for ec in range(2):
                    fw.op(DVE, lambda ec=ec, hg=hg, tsl=tsl: nc.vector.tensor_tensor(
                        yT[:, hg * 8 + ec:hg * 8 + 8:2, tsl], hh[:, ec, :].rearrange("p (a b) -> p a b", a=4),
                        rsd[:, :].rearrange("p (a b) -> p a b", a=4), ALU.mult), reads=[rhh, rrsd], writes=[ry])
                for j, h in enumerate(hs):
                    bank = 5 + (j % 2)
                    fw.mm([lambda j=j, h=h, tl=tl, bank=bank: mm(
                        PS[bank][:, 0:384], ktok[:, tl, h * 128:(h + 1) * 128], vs[:, j, :], start=True, stop=True)],
                        reads=[rkt, rvs], writes=[rPS[bank]])
                    fw.op(DVE, lambda h=h, bank=bank, par=par: nc.vector.scalar_tensor_tensor(
                        Cf[h][:, :], Cf[h][:, :], eL[:, 1 - par, h:h + 1], PS[bank][:, 0:384], ALU.mult, ALU.add),
                        reads=[rCf[h], reL, rPS[bank]], writes=[rCf[h]])
                    fw.op(ACT, lambda h=h, par=par: act_(
                        Cb[h][:, :], Cf[h][:, :], AF.Copy, scale=eL[:, par, h:h + 1]),
                        reads=[rCf[h], reL], writes=[rCb[h]])

        cnt = [0]

        def evo(i, fc, bank):
            f = i * 4 + fc
            k = cnt[0] % 2
            cnt[0] += 1
            fw.op(ACT, lambda: act_(hh[:, k, :], PS[bank][:, :], AF.Sigmoid),
                  reads=[rPS[bank]], writes=[rhh])
            fw.op(DVE, lambda: nc.vector.scalar_tensor_tensor(
                yT[:, f, :], hh[:, k, :], ng_sb[:, f:f + 1], yT[:, f, :], ALU.mult, ALU.mult),
                reads=[rhh, rng_, ry], writes=[ry])
        proj_fm(4, evo)
        for i in range(4):
            wt, rw = wnext()
            for fc in range(4):
                bank = fc
                f = i * 4 + fc
                fw.mm([lambda kc=kc, fc=fc, wt=wt, bank=bank: mm(
                    PS[bank][:, :], wt[:, kc, fc * 128:(fc + 1) * 128], yT[:, kc, :],
                    start=(kc == 0), stop=(kc == KC - 1)) for kc in range(KC)],
                    reads=[rw, ry], writes=[rPS[bank]])
                fw.op(DVE, lambda f=f, bank=bank: nc.vector.scalar_tensor_tensor(
                    x_sb[:, f, :], PS[bank][:, :], modv[:, cmb, 32 + f:33 + f], x_sb[:, f, :], ALU.mult, ALU.add),
                    reads=[rPS[bank], rmod, rx], writes=[rx])

    def rglru_layer(blk):
        cmb = 2
        s.reset()
        adaln(cmb)
        s.reset()
        yT, ry = s.take_tmp("yT", [128, KC, T], BF16)
        xbp, rxbp = s.take_tmp("xbp", [128, 4, T + 3])
        xc, rxc = s.take_tmp("xc", [128, 4, T])
        xcb, rxcb = s.take_tmp("xcb", [128, 4, T], BF16)
        gg, rgg = s.take_tmp("gg", [128, 4, T], BF16)
        tm = [s.take_tmp(f"t{i}", [128, T]) for i in range(4)]
        ta = [s.take_tmp(f"ta{i}", [128, T]) for i in range(4)]
        ta2 = [s.take_tmp(f"tb{i}", [128, T]) for i in range(4)]
        tix = [s.take_tmp(f"tc{i}", [128, T]) for i in range(4)]
        for i in range(4):
            wt, rw = wnext()
            for fc in range(4):
                c = i * 4 + fc
                bank = fc
                fw.mm([lambda kc=kc, fc=fc, wt=wt, bank=bank: mm(
                    PS[bank][:, :], wt[:, kc, fc * 128:(fc + 1) * 128], hT[:, kc, :],
                    start=(kc == 0), stop=(kc == KC - 1)) for kc in range(KC)],
                    reads=[rw, rh], writes=[rPS[bank]])
                fw.op(ACT, lambda fc=fc, c=c: act_(xbp[:, fc, 0:3], halo[:, c, :], AF.Copy),
                      reads=[rhalo], writes=[rxbp])
                fw.op(ACT, lambda fc=fc, bank=bank: act_(xbp[:, fc, 3:T + 3], PS[bank][:, :], AF.Copy),
                      reads=[rPS[bank]], writes=[rxbp])
                fw.op(ACT, lambda fc=fc, c=c: act_(halo[:, c, :], xbp[:, fc, T:T + 3], AF.Copy),
                      reads=[rxbp], writes=[rhalo])
                fw.op(ACT, lambda fc=fc, c=c: act_(
                    xc[:, fc, :], xbp[:, fc, 3:T + 3], AF.Identity, bias=bv_sb[:, 0, c:c + 1], scale=cw_sb[:, c, 3:4]),
                    reads=[rxbp, rbv, rcw], writes=[rxc])
                for k in range(3):
                    fw.op(DVE, lambda fc=fc, c=c, k=k: nc.vector.scalar_tensor_tensor(
                        xc[:, fc, :], xbp[:, fc, k:k + T], cw_sb[:, c, k:k + 1], xc[:, fc, :], ALU.mult, ALU.add),
                        reads=[rxbp, rcw, rxc], writes=[rxc])
                fw.op(ACT, lambda fc=fc: act_(xcb[:, fc, :], xc[:, fc, :], AF.Copy),
                      reads=[rxc], writes=[rxcb])
            wt, rw = wnext()
            for fc in range(4):
                bank = 4 + (fc % 2)
                fw.mm([lambda kc=kc, fc=fc, wt=wt, bank=bank: mm(
                    PS[bank][:, :], wt[:, kc, fc * 128:(fc + 1) * 128], hT[:, kc, :],
                    start=(kc == 0), stop=(kc == KC - 1)) for kc in range(KC)],
                    reads=[rw, rh], writes=[rPS[bank]])
                z2, rz2 = tm[0]
                fw.op(ACT, lambda bank=bank, z2=z2: act_(z2[:, :], PS[bank][:, :], AF.Square),
                      reads=[rPS[bank]], writes=[rz2])
                fw.op(DVE, lambda z2=z2: nc.vector.tensor_scalar(z2[:, :], z2[:, :], 0.044715, 1.0, ALU.mult, ALU.add),
                      reads=[rz2], writes=[rz2])
                fw.op(DVE, lambda bank=bank, z2=z2: nc.vector.tensor_tensor(z2[:, :], z2[:, :], PS[bank][:, :], ALU.mult),
                      reads=[rz2, rPS[bank]], writes=[rz2])
                fw.op(ACT, lambda z2=z2: act_(z2[:, :], z2[:, :], AF.Tanh, scale=0.7978845608028654),
                      reads=[rz2], writes=[rz2])
                fw.op(DVE, lambda fc=fc, bank=bank, z2=z2: nc.vector.scalar_tensor_tensor(
                    gg[:, fc, :], z2[:, :], 1.0, PS[bank][:, :], ALU.add, ALU.mult),
                    reads=[rz2, rPS[bank]], writes=[rgg])
            for fc in range(4):
                c = i * 4 + fc
                n = c // 2
                m = c % 2
                kl = (fc // 2) * 2
                fw.mm([lambda kk=kk, n=n, m=m, kl=kl: mm(PS[6][:, :], wra_bf[:, n, kk, m * 128:(m + 1) * 128], xcb[:, kl + kk, :],
                                                         start=(kk == 0), stop=(kk == 1)) for kk in range(2)],
                      reads=[rwra, rxcb], writes=[rPS[6]])
                fw.mm([lambda kk=kk, n=n, m=m, kl=kl: mm(PS[7][:, :], wri_bf[:, n, kk, m * 128:(m + 1) * 128], xcb[:, kl + kk, :],
                                                         start=(kk == 0), stop=(kk == 1)) for kk in range(2)],
                      reads=[rwri, rxcb], writes=[rPS[7]])
                (r_, rr_), (i_, ri_) = tm[1], tm[2]
                a_, ra_ = ta[fc]
                a2, ra2 = ta2[fc]
                ix, rix = tix[fc]
                fw.op(ACT, lambda c=c: act_(r_[:, :], PS[6][:, :], AF.Tanh, bias=hb_sb[:, 0, c:c + 1], scale=0.5),
                      reads=[rPS[6], rhb], writes=[rr_])
                fw.op(ACT, lambda c=c: act_(i_[:, :], PS[7][:, :], AF.Tanh, bias=hb_sb[:, 1, c:c + 1], scale=0.5),
                      reads=[rPS[7], rhb], writes=[ri_])
                fw.op(ACT, lambda c=c, a_=a_: act_(a_[:, :], r_[:, :], AF.Exp, bias=m4[:, c:c + 1], scale=m4[:, c:c + 1]),
                      reads=[rr_, rm4], writes=[ra_])
                fw.op(ACT, lambda c=c, a2=a2: act_(a2[:, :], r_[:, :], AF.Exp, bias=m8[:, c:c + 1], scale=m8[:, c:c + 1]),
                      reads=[rr_, rm8], writes=[ra2])
                fw.op(DVE, lambda fc=fc, ix=ix: nc.vector.scalar_tensor_tensor(ix[:, :], i_[:, :], 1.0, xc[:, fc, :], ALU.add, ALU.mult),
                      reads=[ri_, rxc], writes=[rix])
            for fc in range(4):
                a2, ra2 = ta2[fc]
                fw.op(ACT, lambda a2=a2: act_(a2[:, :], a2[:, :], AF.Sqrt, bias=one_sb[:, 0:1], scale=-1.0),
                      reads=[ra2, rone], writes=[ra2])
            for fc in range(4):
                c = i * 4 + fc
                a_, ra_ = ta[fc]
                a2, ra2 = ta2[fc]
                ix, rix = tix[fc]
                hs_, rhs_ = tm[3]
                fw.op(DVE, lambda a2=a2, ix=ix: nc.vector.scalar_tensor_tensor(ix[:, :], a2[:, :], 0.5, ix[:, :], ALU.mult, ALU.mult),
                      reads=[ra2, rix], writes=[rix])
                fw.op(DVE, lambda c=c, a_=a_, ix=ix: nc.vector.tensor_tensor_scan(hs_[:, :], a_[:, :], ix[:, :], hst[:, c:c + 1], ALU.mult, ALU.add),
                      reads=[ra_, rix, rhst], writes=[rhs_])
                fw.op(ACT, lambda c=c: act_(hst[:, c:c + 1], hs_[:, T - 1:T], AF.Copy),
                      reads=[rhs_], writes=[rhst])
                fw.op(DVE, lambda c=c, fc=fc: nc.vector.scalar_tensor_tensor(yT[:, c, :], hs_[:, :], 0.5, gg[:, fc, :], ALU.mult, ALU.mult),
                      reads=[rhs_, rgg], writes=[ry])
            if dbg == "l1dbg" and i == 0:
                taps = [(xc[:, 0, :], rxc), (ta[0][0][:, :], ta[0][1]), (ta2[0][0][:, :], ta2[0][1]), (tix[0][0][:, :], tix[0][1]),
                        (xbp[:, 0, 3:T + 3], rxbp), (tm[3][0][:, :], tm[3][1])]
                for j, (ap_, r_) in enumerate(taps):
                    ev = fw.dma(SP, outT[j * 128:(j + 1) * 128, 0:T], ap_, "out", reads=[r_])
                    out_evs.append(ev)
                    fw._wait(SP, [ev])
                return
        for i in range(4):
            wt, rw = wnext()
            for fc in range(4):
                bank = fc
                f = i * 4 + fc
                fw.mm([lambda kc=kc, fc=fc, wt=wt, bank=bank: mm(
                    PS[bank][:, :], wt[:, kc, fc * 128:(fc + 1) * 128], yT[:, kc, :],
                    start=(kc == 0), stop=(kc == KC - 1)) for kc in range(KC)],
                    reads=[rw, ry], writes=[rPS[bank]])
                fw.op(DVE, lambda f=f, bank=bank: nc.vector.scalar_tensor_tensor(
                    x_sb[:, f, :], PS[bank][:, :], modv[:, cmb, 32 + f:33 + f], x_sb[:, f, :], ALU.mult, ALU.add),
                    reads=[rPS[bank], rmod, rx], writes=[rx])

    out_evs = []

    def final_norm_store(blk):
        s.reset()
        sq = s.take_tmp("sq", [128, 2, T], BF16, nres=2)
        for kc in range(KC):
            fw.op(ACT, lambda kc=kc: act_(sq[0][:, kc % 2, :], x_sb[:, kc, :], AF.Square),
                  reads=[rx], writes=[sq[1][kc % 2]])
            fw.mm([lambda kc=kc: mm(PS[7][:, :], ones_bf[:, :], sq[0][:, kc % 2, :], start=(kc == 0), stop=(kc == KC - 1))],
                  reads=[sq[1][kc % 2], rones], writes=[rPS[7]])
        rstd = s.take_tmp("rstd", [128, T])
        fw.op(ACT, lambda: act_(rstd[0][:, :], PS[7][:, :], AF.Sqrt, bias=eps_sb[:, 0:1], scale=1.0 / D),
              reads=[rPS[7], reps], writes=[rstd[1]])
        fw.op(DVE, lambda: nc.vector.reciprocal(rstd[0][:, :], rstd[0][:, :]),
              reads=[rstd[1]], writes=[rstd[1]])
        ob, rob = s.take_tmp("ob", [128, KC, T])
        for kc in range(KC):
            fw.op(DVE, lambda kc=kc: nc.vector.scalar_tensor_tensor(
                ob[:, kc, :], x_sb[:, kc, :], fg_sb[:, kc:kc + 1], rstd[0][:, :], ALU.mult, ALU.mult),
                reads=[rx, rfg, rstd[1]], writes=[rob])
        ev = fw.dma(SP, outT[:, blk * T:(blk + 1) * T].rearrange("(kc p) t -> p kc t", p=128), ob[:, :, :], "out", reads=[rob])
        out_evs.append(ev)

    def dbg_store():
        ev = fw.dma(SP, outT[:, 0:T].rearrange("(kc p) t -> p kc t", p=128), x_sb[:, :, :], "out", reads=[rx])
        out_evs.append(ev)

    for blk in range(nblk):
        fw.dma(SP, x_sb[:, :, :], xT[:, blk * T:(blk + 1) * T].rearrange("(kc p) t -> p kc t", p=128), "xin", writes=[rx])
        mlstm_layer(blk)
        if dbg == "l0mix":
            dbg_store()
            break
        if dbg == "fntest":
            final_norm_store(blk)
            break
        mlp(0, 1)
        if dbg == "l0":
            dbg_store()
            break
        rglru_layer(blk)
        if dbg == "l1dbg":
            break
        if dbg == "l1mix":
            dbg_store()
            break
        mlp(1, 3)
        if dbg == "l1":
            dbg_store()
            break
        final_norm_store(blk)

    if dbg is None:
        assert wstate["next_use"] == len(plan), (wstate, len(plan))
    fw._wait(SP, out_evs)
    return nc


def _prep_inputs(inp, b):
    f32 = np.float32
    x = inp["x"]

    def fm(v):
        return np.ascontiguousarray(np.asarray(v, f32).reshape(KC, 128).T)

    m = {}
    m["xT"] = np.ascontiguousarray(x[b].T)
    m["cT"] = fm(inp["c"][b])
    m["ada_w"] = np.ascontiguousarray(inp["ada_w"].reshape(4, D, 3 * D))
    m["ada_b"] = np.ascontiguousarray(inp["ada_b"].reshape(4, 48, 128).transpose(2, 0, 1).reshape(128, 4 * 48))
    m["a_w_in"] = np.ascontiguousarray(inp["a_w_in"][0])
    bg = inp["a_b_gate"][0].reshape(16)
    m["a_bg"] = np.ascontiguousarray(np.broadcast_to(np.tile(bg, 4)[None, :], (128, 64))).astype(f32)
    m["a_ng"] = fm(inp["a_norm_g"][0])
    m["a_w_out"] = np.ascontiguousarray(inp["a_w_out"][0])
    m["b_w_in"] = np.ascontiguousarray(inp["b_w_in"][0])
    cw = inp["b_conv_w"][0]
    m["b_cw"] = np.ascontiguousarray(cw.T.reshape(KC, 128, 4).transpose(1, 0, 2).reshape(128, KC * 4))
    m["b_vec"] = np.ascontiguousarray(np.concatenate(
        [fm(inp["b_conv_b"][0]), fm(inp["b_b_ra"][0]), fm(inp["b_b_ri"][0]), fm(inp["b_lam"][0])], axis=1))
    m["b_w_ra"] = np.ascontiguousarray(inp["b_w_ra"][0])
    m["b_w_ri"] = np.ascontiguousarray(inp["b_w_ri"][0])
    m["b_w_out"] = np.ascontiguousarray(inp["b_w_out"][0])
    m["mlp_w1"] = np.ascontiguousarray(inp["mlp_w1"])
    m["mlp_w2"] = np.ascontiguousarray(inp["mlp_w2"])
    m["fin_g"] = fm(inp["final_g"])
    m["triu"] = np.triu(np.ones((128, 128), f32))
    return m


_NC_CACHE = {}


def kernel(**inputs):
    inp = {k: np.asarray(v, dtype=np.float32) for k, v in inputs.items()}
    if "full" not in _NC_CACHE:
        _NC_CACHE["full"] = build()
    nc = _NC_CACHE["full"]
    in_maps = [_prep_inputs(inp, b) for b in range(NB)]
    res = run_bass_kernel_spmd(nc, in_maps, core_ids=list(range(NCORES)))
    out = np.stack([np.ascontiguousarray(res.results[b]["outT"].T) for b in range(NB)], axis=0)
    return out.astype(np.float32)
```

```python
import numpy as np
import concourse.bass as bass
import concourse.mybir as mybir
from concourse.bass_utils import run_bass_kernel_spmd

F32 = mybir.dt.float32
BF16 = mybir.dt.bfloat16
ALU = mybir.AluOpType
AF = mybir.ActivationFunctionType

D = 2048
SEQ = 4096
NB = 4
KC = 16
T = 512
DFF = 8192
IN_A = 6160
EPS = 1e-6
NCORES = 8
TOK = 2048
NBLK = TOK // 512
WDEPTH = 2


class Res:
    __slots__ = ("name", "w", "r")

    def __init__(self, name):
        self.name = name
        self.w = None
        self.r = {}

    def add_r(self, ev):
        k = id(ev[0])
        if k not in self.r or self.r[k][1] < ev[1]:
            self.r[k] = ev


class Eng:
    def __init__(self, nc, name, h):
        self.name = name
        self.h = h
        self.sem = nc.alloc_semaphore(name="s_" + name)
        self.count = 0
        self.known = {}


class FW:
    def __init__(self, nc):
        self.nc = nc
        self.pe = Eng(nc, "pe", nc.tensor)
        self.act = Eng(nc, "act", nc.scalar)
        self.dve = Eng(nc, "dve", nc.vector)
        self.pool = Eng(nc, "pool", nc.gpsimd)
        self.sp = Eng(nc, "sp", nc.sync)
        self.dsems = {}

    def _wait(self, eng, evs):
        best = {}
        for ev in evs:
            if ev is None:
                continue
            sem, val, owner = ev
            if owner is eng:
                continue
            k = id(sem)
            if k not in best or best[k][1] < val:
                best[k] = ev
        for k, (sem, val, owner) in best.items():
            if eng.known.get(k, 0) >= val:
                continue
            eng.h.wait_ge(sem, val)
            eng.known[k] = val

    @staticmethod
    def _deps(reads, writes):
        evs = []
        for r in reads:
            evs.append(r.w)
        for w in writes:
            evs.append(w.w)
            evs.extend(w.r.values())
        return evs

    @staticmethod
    def _record(ev, reads, writes):
        for r in reads:
            r.add_r(ev)
        for w in writes:
            w.w = ev
            w.r = {}

    def op(self, eng, fn, reads=(), writes=()):
        self._wait(eng, self._deps(reads, writes))
        ins = fn()
        eng.count += 1
        ins.then_inc(eng.sem, 1)
        self._record((eng.sem, eng.count, eng), reads, writes)

    def mm(self, fns, reads=(), writes=()):
        eng = self.pe
        self._wait(eng, self._deps(reads, writes))
        ins = None
        for f in fns:
            ins = f()
        eng.count += 1
        ins.then_inc(eng.sem, 1)
        self._record((eng.sem, eng.count, eng), reads, writes)

    def dma(self, q, out, in_, key, reads=(), writes=()):
        if key not in self.dsems:
            self.dsems[key] = [self.nc.alloc_semaphore(name="d_" + key), 0]
        ds = self.dsems[key]
        self._wait(q, self._deps(reads, writes))
        q.h.dma_start(out=out, in_=in_).then_inc(ds[0], 16)
        ds[1] += 16
        ev = (ds[0], ds[1], None)
        self._record(ev, reads, writes)
        return ev


def _prune(res_list):
    pass


def build(ncores=NCORES, dbg=None, stub_cc=False):
    nblk = NBLK
    nc = bass.Bass("TRN2", target_bir_lowering=False)
    fw = FW(nc)
    PE, ACT, DVE, POOL, SP = fw.pe, fw.act, fw.dve, fw.pool, fw.sp
    mm = nc.tensor.matmul
    _zb = []

    def act_(out, in_, func, bias=None, scale=1.0):
        if func == AF.Copy:
            assert bias is None
            return nc.scalar.activation(out, in_, func, scale=scale)
        if bias is None:
            bias = _zb[0][:, 0:1]
        elif isinstance(bias, float):
            raise ValueError("float bias")
        return nc.scalar.activation(out, in_, func, bias=bias, scale=scale)

    def din(name, shape):
        return nc.dram_tensor(name, shape, F32, kind="ExternalInput").ap()

    xT = din("xT", [D, TOK])
    xpre = din("xpre", [D, TOK])
    flag_d = din("flag", [128, 1])
    cT = din("cT", [128, KC])
    ada_w = din("ada_w", [4, D, 3 * D])
    ada_b = din("ada_b", [128, 4 * 48])
    a_w_in = din("a_w_in", [D, IN_A])
    a_bg = din("a_bg", [128, 64])
    a_ng = din("a_ng", [128, KC])
    a_w_out = din("a_w_out", [D, D])
    b_w_in = din("b_w_in", [D, 2 * D])
    b_cw = din("b_cw", [128, KC * 4])
    b_vec = din("b_vec", [128, 4 * KC])
    b_w_ra = din("b_w_ra", [8, 256, 256])
    b_w_ri = din("b_w_ri", [8, 256, 256])
    b_w_out = din("b_w_out", [D, D])
    mlp_w1 = din("mlp_w1", [2, D, DFF])
    mlp_w2 = din("mlp_w2", [2, DFF, D])
    fin_g = din("fin_g", [128, KC])
    triu = din("triu", [128, 128])
    outT = nc.dram_tensor("outT", [D, TOK], F32, kind="ExternalOutput").ap()
    x2T = nc.dram_tensor("x2T", [D, TOK], F32).ap()
    y0T = nc.dram_tensor("y0T", [D, TOK], F32).ap()
    AgT = nc.dram_tensor("AgT", [D, TOK], F32).ap()
    halo_src = nc.dram_tensor("halo_src", [128, 64], F32)
    halo_all = nc.dram_tensor("halo_all", [256, 64], F32)
    hend_src = nc.dram_tensor("hend_src", [128, KC], F32)
    hend_all = nc.dram_tensor("hend_all", [256, KC], F32)
    rx2 = [Res(f"x2_{j}") for j in range(4)]
    ry0 = [Res(f"y0_{j}") for j in range(4)]
    rAg = [Res(f"Ag_{j}") for j in range(4)]
    rhalo_src, rhalo_all, rhend_src, rhend_all = Res("hs"), Res("ha"), Res("es"), Res("ea")

    def sb(name, shape, dt=F32):
        return nc.alloc_sbuf_tensor(name, shape, dt), Res(name)

    U32, rU32 = sb("U32", [128, 128])
    U4, rU4 = sb("U4", [128, 4, 128])
    ones_bf, rones = sb("ones_bf", [128, 128], BF16)
    c_sb, rc = sb("c_sb", [128, KC])
    sc_bf, rsc = sb("sc_bf", [128, KC], BF16)
    modv, rmod = sb("modv", [128, 4, 48])
    adab, radab = sb("adab", [128, 4, 48])
    bg_sb, rbg = sb("bg_sb", [128, 4, 16])
    ng_sb, rng_ = sb("ng_sb", [128, KC])
    cw_sb, rcw = sb("cw_sb", [128, KC, 4])
    bv_sb, rbv = sb("bv_sb", [128, 4, KC])
    m8, rm8 = sb("m8", [128, KC])
    m16, rm16 = sb("m16", [128, KC])
    fg_sb, rfg = sb("fg_sb", [128, KC])
    eps_sb, reps = sb("eps_sb", [128, 1])
    flag_sb, rflag = sb("flag_sb", [128, 1])
    ones32, rones32 = sb("ones32", [128, 128])
    zeros_t, rzeros = sb("zeros_t", [128, T])
    Ast, rAst = sb("Ast", [128, KC])
    h_init, rhinit = sb("h_init", [128, KC])
    x_halo, rxhalo = sb("x_halo", [128, KC, 4])
    hT_halo, rhhalo = sb("hT_halo", [128, KC, 4], BF16)
    zero_sb, rzero = sb("zero_sb", [128, 1])
    _zb.append(zero_sb)
    one_sb, rone = sb("one_sb", [128, 1])
    hb_sb, rhb = sb("hb_sb", [128, 2, KC])
    m4, rm4 = sb("m4", [128, KC])
    wg_bf, rwg = sb("wg_bf", [128, KC, 16], BF16)
    wra_bf, rwra = sb("wra_bf", [128, 8, 2, 256], BF16)
    wri_bf, rwri = sb("wri_bf", [128, 8, 2, 256], BF16)

    x_sb, rx = sb("x_sb", [128, KC, T])
    hT, rh = sb("hT", [128, KC, T], BF16)
    Cf, rCf = [], []
    for h in range(8):
        a, r = sb(f"Cf{h}", [128, 384])
        Cf.append(a)
        rCf.append(r)
    Cb, rCb = [], []
    for h in range(8):
        a, r = sb(f"Cb{h}", [128, 384], BF16)
        Cb.append(a)
        rCb.append(r)
    eL, reL = sb("eL", [128, 2, 8])
    hst, rhst = sb("hst", [128, KC])
    halo, rhalo = sb("halo", [128, KC, 3])

    wslot = []
    for i in range(WDEPTH):
        wslot.append(sb(f"wslot{i}", [128, KC, 512], BF16))

    SCR_BYTES = 80 * 1024
    scr_used = [0]
    scr = nc.alloc_sbuf_tensor("scr", [128, SCR_BYTES // 4], F32)
    rscr_all = Res("scr_all")

    PS = [nc.alloc_psum_tensor(f"ps{i}", [128, 512], F32) for i in range(8)]
    rPS = [Res(f"ps{i}") for i in range(8)]

    plan = []

    def wtile(src2d, r0, c0, ncols):
        return src2d[r0:r0 + D, c0:c0 + ncols].rearrange("(kc p) n -> p kc n", p=128), ncols

    for cmb in range(4):
        for i in range(12):
            plan.append(wtile(ada_w[cmb], 0, i * 512, 512))
    for _ in range(nblk):
        for i in range(8):
            plan.append(wtile(a_w_in, 0, i * 512, 512))
        for i in range(8, 12):
            plan.append(wtile(a_w_in, 0, i * 512, 512))
        for i in range(4):
            plan.append(wtile(a_w_out, 0, i * 512, 512))
        for lyr in range(2):
            if lyr == 1:
                for i in range(4):
                    plan.append(wtile(b_w_in, 0, i * 512, 512))
                    plan.append(wtile(b_w_in, 0, D + i * 512, 512))
                for i in range(4):
                    plan.append(wtile(b_w_out, 0, i * 512, 512))
            for i in range(16):
                plan.append(wtile(mlp_w1[lyr], 0, i * 512, 512))
            for cg in range(4):
                for kg in range(4):
                    plan.append(wtile(mlp_w2[lyr], kg * D, cg * 512, 512))
    plan = []
    for cmb in range(4):
        for i in range(12):
            plan.append(wtile(ada_w[cmb], 0, i * 512, 512))
    for _ in range(nblk):
        for i in range(2, 8):
            plan.append(wtile(a_w_in, 0, i * 512, 512))
    for _ in range(nblk):
        for i in range(12):
            plan.append(wtile(a_w_in, 0, i * 512, 512))
        for i in range(4):
            plan.append(wtile(a_w_out, 0, i * 512, 512))
        for i in range(16):
            plan.append(wtile(mlp_w1[0], 0, i * 512, 512))
        for cg in range(4):
            for kg in range(4):
                plan.append(wtile(mlp_w2[0], kg * D, cg * 512, 512))
    for _ in range(nblk):
        for i in range(4):
            plan.append(wtile(b_w_in, 0, i * 512, 512))
            plan.append(wtile(b_w_in, 0, D + i * 512, 512))
    for _ in range(nblk):
        for i in range(4):
            plan.append(wtile(b_w_out, 0, i * 512, 512))
        for i in range(16):
            plan.append(wtile(mlp_w1[1], 0, i * 512, 512))
        for cg in range(4):
            for kg in range(4):
                plan.append(wtile(mlp_w2[1], kg * D, cg * 512, 512))

    wstate = {"next_load": 0, "next_use": 0}

    def _issue_load():
        i = wstate["next_load"]
        if i >= len(plan):
            return
        src, ncols = plan[i]
        t, r = wslot[i % WDEPTH]
        fw.dma(POOL, t[:, :, 0:ncols], src, f"w{i % WDEPTH}", writes=[r])
        wstate["next_load"] = i + 1

    def wnext():
        i = wstate["next_use"]
        if i == 0:
            for _ in range(WDEPTH):
                _issue_load()
        else:
            _issue_load()
        wstate["next_use"] = i + 1
        return wslot[i % WDEPTH]

    def cload(t, r, src):
        fw.dma(SP, t, src, "const", writes=[r])
        SP.h.wait_ge(fw.dsems["const"][0], fw.dsems["const"][1])
        SP.known[id(fw.dsems["const"][0])] = fw.dsems["const"][1]

    cload(U32[:, :], rU32, triu)
    cload(c_sb[:, :], rc, cT)
    cload(adab[:, :, :], radab, ada_b.rearrange("p (a b) -> p a b", a=4))
    cload(bg_sb[:, :, :], rbg, a_bg.rearrange("p (a b) -> p a b", a=4))
    cload(ng_sb[:, :], rng_, a_ng)
    cload(cw_sb[:, :, :], rcw, b_cw.rearrange("p (a b) -> p a b", a=KC))
    cload(bv_sb[:, :, :], rbv, b_vec.rearrange("p (a b) -> p a b", a=4))
    cload(fg_sb[:, :], rfg, fin_g)
    cload(flag_sb[:, :], rflag, flag_d)
    fw.dma(POOL, wg_bf[:, :, :], a_w_in[:, 6144:6160].rearrange("(kc p) n -> p kc n", p=128), "cw0", writes=[rwg])
    fw.dma(POOL, wra_bf[:, :, :, :], b_w_ra.rearrange("n (kc p) d -> p n kc d", p=128), "cw1", writes=[rwra])
    fw.dma(POOL, wri_bf[:, :, :, :], b_w_ri.rearrange("n (kc p) d -> p n kc d", p=128), "cw2", writes=[rwri])

    for j in range(4):
        fw.op(DVE, lambda j=j: nc.vector.tensor_copy(U4[:, j, :], U32[:, :]), reads=[rU32], writes=[rU4])
    fw.op(DVE, lambda: nc.vector.memset(ones_bf[:, :], 1.0), writes=[rones])
    fw.op(DVE, lambda: nc.vector.memset(eps_sb[:, :], EPS), writes=[reps])
    fw.op(DVE, lambda: nc.vector.memset(zero_sb[:, :], 0.0), writes=[rzero])
    fw.op(DVE, lambda: nc.vector.memset(ones32[:, :], 1.0), writes=[rones32])
    fw.op(DVE, lambda: nc.vector.memset(zeros_t[:, :], 0.0), writes=[rzeros])
    fw.op(DVE, lambda: nc.vector.memset(Ast[:, :], 1.0), writes=[rAst])
    fw.op(DVE, lambda: nc.vector.memset(one_sb[:, :], 1.0), writes=[rone])
    fw.op(DVE, lambda: nc.vector.tensor_scalar(hb_sb[:, :, :], bv_sb[:, 1:3, :], 0.5, None, ALU.mult), reads=[rbv], writes=[rhb])
    for h in range(8):
        fw.op(DVE, lambda h=h: nc.vector.memset(Cf[h][:, :], 0.0), writes=[rCf[h]])
        fw.op(DVE, lambda h=h: nc.vector.memset(Cb[h][:, :], 0.0), writes=[rCb[h]])
    fw.op(DVE, lambda: nc.vector.memset(eL[:, :, :], 1.0), writes=[reL])
    fw.op(DVE, lambda: nc.vector.memset(hst[:, :], 0.0), writes=[rhst])
    fw.op(DVE, lambda: nc.vector.memset(halo[:, :, :], 0.0), writes=[rhalo])
    fw.op(ACT, lambda: act_(sc_bf[:, :], c_sb[:, :], AF.Silu), reads=[rc, rzero, rone, reps], writes=[rsc])
    fw.op(ACT, lambda: act_(m8[:, :], bv_sb[:, 3, :], AF.Exp, scale=-1.0), reads=[rbv], writes=[rm8])
    fw.op(ACT, lambda: act_(m16[:, :], m8[:, :], AF.Ln, bias=one_sb[:, 0:1]), reads=[rm8], writes=[rm16])
    fw.op(DVE, lambda: nc.vector.tensor_scalar(m8[:, :], m16[:, :], -8.0, None, ALU.mult), reads=[rm16], writes=[rm8])
    fw.op(DVE, lambda: nc.vector.tensor_scalar(m4[:, :], m16[:, :], -4.0, None, ALU.mult), reads=[rm16], writes=[rm4])
    fw.op(DVE, lambda: nc.vector.tensor_scalar(m16[:, :], m16[:, :], -16.0, None, ALU.mult), reads=[rm16], writes=[rm16])

    if dbg == "mdbg":
        dbt, rdbt = sb("dbt", [128, 96])
        fw.op(ACT, lambda: act_(dbt[:, 0:16], bv_sb[:, 3, :], AF.Exp, scale=-1.0), reads=[rbv], writes=[rdbt])
        fw.op(ACT, lambda: act_(dbt[:, 16:32], dbt[:, 0:16], AF.Ln, bias=one_sb[:, 0:1]), reads=[rdbt], writes=[rdbt])
        fw.op(ACT, lambda: act_(dbt[:, 32:48], dbt[:, 0:16], AF.Ln, bias=one_sb[:, 0:1]), reads=[rdbt, rone], writes=[rdbt])
        fw.op(DVE, lambda: nc.vector.tensor_scalar(dbt[:, 48:64], dbt[:, 0:16], 1.0, None, ALU.add), reads=[rdbt], writes=[rdbt])
        fw.op(ACT, lambda: act_(dbt[:, 64:80], dbt[:, 48:64], AF.Ln), reads=[rdbt], writes=[rdbt])
        fw.op(DVE, lambda: nc.vector.tensor_copy(dbt[:, 80:96], m16[:, :]), reads=[rm16], writes=[rdbt])
        ev = fw.dma(SP, outT[0:128, 0:96], dbt[:, :], "out", reads=[rdbt])
        fw._wait(SP, [ev])
        return nc
    for cmb in range(4):
        for i in range(12):
            wt, rw = wnext()
            fns = []
            for fc in range(4):
                col = i * 4 + fc
                for kc in range(KC):
                    fns.append(lambda fc=fc, kc=kc, col=col, wt=wt: mm(
                        PS[0][:, col:col + 1], wt[:, kc, fc * 128:(fc + 1) * 128], sc_bf[:, kc:kc + 1],
                        start=(kc == 0), stop=(kc == KC - 1)))
            fw.mm(fns, reads=[rw, rsc], writes=[rPS[0]])
        fw.op(DVE, lambda cmb=cmb: nc.vector.tensor_tensor(modv[:, cmb, :], PS[0][:, 0:48], adab[:, cmb, :], ALU.add),
              reads=[rPS[0], radab], writes=[rmod])
        fw.op(DVE, lambda cmb=cmb: nc.vector.tensor_scalar(modv[:, cmb, 16:32], modv[:, cmb, 16:32], 1.0, None, ALU.add),
              reads=[rmod], writes=[rmod])

    class Scr:
        def __init__(self):
            self.off = 0

        def take(self, name, shape, dt=F32):
            n = int(np.prod(shape[1:]))
            nbytes = n * (4 if dt == F32 else 2)
            nbytes = (nbytes + 31) // 32 * 32
            w0 = self.off // 4
            self.off += nbytes
            assert self.off <= SCR_BYTES, (name, self.off)
            v = scr[:, w0:w0 + nbytes // 4]
            if dt != F32:
                v = v.bitcast(dt)
            v = v[:, 0:n]
            if len(shape) == 3:
                v = v.rearrange("p (a b) -> p a b", a=shape[1])
            elif len(shape) == 4:
                v = v.rearrange("p (a b c) -> p a b c", a=shape[1], b=shape[2])
            return v

    phase_res = []

    def new_phase():
        prev = list(phase_res)
        phase_res.clear()
        return prev

    def adaln(cmb, x_sb=x_sb, rx=rx, hT=hT, rh=rh, n=T):
        sq = s.take_tmp("sq", [128, 2, T], BF16, nres=2)
        for kc in range(KC):
            fw.op(ACT, lambda kc=kc: act_(sq[0][:, kc % 2, 0:n], x_sb[:, kc, 0:n], AF.Square),
                  reads=[rx], writes=[sq[1][kc % 2]])
            fw.mm([lambda kc=kc: mm(PS[7][:, 0:n], ones_bf[:, :], sq[0][:, kc % 2, 0:n], start=(kc == 0), stop=(kc == KC - 1))],
                  reads=[sq[1][kc % 2], rones], writes=[rPS[7]])
        rstd = s.take_tmp("rstd", [128, T])
        fw.op(ACT, lambda: act_(rstd[0][:, 0:n], PS[7][:, 0:n], AF.Sqrt, bias=eps_sb[:, 0:1], scale=1.0 / D),
              reads=[rPS[7], reps], writes=[rstd[1]])
        fw.op(DVE, lambda: nc.vector.reciprocal(rstd[0][:, 0:n], rstd[0][:, 0:n]),
              reads=[rstd[1]], writes=[rstd[1]])
        tmp = s.take_tmp("adatmp", [128, 2, T], nres=2)
        for kc in range(KC):
            fw.op(DVE, lambda kc=kc: nc.vector.scalar_tensor_tensor(
                tmp[0][:, kc % 2, 0:n], x_sb[:, kc, 0:n], modv[:, cmb, 16 + kc:17 + kc], rstd[0][:, 0:n], ALU.mult, ALU.mult),
                reads=[rx, rmod, rstd[1]], writes=[tmp[1][kc % 2]])
            fw.op(ACT, lambda kc=kc: act_(hT[:, kc, 0:n], tmp[0][:, kc % 2, 0:n], AF.Identity,
                                                          bias=modv[:, cmb, kc:kc + 1]),
                  reads=[tmp[1][kc % 2], rmod], writes=[rh])

    class S:
        def __init__(self):
            self.scr = Scr()
            self.bufs = {}
            self.haz = {}

        def reset(self):
            for (_, res) in self.bufs.values():
                for o in (res if isinstance(res, list) else [res]):
                    evs = list(o.r.values())
                    if o.w is not None:
                        evs.append(o.w)
                    for ev in evs:
                        k = id(ev[0])
                        if k not in self.haz or self.haz[k][1] < ev[1]:
                            self.haz[k] = ev
            self.scr = Scr()
            self.bufs = {}

        def take_tmp(self, name, shape, dt=F32, nres=0):
            if name in self.bufs:
                return self.bufs[name]
            ap = self.scr.take(name, shape, dt)
            res = [Res(name + str(i)) for i in range(nres)] if nres else Res(name)
            for rr in (res if isinstance(res, list) else [res]):
                rr.r = dict(self.haz)
            self.bufs[name] = (ap, res)
            return self.bufs[name]

    s = S()

    def proj_fm(ntiles, evac):
        for i in range(ntiles):
            wt, rw = wnext()
            for fc in range(4):
                bank = (i * 4 + fc) % 4
                fw.mm([lambda kc=kc, fc=fc, wt=wt, bank=bank: mm(
                    PS[bank][:, :], wt[:, kc, fc * 128:(fc + 1) * 128], hT[:, kc, :],
                    start=(kc == 0), stop=(kc == KC - 1)) for kc in range(KC)],
                    reads=[rw, rh], writes=[rPS[bank]])
                evac(i, fc, bank)

    def resid_evac(gate_col0):
        def ev(i, fc, bank):
            f = i * 4 + fc
            fw.op(DVE, lambda: nc.vector.scalar_tensor_tensor(
                x_sb[:, f, :], PS[bank][:, :], modv[:, gate_col0[0], gate_col0[1] + f:gate_col0[1] + f + 1],
                x_sb[:, f, :], ALU.mult, ALU.add), reads=[rPS[bank], rmod, rx], writes=[rx])
        return ev

    def mlp(lyr, cmb):
        s.reset()
        adaln(cmb)
        s.reset()
        aT, raT = s.take_tmp("aT", [128, 64, T], BF16)
        rl = s.take_tmp("relu", [128, 2, T], nres=2)
        rrl = rl[1]
        cnt = [0]

        def ev1(i, fc, bank):
            f = i * 4 + fc
            k = cnt[0] % 2
            cnt[0] += 1
            fw.op(ACT, lambda: act_(rl[0][:, k, :], PS[bank][:, :], AF.Relu),
                  reads=[rPS[bank]], writes=[rrl[k]])
            fw.op(DVE, lambda: nc.vector.tensor_tensor(aT[:, f, :], rl[0][:, k, :], rl[0][:, k, :], ALU.mult),
                  reads=[rrl[k]], writes=[raT])
        proj_fm(16, ev1)
        for cg in range(4):
            for kg in range(4):
                wt, rw = wnext()
                for fc in range(4):
                    fw.mm([lambda kc=kc, fc=fc, wt=wt, kg=kg: mm(
                        PS[fc][:, :], wt[:, kc, fc * 128:(fc + 1) * 128], aT[:, kg * 16 + kc, :],
                        start=(kg == 0 and kc == 0), stop=(kg == 3 and kc == KC - 1)) for kc in range(KC)],
                        reads=[rw, raT], writes=[rPS[fc]])
            for fc in range(4):
                f = cg * 4 + fc
                fw.op(DVE, lambda f=f, fc=fc: nc.vector.scalar_tensor_tensor(
                    x_sb[:, f, :], PS[fc][:, :], modv[:, cmb, 32 + f:33 + f], x_sb[:, f, :], ALU.mult, ALU.add),
                    reads=[rPS[fc], rmod, rx], writes=[rx])

    def mlstm_layer(chunk_base, prepass=False):
        cmb = 0
        s.reset()
        adaln(cmb)
        s.reset()
        qT, rq = s.take_tmp("qT", [128, 8, T], BF16)
        kT, rk = s.take_tmp("kT", [128, 8, T], BF16)
        ktok, rkt = s.take_tmp("ktok", [128, 4, 1024], BF16)
        vaug, rv = s.take_tmp("vaug", [128, 4, 8, 256], BF16)
        yT, ry = s.take_tmp("yT", [128, KC, T], BF16)
        zs, rzs = s.take_tmp("zs", [128, 4, 16])
        nlf, rnlf = s.take_tmp("nlf", [128, 4, 8])
        ecol, recol = s.take_tmp("ecol", [128, 4, 8])
        nlfrep, rnr = s.take_tmp("nlfrep", [128, 8, 128])
        rb, rrb = s.take_tmp("rb", [128, 4, 128])
        vs, rvs = s.take_tmp("vs", [128, 4, 384], BF16)
        MT, rMT = s.take_tmp("MT", [128, 4, 128], BF16)
        den, rden = s.take_tmp("den", [128, 4, 128])
        hh, rhh = s.take_tmp("hh", [128, 2, 512])
        sqh, rsqh = s.take_tmp("sqh", [128, 2, 512], BF16)
        rsd, rrsd = s.take_tmp("rsd", [128, 512])

        for i in (range(2, 8) if prepass else range(8)):
            wt, rw = wnext()
            if i < 4 and not prepass:
                for fc in range(4):
                    bank = fc
                    fw.mm([lambda kc=kc, fc=fc, wt=wt, bank=bank: mm(
                        PS[bank][:, :], wt[:, kc, fc * 128:(fc + 1) * 128], hT[:, kc, :],
                        start=(kc == 0), stop=(kc == KC - 1)) for kc in range(KC)],
                        reads=[rw, rh], writes=[rPS[bank]])
                    hd = (i % 2) * 4 + fc
                    if i < 2:
                        fw.op(ACT, lambda hd=hd, bank=bank: act_(
                            qT[:, hd, :], PS[bank][:, :], AF.Copy, scale=float(128 ** -0.5)),
                            reads=[rPS[bank]], writes=[rq])
                    else:
                        fw.op(ACT, lambda hd=hd, bank=bank: act_(
                            kT[:, hd, :], PS[bank][:, :], AF.Copy), reads=[rPS[bank]], writes=[rk])
            if i >= 2:
                for tl in range(4):
                    bank = 4 + (tl % 2)
                    fw.mm([lambda kc=kc, tl=tl, wt=wt, bank=bank: mm(
                        PS[bank][:, :], hT[:, kc, tl * 128:(tl + 1) * 128], wt[:, kc, :],
                        start=(kc == 0), stop=(kc == KC - 1)) for kc in range(KC)],
                        reads=[rw, rh], writes=[rPS[bank]])
                    if i < 4:
                        fw.op(DVE, lambda tl=tl, bank=bank, i=i: nc.vector.tensor_copy(
                            ktok[:, tl, (i - 2) * 512:(i - 1) * 512], PS[bank][:, :]),
                            reads=[rPS[bank]], writes=[rkt])
                    else:
                        h0 = (i - 4) * 2
                        fw.op(DVE, lambda tl=tl, bank=bank, h0=h0: nc.vector.tensor_copy(
                            vaug[:, tl, h0:h0 + 2, 0:256], PS[bank][:, :].rearrange("p (a b) -> p a b", a=2)),
                            reads=[rPS[bank]], writes=[rv])
        fw.mm([lambda kc=kc, tl=tl: mm(PS[6][:, tl * 16:(tl + 1) * 16], hT[:, kc, tl * 128:(tl + 1) * 128], wg_bf[:, kc, :],
                                       start=(kc == 0), stop=(kc == KC - 1)) for tl in range(4) for kc in range(KC)],
              reads=[rh, rwg], writes=[rPS[6]])
        fw.op(DVE, lambda: nc.vector.tensor_tensor(zs[:, :, :], PS[6][:, 0:64].rearrange("p (a b) -> p a b", a=4),
                                                   bg_sb[:, :, :], ALU.add), reads=[rPS[6], rbg], writes=[rzs])
        fw.op(ACT, lambda: act_(nlf[:, :, :], zs[:, :, 8:16], AF.Exp, scale=-1.0), reads=[rzs], writes=[rnlf])
        fw.op(ACT, lambda: act_(nlf[:, :, :], nlf[:, :, :], AF.Ln, bias=one_sb[:, 0:1]), reads=[rnlf], writes=[rnlf])
        fw.mm([lambda tl=tl: mm(PS[6][:, 64 + tl * 8:72 + tl * 8], U32[:, :], nlf[:, tl, :], start=True, stop=True)
               for tl in range(4)] +
              [lambda tl=tl: mm(PS[6][:, 96 + tl * 8:104 + tl * 8], ones32[:, :], nlf[:, tl, :], start=True, stop=True)
               for tl in range(4)], reads=[rU32, rones32, rnlf], writes=[rPS[6]])
        fw.op(DVE, lambda: nc.vector.tensor_tensor(ecol[:, :, :], PS[6][:, 64:96].rearrange("p (a b) -> p a b", a=4),
                                                   zs[:, :, 0:8], ALU.add), reads=[rPS[6], rzs], writes=[recol])
        fw.op(ACT, lambda: act_(ecol[:, :, :], ecol[:, :, :], AF.Exp), reads=[recol], writes=[recol])

        for tl in range(4):
            ch = chunk_base + tl
            par = ch % 2
            tsl = slice(tl * 128, (tl + 1) * 128)
            if prepass:
                fw.op(ACT, lambda tl=tl, par=par: act_(eL[:, par, :], PS[6][:, 96 + tl * 8:104 + tl * 8], AF.Exp, scale=-1.0),
                      reads=[rPS[6]], writes=[reL])
                for hg in range(2):
                    hs = [hg * 4 + j for j in range(4)]
                    fw.op(DVE, lambda tl=tl, hg=hg: nc.vector.tensor_tensor(
                        vs[:, :, 0:256], vaug[:, tl, hg * 4:hg * 4 + 4, :],
                        ecol[:, tl, hg * 4:hg * 4 + 4].unsqueeze(2).broadcast_to([128, 4, 256]), ALU.mult),
                        reads=[rv, recol], writes=[rvs])
                    fw.op(DVE, lambda tl=tl, hg=hg: nc.vector.tensor_copy(
                        vs[:, :, 256:384], ecol[:, tl, hg * 4:hg * 4 + 4].unsqueeze(2).broadcast_to([128, 4, 128])),
                        reads=[recol], writes=[rvs])
                    for j, h in enumerate(hs):
                        bank = 4 + (j % 2)
                        fw.mm([lambda j=j, h=h, tl=tl, bank=bank: mm(
                            PS[bank][:, 0:384], ktok[:, tl, h * 128:(h + 1) * 128], vs[:, j, :], start=True, stop=True)],
                            reads=[rkt, rvs], writes=[rPS[bank]])
                        fw.op(DVE, lambda h=h, bank=bank, par=par: nc.vector.scalar_tensor_tensor(
                            Cf[h][:, :], Cf[h][:, :], eL[:, 1 - par, h:h + 1], PS[bank][:, 0:384], ALU.mult, ALU.add),
                            reads=[rCf[h], reL, rPS[bank]], writes=[rCf[h]])
                        fw.op(ACT, lambda h=h, par=par: act_(
                            Cb[h][:, :], Cf[h][:, :], AF.Copy, scale=eL[:, par, h:h + 1]),
                            reads=[rCf[h], reL], writes=[rCb[h]])
                continue
            fw.op(DVE, lambda tl=tl: nc.vector.tensor_copy(
                nlfrep[:, :, :], nlf[:, tl, :].unsqueeze(2).broadcast_to([128, 8, 128])), reads=[rnlf], writes=[rnr])
            for hg in range(2):
                hs = [hg * 4 + j for j in range(4)]
                fw.mm([lambda j=j, h=h: mm(PS[7][:, j * 128:(j + 1) * 128], nlfrep[:, h, :], U32[:, :], start=True, stop=True)
                       for j, h in enumerate(hs)], reads=[rnr, rU32], writes=[rPS[7]])
                fw.op(ACT, lambda: act_(rb[:, :, :], PS[7][:, :].rearrange("p (a b) -> p a b", a=4), AF.Exp),
                      reads=[rPS[7]], writes=[rrb])
                fw.op(ACT, lambda hg=hg, par=par: act_(
                    eL[:, par, hg * 4:hg * 4 + 4], PS[7][:, :].rearrange("p (a b) -> p a b", a=4)[:, :, 127], AF.Exp, scale=-1.0),
                    reads=[rPS[7]], writes=[reL])
                fw.op(DVE, lambda tl=tl, hg=hg: nc.vector.tensor_tensor(
                    vs[:, :, 0:256], vaug[:, tl, hg * 4:hg * 4 + 4, :],
                    ecol[:, tl, hg * 4:hg * 4 + 4].unsqueeze(2).broadcast_to([128, 4, 256]), ALU.mult),
                    reads=[rv, recol], writes=[rvs])
                fw.op(DVE, lambda tl=tl, hg=hg: nc.vector.tensor_copy(
                    vs[:, :, 256:384], ecol[:, tl, hg * 4:hg * 4 + 4].unsqueeze(2).broadcast_to([128, 4, 128])),
                    reads=[recol], writes=[rvs])
                fw.mm([lambda j=j, h=h, tsl=tsl: mm(PS[0][:, j * 128:(j + 1) * 128], kT[:, h, tsl], qT[:, h, tsl], start=True, stop=True)
                       for j, h in enumerate(hs)], reads=[rk, rq], writes=[rPS[0]])
                fw.op(DVE, lambda: nc.vector.tensor_tensor(MT[:, :, :], PS[0][:, :].rearrange("p (a b) -> p a b", a=4),
                                                           U4[:, :, :], ALU.mult), reads=[rPS[0], rU4], writes=[rMT])
                for ec in range(3):
                    fns = []
                    for j, h in enumerate(hs):
                        fns.append(lambda j=j, h=h, ec=ec, tsl=tsl: mm(
                            PS[1 + ec][:, j * 128:(j + 1) * 128], Cb[h][:, ec * 128:(ec + 1) * 128], qT[:, h, tsl],
                            start=True, stop=False))
                        fns.append(lambda j=j, h=h, ec=ec: mm(
                            PS[1 + ec][:, j * 128:(j + 1) * 128], vs[:, j, ec * 128:(ec + 1) * 128], MT[:, j, :],
                            start=False, stop=True))
                    fw.mm(fns, reads=[rCb[h] for h in hs] + [rq, rvs, rMT], writes=[rPS[1 + ec]])
                fw.op(ACT, lambda: act_(den[:, :, :], PS[3][:, :].rearrange("p (a b) -> p a b", a=4), AF.Abs),
                      reads=[rPS[3]], writes=[rden])
                fw.op(DVE, lambda: nc.vector.tensor_tensor(den[:, :, :], den[:, :, :], rb[:, :, :], ALU.max),
                      reads=[rden, rrb], writes=[rden])
                fw.op(DVE, lambda: nc.vector.reciprocal(den[:, :, :], den[:, :, :]), reads=[rden], writes=[rden])
                for ec in range(2):
                    fw.op(DVE, lambda ec=ec: nc.vector.tensor_tensor(
                        hh[:, ec, :], PS[1 + ec][:, :], den[:, :, :].rearrange("p a b -> p (a b)"), ALU.mult),
                        reads=[rPS[1 + ec], rden], writes=[rhh])
                    fw.op(ACT, lambda ec=ec: act_(sqh[:, ec, :], hh[:, ec, :], AF.Square),
                          reads=[rhh], writes=[rsqh])
                fw.mm([lambda ec=ec: mm(PS[4][:, :], ones_bf[:, :], sqh[:, ec, :], start=(ec == 0), stop=(ec == 1))
                       for ec in range(2)], reads=[rones, rsqh], writes=[rPS[4]])
                fw.op(ACT, lambda: act_(rsd[:, :], PS[4][:, :], AF.Sqrt, bias=eps_sb[:, 0:1], scale=1.0 / 256),
                      reads=[rPS[4], reps], writes=[rrsd])
                fw.op(DVE, lambda: nc.vector.reciprocal(rsd[:, :], rsd[:, :]),
                      reads=[rrsd], writes=[rrsd])
                for ec in range(2):
                    fw.op(DVE, lambda ec=ec, hg=hg, tsl=tsl: nc.vector.tensor_tensor(
                        yT[:, hg * 8 + ec:hg * 8 + 8:2, tsl], hh[:, ec, :].rearrange("p (a b) -> p a b", a=4),
                        rsd[:, :].rearrange("p (a b) -> p a b", a=4), ALU.mult), reads=[rhh, rrsd], writes=[ry])
                for j, h in enumerate(hs):
                    bank = 5 + (j % 2)
                    fw.mm([lambda j=j, h=h, tl=tl, bank=bank: mm(
                        PS[bank][:, 0:384], ktok[:, tl, h * 128:(h + 1) * 128], vs[:, j, :], start=True, stop=True)],
                        reads=[rkt, rvs], writes=[rPS[bank]])
                    fw.op(DVE, lambda h=h, bank=bank, par=par: nc.vector.scalar_tensor_tensor(
                        Cf[h][:, :], Cf[h][:, :], eL[:, 1 - par, h:h + 1], PS[bank][:, 0:384], ALU.mult, ALU.add),
                        reads=[rCf[h], reL, rPS[bank]], writes=[rCf[h]])
                    fw.op(ACT, lambda h=h, par=par: act_(
                        Cb[h][:, :], Cf[h][:, :], AF.Copy, scale=eL[:, par, h:h + 1]),
                        reads=[rCf[h], reL], writes=[rCb[h]])

        if prepass:
            return
        cnt = [0]

        def evo(i, fc, bank):
            f = i * 4 + fc
            k = cnt[0] % 2
            cnt[0] += 1
            fw.op(ACT, lambda: act_(hh[:, k, :], PS[bank][:, :], AF.Sigmoid),
                  reads=[rPS[bank]], writes=[rhh])
            fw.op(DVE, lambda: nc.vector.scalar_tensor_tensor(
                yT[:, f, :], hh[:, k, :], ng_sb[:, f:f + 1], yT[:, f, :], ALU.mult, ALU.mult),
                reads=[rhh, rng_, ry], writes=[ry])
        proj_fm(4, evo)
        for i in range(4):
            wt, rw = wnext()
            for fc in range(4):
                bank = fc
                f = i * 4 + fc
                fw.mm([lambda kc=kc, fc=fc, wt=wt, bank=bank: mm(
                    PS[bank][:, :], wt[:, kc, fc * 128:(fc + 1) * 128], yT[:, kc, :],
                    start=(kc == 0), stop=(kc == KC - 1)) for kc in range(KC)],
                    reads=[rw, ry], writes=[rPS[bank]])
                fw.op(DVE, lambda f=f, bank=bank: nc.vector.scalar_tensor_tensor(
                    x_sb[:, f, :], PS[bank][:, :], modv[:, cmb, 32 + f:33 + f], x_sb[:, f, :], ALU.mult, ALU.add),
                    reads=[rPS[bank], rmod, rx], writes=[rx])

    def rglru_front(blk):
        cmb = 2
        s.reset()
        adaln(cmb)
        if blk == 0:
            adaln(cmb, x_sb=x_halo, rx=rxhalo, hT=hT_halo, rh=rhhalo, n=4)
        s.reset()
        ysg = s.take_tmp("ysg", [128, 2, T], nres=2)
        asg = s.take_tmp("asg", [128, 2, T], nres=2)
        acum, racum = s.take_tmp("acum", [128, T])
        xbp, rxbp = s.take_tmp("xbp", [128, 4, T + 3])
        xc, rxc = s.take_tmp("xc", [128, 4, T])
        xcb, rxcb = s.take_tmp("xcb", [128, 4, T], BF16)
        gg, rgg = s.take_tmp("gg", [128, 4, T], BF16)
        tm = [s.take_tmp(f"t{i}", [128, T]) for i in range(4)]
        ta = [s.take_tmp(f"ta{i}", [128, T]) for i in range(4)]
        ta2 = [s.take_tmp(f"tb{i}", [128, T]) for i in range(4)]
        tix = [s.take_tmp(f"tc{i}", [128, T]) for i in range(4)]
        for i in range(4):
            wt, rw = wnext()
            for fc in range(4):
                c = i * 4 + fc
                bank = fc
                if blk == 0:
                    fw.mm([lambda kc=kc, fc=fc, wt=wt, bank=bank: mm(
                        PS[bank][:, 0:4], wt[:, kc, fc * 128:(fc + 1) * 128], hT_halo[:, kc, :],
                        start=(kc == 0), stop=(kc == KC - 1)) for kc in range(KC)],
                        reads=[rw, rhhalo], writes=[rPS[bank]])
                    fw.op(ACT, lambda c=c, bank=bank: act_(halo[:, c, :], PS[bank][:, 1:4], AF.Copy, scale=flag_sb[:, 0:1]),
                          reads=[rPS[bank], rflag], writes=[rhalo])
                fw.mm([lambda kc=kc, fc=fc, wt=wt, bank=bank: mm(
                    PS[bank][:, :], wt[:, kc, fc * 128:(fc + 1) * 128], hT[:, kc, :],
                    start=(kc == 0), stop=(kc == KC - 1)) for kc in range(KC)],
                    reads=[rw, rh], writes=[rPS[bank]])
                fw.op(ACT, lambda fc=fc, c=c: act_(xbp[:, fc, 0:3], halo[:, c, :], AF.Copy),
                      reads=[rhalo], writes=[rxbp])
                fw.op(ACT, lambda fc=fc, bank=bank: act_(xbp[:, fc, 3:T + 3], PS[bank][:, :], AF.Copy),
                      reads=[rPS[bank]], writes=[rxbp])
                fw.op(ACT, lambda fc=fc, c=c: act_(halo[:, c, :], xbp[:, fc, T:T + 3], AF.Copy),
                      reads=[rxbp], writes=[rhalo])
                fw.op(ACT, lambda fc=fc, c=c: act_(
                    xc[:, fc, :], xbp[:, fc, 3:T + 3], AF.Identity, bias=bv_sb[:, 0, c:c + 1], scale=cw_sb[:, c, 3:4]),
                    reads=[rxbp, rbv, rcw], writes=[rxc])
                for k in range(3):
                    fw.op(DVE, lambda fc=fc, c=c, k=k: nc.vector.scalar_tensor_tensor(
                        xc[:, fc, :], xbp[:, fc, k:k + T], cw_sb[:, c, k:k + 1], xc[:, fc, :], ALU.mult, ALU.add),
                        reads=[rxbp, rcw, rxc], writes=[rxc])
                fw.op(ACT, lambda fc=fc: act_(xcb[:, fc, :], xc[:, fc, :], AF.Copy),
                      reads=[rxc], writes=[rxcb])
            wt, rw = wnext()
            for fc in range(4):
                bank = 4 + (fc % 2)
                fw.mm([lambda kc=kc, fc=fc, wt=wt, bank=bank: mm(
                    PS[bank][:, :], wt[:, kc, fc * 128:(fc + 1) * 128], hT[:, kc, :],
                    start=(kc == 0), stop=(kc == KC - 1)) for kc in range(KC)],
                    reads=[rw, rh], writes=[rPS[bank]])
                z2, rz2 = tm[0]
                fw.op(ACT, lambda bank=bank, z2=z2: act_(z2[:, :], PS[bank][:, :], AF.Square),
                      reads=[rPS[bank]], writes=[rz2])
                fw.op(DVE, lambda z2=z2: nc.vector.tensor_scalar(z2[:, :], z2[:, :], 0.044715, 1.0, ALU.mult, ALU.add),
                      reads=[rz2], writes=[rz2])
                fw.op(DVE, lambda bank=bank, z2=z2: nc.vector.tensor_tensor(z2[:, :], z2[:, :], PS[bank][:, :], ALU.mult),
                      reads=[rz2, rPS[bank]], writes=[rz2])
                fw.op(ACT, lambda z2=z2: act_(z2[:, :], z2[:, :], AF.Tanh, scale=0.7978845608028654),
                      reads=[rz2], writes=[rz2])
                fw.op(DVE, lambda fc=fc, bank=bank, z2=z2: nc.vector.scalar_tensor_tensor(
                    gg[:, fc, :], z2[:, :], 1.0, PS[bank][:, :], ALU.add, ALU.mult),
                    reads=[rz2, rPS[bank]], writes=[rgg])
            for fc in range(4):
                c = i * 4 + fc
                n = c // 2
                m = c % 2
                kl = (fc // 2) * 2
                fw.mm([lambda kk=kk, n=n, m=m, kl=kl: mm(PS[6][:, :], wra_bf[:, n, kk, m * 128:(m + 1) * 128], xcb[:, kl + kk, :],
                                                         start=(kk == 0), stop=(kk == 1)) for kk in range(2)],
                      reads=[rwra, rxcb], writes=[rPS[6]])
                fw.mm([lambda kk=kk, n=n, m=m, kl=kl: mm(PS[7][:, :], wri_bf[:, n, kk, m * 128:(m + 1) * 128], xcb[:, kl + kk, :],
                                                         start=(kk == 0), stop=(kk == 1)) for kk in range(2)],
                      reads=[rwri, rxcb], writes=[rPS[7]])
                (r_, rr_), (i_, ri_) = tm[1], tm[2]
                a_, ra_ = ta[fc]
                a2, ra2 = ta2[fc]
                ix, rix = tix[fc]
                fw.op(ACT, lambda c=c: act_(r_[:, :], PS[6][:, :], AF.Tanh, bias=hb_sb[:, 0, c:c + 1], scale=0.5),
                      reads=[rPS[6], rhb], writes=[rr_])
                fw.op(ACT, lambda c=c: act_(i_[:, :], PS[7][:, :], AF.Tanh, bias=hb_sb[:, 1, c:c + 1], scale=0.5),
                      reads=[rPS[7], rhb], writes=[ri_])
                fw.op(ACT, lambda c=c, a_=a_: act_(a_[:, :], r_[:, :], AF.Exp, bias=m4[:, c:c + 1], scale=m4[:, c:c + 1]),
                      reads=[rr_, rm4], writes=[ra_])
                fw.op(ACT, lambda c=c, a2=a2: act_(a2[:, :], r_[:, :], AF.Exp, bias=m8[:, c:c + 1], scale=m8[:, c:c + 1]),
                      reads=[rr_, rm8], writes=[ra2])
                fw.op(DVE, lambda fc=fc, ix=ix: nc.vector.scalar_tensor_tensor(ix[:, :], i_[:, :], 1.0, xc[:, fc, :], ALU.add, ALU.mult),
                      reads=[ri_, rxc], writes=[rix])
            for fc in range(4):
                a2, ra2 = ta2[fc]
                fw.op(ACT, lambda a2=a2: act_(a2[:, :], a2[:, :], AF.Sqrt, bias=one_sb[:, 0:1], scale=-1.0),
                      reads=[ra2, rone], writes=[ra2])
            for fc in range(4):
                c = i * 4 + fc
                a_, ra_ = ta[fc]
                a2, ra2 = ta2[fc]
                ix, rix = tix[fc]
                hs_, rhs_ = tm[3]
                fw.op(DVE, lambda a2=a2, ix=ix: nc.vector.scalar_tensor_tensor(ix[:, :], a2[:, :], 0.5, ix[:, :], ALU.mult, ALU.mult),
                      reads=[ra2, rix], writes=[rix])
                fw.op(DVE, lambda c=c, a_=a_, ix=ix: nc.vector.tensor_tensor_scan(hs_[:, :], a_[:, :], ix[:, :], hst[:, c:c + 1], ALU.mult, ALU.add),
                      reads=[ra_, rix, rhst], writes=[rhs_])
                fw.op(ACT, lambda c=c: act_(hst[:, c:c + 1], hs_[:, T - 1:T], AF.Copy),
                      reads=[rhs_], writes=[rhst])
                fw.op(DVE, lambda c=c, a_=a_: nc.vector.tensor_tensor_scan(acum[:, :], a_[:, :], zeros_t[:, :], Ast[:, c:c + 1], ALU.mult, ALU.add),
                      reads=[ra_, rzeros, rAst], writes=[racum])
                fw.op(ACT, lambda c=c: act_(Ast[:, c:c + 1], acum[:, T - 1:T], AF.Copy), reads=[racum], writes=[rAst])
                k = c % 2
                fw.op(DVE, lambda k=k, fc=fc: nc.vector.scalar_tensor_tensor(ysg[0][:, k, :], hs_[:, :], 0.5, gg[:, fc, :], ALU.mult, ALU.mult),
                      reads=[rhs_, rgg], writes=[ysg[1][k]])
                fw.op(DVE, lambda k=k, fc=fc: nc.vector.scalar_tensor_tensor(asg[0][:, k, :], acum[:, :], 0.5, gg[:, fc, :], ALU.mult, ALU.mult),
                      reads=[racum, rgg], writes=[asg[1][k]])
                fw.dma(SP, y0T[c * 128:(c + 1) * 128, blk * T:(blk + 1) * T], ysg[0][:, k, :], f"ys{k}", reads=[ysg[1][k]], writes=[ry0[blk]])
                fw.dma(SP, AgT[c * 128:(c + 1) * 128, blk * T:(blk + 1) * T], asg[0][:, k, :], f"as{k}", reads=[asg[1][k]], writes=[rAg[blk]])

    def rglru_back(blk):
        cmb = 2
        s.reset()
        yT, ry = s.take_tmp("yT", [128, KC, T], BF16)
        ly = s.take_tmp("ly", [128, 2, 4, T], nres=2)
        la = s.take_tmp("la", [128, 2, 4, T], nres=2)
        for q in range(4):
            k = q % 2
            fw.dma(SP, ly[0][:, k, :, :], y0T[q * 512:(q + 1) * 512, blk * T:(blk + 1) * T].rearrange("(c p) t -> p c t", p=128),
                   f"ly{k}", reads=[ry0[blk]], writes=[ly[1][k]])
            fw.dma(SP, la[0][:, k, :, :], AgT[q * 512:(q + 1) * 512, blk * T:(blk + 1) * T].rearrange("(c p) t -> p c t", p=128),
                   f"la{k}", reads=[rAg[blk]], writes=[la[1][k]])
            for cc in range(4):
                c = q * 4 + cc
                fw.op(DVE, lambda k=k, cc=cc, c=c: nc.vector.scalar_tensor_tensor(
                    yT[:, c, :], la[0][:, k, cc, :], h_init[:, c:c + 1], ly[0][:, k, cc, :], ALU.mult, ALU.add),
                    reads=[la[1][k], ly[1][k], rhinit], writes=[ry])
        for i in range(4):
            wt, rw = wnext()
            for fc in range(4):
                bank = fc
                f = i * 4 + fc
                fw.mm([lambda kc=kc, fc=fc, wt=wt, bank=bank: mm(
                    PS[bank][:, :], wt[:, kc, fc * 128:(fc + 1) * 128], yT[:, kc, :],
                    start=(kc == 0), stop=(kc == KC - 1)) for kc in range(KC)],
                    reads=[rw, ry], writes=[rPS[bank]])
                fw.op(DVE, lambda f=f, bank=bank: nc.vector.scalar_tensor_tensor(
                    x_sb[:, f, :], PS[bank][:, :], modv[:, cmb, 32 + f:33 + f], x_sb[:, f, :], ALU.mult, ALU.add),
                    reads=[rPS[bank], rmod, rx], writes=[rx])

    out_evs = []

    def final_norm_store(blk):
        s.reset()
        sq = s.take_tmp("sq", [128, 2, T], BF16, nres=2)
        for kc in range(KC):
            fw.op(ACT, lambda kc=kc: act_(sq[0][:, kc % 2, :], x_sb[:, kc, :], AF.Square),
                  reads=[rx], writes=[sq[1][kc % 2]])
            fw.mm([lambda kc=kc: mm(PS[7][:, :], ones_bf[:, :], sq[0][:, kc % 2, :], start=(kc == 0), stop=(kc == KC - 1))],
                  reads=[sq[1][kc % 2], rones], writes=[rPS[7]])
        rstd = s.take_tmp("rstd", [128, T])
        fw.op(ACT, lambda: act_(rstd[0][:, :], PS[7][:, :], AF.Sqrt, bias=eps_sb[:, 0:1], scale=1.0 / D),
              reads=[rPS[7], reps], writes=[rstd[1]])
        fw.op(DVE, lambda: nc.vector.reciprocal(rstd[0][:, :], rstd[0][:, :]),
              reads=[rstd[1]], writes=[rstd[1]])
        ob, rob = s.take_tmp("ob", [128, KC, T])
        for kc in range(KC):
            fw.op(DVE, lambda kc=kc: nc.vector.scalar_tensor_tensor(
                ob[:, kc, :], x_sb[:, kc, :], fg_sb[:, kc:kc + 1], rstd[0][:, :], ALU.mult, ALU.mult),
                reads=[rx, rfg, rstd[1]], writes=[rob])
        ev = fw.dma(SP, outT[:, blk * T:(blk + 1) * T].rearrange("(kc p) t -> p kc t", p=128), ob[:, :, :], "out", reads=[rob])
        out_evs.append(ev)

    def dbg_store():
        ev = fw.dma(SP, outT[:, 0:T].rearrange("(kc p) t -> p kc t", p=128), x_sb[:, :, :], "out", reads=[rx])
        out_evs.append(ev)

    groups = [[2 * i, 2 * i + 1] for i in range(ncores // 2)]
    cc_n = [0]

    def allgather(src_t, rsrc, dst_t, rdst):
        if stub_cc:
            fw.dma(POOL, dst_t[0:128, :], src_t[:, :], "ccstub", reads=[rsrc], writes=[rdst])
            return
        sem = nc.alloc_semaphore(name=f"cc{cc_n[0]}")
        cc_n[0] += 1
        fw._wait(POOL, fw._deps([rsrc], [rdst]))
        nc.gpsimd.collective_compute("AllGather", ALU.bypass, replica_groups=groups,
                                     ins=[src_t.ap()], outs=[dst_t.ap()]).then_inc(sem, 1)
        fw._record((sem, 1, None), [rsrc], [rdst])

    def xload(src, blk):
        rd = [rx2[blk]] if src is x2T else []
        fw.dma(SP, x_sb[:, :, :], src[:, blk * T:(blk + 1) * T].rearrange("(kc p) t -> p kc t", p=128), "xin",
               reads=rd, writes=[rx])

    for pj in range(nblk):
        xload(xpre, pj)
        mlstm_layer(pj * 4, prepass=True)
    for h in range(8):
        fw.op(DVE, lambda h=h: nc.vector.tensor_scalar(Cf[h][:, :], Cf[h][:, :], flag_sb[:, 0:1], None, ALU.mult),
              reads=[rCf[h], rflag], writes=[rCf[h]])
        fw.op(DVE, lambda h=h: nc.vector.tensor_scalar(Cb[h][:, :], Cb[h][:, :], flag_sb[:, 0:1], None, ALU.mult),
              reads=[rCb[h], rflag], writes=[rCb[h]])
    for blk in range(nblk):
        xload(xT, blk)
        mlstm_layer(16 + blk * 4)
        mlp(0, 1)
        fw.dma(SP, x2T[:, blk * T:(blk + 1) * T].rearrange("(kc p) t -> p kc t", p=128), x_sb[:, :, :], "x2s",
               reads=[rx], writes=[rx2[blk]])
        if blk == nblk - 1:
            fw.dma(SP, halo_src[:, :].rearrange("p (kc t) -> p kc t", kc=KC), x_sb[:, :, T - 4:T], "halo",
                   reads=[rx], writes=[rhalo_src])
    allgather(halo_src, rhalo_src, halo_all, rhalo_all)
    fw.dma(SP, x_halo[:, :, :], halo_all[0:128, :].rearrange("p (kc t) -> p kc t", kc=KC), "haloin",
           reads=[rhalo_all], writes=[rxhalo])
    for blk in range(nblk):
        xload(x2T, blk)
        rglru_front(blk)
    fw.dma(SP, hend_src[:, :], hst[:, :], "hend", reads=[rhst], writes=[rhend_src])
    allgather(hend_src, rhend_src, hend_all, rhend_all)
    fw.dma(SP, h_init[:, :], hend_all[0:128, :], "hendin", reads=[rhend_all], writes=[rhinit])
    fw.op(DVE, lambda: nc.vector.tensor_scalar(h_init[:, :], h_init[:, :], flag_sb[:, 0:1], None, ALU.mult),
          reads=[rhinit, rflag], writes=[rhinit])
    for blk in range(nblk):
        xload(x2T, blk)
        rglru_back(blk)
        mlp(1, 3)
        final_norm_store(blk)

    if dbg is None:
        assert wstate["next_use"] == len(plan), (wstate, len(plan))
    fw._wait(SP, out_evs)
    return nc


def _prep_inputs(inp, b):
    f32 = np.float32
    x = inp["x"]

    def fm(v):
        return np.ascontiguousarray(np.asarray(v, f32).reshape(KC, 128).T)

    m = {}
    m["cT"] = fm(inp["c"][b])
    m["ada_w"] = np.ascontiguousarray(inp["ada_w"].reshape(4, D, 3 * D))
    m["ada_b"] = np.ascontiguousarray(inp["ada_b"].reshape(4, 48, 128).transpose(2, 0, 1).reshape(128, 4 * 48))
    m["a_w_in"] = np.ascontiguousarray(inp["a_w_in"][0])
    bg = inp["a_b_gate"][0].reshape(16)
    m["a_bg"] = np.ascontiguousarray(np.broadcast_to(np.tile(bg, 4)[None, :], (128, 64))).astype(f32)
    m["a_ng"] = fm(inp["a_norm_g"][0])
    m["a_w_out"] = np.ascontiguousarray(inp["a_w_out"][0])
    m["b_w_in"] = np.ascontiguousarray(inp["b_w_in"][0])
    cw = inp["b_conv_w"][0]
    m["b_cw"] = np.ascontiguousarray(cw.T.reshape(KC, 128, 4).transpose(1, 0, 2).reshape(128, KC * 4))
    m["b_vec"] = np.ascontiguousarray(np.concatenate(
        [fm(inp["b_conv_b"][0]), fm(inp["b_b_ra"][0]), fm(inp["b_b_ri"][0]), fm(inp["b_lam"][0])], axis=1))
    m["b_w_ra"] = np.ascontiguousarray(inp["b_w_ra"][0])
    m["b_w_ri"] = np.ascontiguousarray(inp["b_w_ri"][0])
    m["b_w_out"] = np.ascontiguousarray(inp["b_w_out"][0])
    m["mlp_w1"] = np.ascontiguousarray(inp["mlp_w1"])
    m["mlp_w2"] = np.ascontiguousarray(inp["mlp_w2"])
    m["fin_g"] = fm(inp["final_g"])
    m["triu"] = np.triu(np.ones((128, 128), f32))
    return m


_NC_CACHE = {}


def kernel(**inputs):
    inp = {k: np.asarray(v, dtype=np.float32) for k, v in inputs.items()}
    if "full" not in _NC_CACHE:
        _NC_CACHE["full"] = build()
    nc = _NC_CACHE["full"]
    in_maps = []
    for core in range(NCORES):
        b, half = core // 2, core % 2
        m = _prep_inputs(inp, b)
        xb = inp["x"][b]
        m["xT"] = np.ascontiguousarray(xb[half * TOK:(half + 1) * TOK].T)
        m["xpre"] = np.ascontiguousarray(xb[0:TOK].T)
        m["flag"] = np.full((128, 1), float(half), np.float32)
        in_maps.append(m)
    res = run_bass_kernel_spmd(nc, in_maps, core_ids=list(range(NCORES)))
    out = np.empty((NB, SEQ, D), np.float32)
    for core in range(NCORES):
        b, half = core // 2, core % 2
        out[b, half * TOK:(half + 1) * TOK, :] = res.results[core]["outT"].T
    return out
```

```python
import numpy as np
import concourse.bass as bass
import concourse.mybir as mybir
from concourse.bass_utils import run_bass_kernel_spmd

F32 = mybir.dt.float32
BF16 = mybir.dt.bfloat16
ALU = mybir.AluOpType
AF = mybir.ActivationFunctionType

D = 2048
SEQ = 4096
NB = 4
KC = 16
T = 512
DFF = 8192
IN_A = 6160
EPS = 1e-6
NCORES = 8
TOK = 2048
NBLK = TOK // 512
WDEPTH = 2


class Res:
    __slots__ = ("name", "w", "r")

    def __init__(self, name):
        self.name = name
        self.w = None
        self.r = {}

    def add_r(self, ev):
        k = id(ev[0])
        if k not in self.r or self.r[k][1] < ev[1]:
            self.r[k] = ev


class Eng:
    def __init__(self, nc, name, h):
        self.name = name
        self.h = h
        self.sem = nc.alloc_semaphore(name="s_" + name)
        self.count = 0
        self.known = {}


class FW:
    def __init__(self, nc):
        self.nc = nc
        self.pe = Eng(nc, "pe", nc.tensor)
        self.act = Eng(nc, "act", nc.scalar)
        self.dve = Eng(nc, "dve", nc.vector)
        self.pool = Eng(nc, "pool", nc.gpsimd)
        self.sp = Eng(nc, "sp", nc.sync)
        self.dsems = {}

    def _wait(self, eng, evs):
        best = {}
        for ev in evs:
            if ev is None:
                continue
            sem, val, owner = ev
            if owner is eng:
                continue
            k = id(sem)
            if k not in best or best[k][1] < val:
                best[k] = ev
        for k, (sem, val, owner) in best.items():
            if eng.known.get(k, 0) >= val:
                continue
            eng.h.wait_ge(sem, val)
            eng.known[k] = val

    @staticmethod
    def _deps(reads, writes):
        evs = []
        for r in reads:
            evs.append(r.w)
        for w in writes:
            evs.append(w.w)
            evs.extend(w.r.values())
        return evs

    @staticmethod
    def _record(ev, reads, writes):
        for r in reads:
            r.add_r(ev)
        for w in writes:
            w.w = ev
            w.r = {}

    def op(self, eng, fn, reads=(), writes=()):
        self._wait(eng, self._deps(reads, writes))
        ins = fn()
        eng.count += 1
        ins.then_inc(eng.sem, 1)
        self._record((eng.sem, eng.count, eng), reads, writes)

    def mm(self, fns, reads=(), writes=()):
        eng = self.pe
        self._wait(eng, self._deps(reads, writes))
        ins = None
        for f in fns:
            ins = f()
        eng.count += 1
        ins.then_inc(eng.sem, 1)
        self._record((eng.sem, eng.count, eng), reads, writes)

    def dma(self, q, out, in_, key, reads=(), writes=()):
        if key not in self.dsems:
            self.dsems[key] = [self.nc.alloc_semaphore(name="d_" + key), 0]
        ds = self.dsems[key]
        self._wait(q, self._deps(reads, writes))
        q.h.dma_start(out=out, in_=in_).then_inc(ds[0], 16)
        ds[1] += 16
        ev = (ds[0], ds[1], None)
        self._record(ev, reads, writes)
        return ev


def _prune(res_list):
    pass


def build(ncores=NCORES, dbg=None, stub_cc=False):
    nblk = NBLK
    nc = bass.Bass("TRN2", target_bir_lowering=False)
    fw = FW(nc)
    PE, ACT, DVE, POOL, SP = fw.pe, fw.act, fw.dve, fw.pool, fw.sp
    mm = nc.tensor.matmul
    _zb = []

    def act_(out, in_, func, bias=None, scale=1.0):
        if func == AF.Copy:
            assert bias is None
            return nc.scalar.activation(out, in_, func, scale=scale)
        if bias is None:
            bias = _zb[0][:, 0:1]
        elif isinstance(bias, float):
            raise ValueError("float bias")
        return nc.scalar.activation(out, in_, func, bias=bias, scale=scale)

    def din(name, shape):
        return nc.dram_tensor(name, shape, F32, kind="ExternalInput").ap()

    xT = din("xT", [D, TOK])
    xpre = din("xpre", [D, TOK])
    flag_d = din("flag", [128, 1])
    cT = din("cT", [128, KC * NB])
    bsel_d = din("bsel", [128, NB])
    ada_sl = din("ada_sl", [D, 3072])
    ada_b = din("ada_b", [128, 4 * 48])
    a_w_in = din("a_w_in", [D, IN_A])
    a_bg = din("a_bg", [128, 64])
    a_ng = din("a_ng", [128, KC])
    a_w_out = din("a_w_out", [D, D])
    b_w_in = din("b_w_in", [D, 2 * D])
    b_cw = din("b_cw", [128, KC * 4])
    b_vec = din("b_vec", [128, 4 * KC])
    b_w_ra = din("b_w_ra", [8, 256, 256])
    b_w_ri = din("b_w_ri", [8, 256, 256])
    b_w_out = din("b_w_out", [D, D])
    mlp_w1 = din("mlp_w1", [2, D, DFF])
    mlp_w2 = din("mlp_w2", [2, DFF, D])
    fin_g = din("fin_g", [128, KC])
    triu = din("triu", [128, 128])
    outT = nc.dram_tensor("outT", [D, TOK], F32, kind="ExternalOutput").ap()
    x2T = nc.dram_tensor("x2T", [D, TOK], F32).ap()
    y0T = nc.dram_tensor("y0T", [D, TOK], F32).ap()
    AgT = nc.dram_tensor("AgT", [D, TOK], F32).ap()
    halo_src = nc.dram_tensor("halo_src", [128, 64], F32)
    halo_all = nc.dram_tensor("halo_all", [256, 64], F32)
    hend_src = nc.dram_tensor("hend_src", [128, KC], F32)
    hend_all = nc.dram_tensor("hend_all", [256, KC], F32)
    mod_src = nc.dram_tensor("mod_src", [128, 96], F32)
    mod_q = nc.dram_tensor("mod_q", [512, 96], F32)
    mod_all = nc.dram_tensor("mod_all", [1024, 96], F32)
    rmod_src, rmod_q, rmod_all = Res("ms"), Res("mq"), Res("ma")
    rx2 = [Res(f"x2_{j}") for j in range(4)]
    ry0 = [Res(f"y0_{j}") for j in range(4)]
    rAg = [Res(f"Ag_{j}") for j in range(4)]
    rhalo_src, rhalo_all, rhend_src, rhend_all = Res("hs"), Res("ha"), Res("es"), Res("ea")

    def sb(name, shape, dt=F32):
        return nc.alloc_sbuf_tensor(name, shape, dt), Res(name)

    U32, rU32 = sb("U32", [128, 128])
    U4, rU4 = sb("U4", [128, 4, 128])
    ones_bf, rones = sb("ones_bf", [128, 128], BF16)
    c_sb, rc = sb("c_sb", [128, KC, NB])
    sc_bf, rsc = sb("sc_bf", [128, KC, NB], BF16)
    bsel, rbsel = sb("bsel_sb", [128, NB])
    modv, rmod = sb("modv", [128, 4, 48])
    adab, radab = sb("adab", [128, 4, 48])
    bg_sb, rbg = sb("bg_sb", [128, 4, 16])
    ng_sb, rng_ = sb("ng_sb", [128, KC])
    cw_sb, rcw = sb("cw_sb", [128, KC, 4])
    bv_sb, rbv = sb("bv_sb", [128, 4, KC])
    m8, rm8 = sb("m8", [128, KC])
    m16, rm16 = sb("m16", [128, KC])
    fg_sb, rfg = sb("fg_sb", [128, KC])
    eps_sb, reps = sb("eps_sb", [128, 1])
    flag_sb, rflag = sb("flag_sb", [128, 1])
    ones32, rones32 = sb("ones32", [128, 128])
    zeros_t, rzeros = sb("zeros_t", [128, T])
    Ast, rAst = sb("Ast", [128, KC])
    h_init, rhinit = sb("h_init", [128, KC])
    x_halo, rxhalo = sb("x_halo", [128, KC, 4])
    hT_halo, rhhalo = sb("hT_halo", [128, KC, 4], BF16)
    zero_sb, rzero = sb("zero_sb", [128, 1])
    _zb.append(zero_sb)
    one_sb, rone = sb("one_sb", [128, 1])
    hb_sb, rhb = sb("hb_sb", [128, 2, KC])
    m4, rm4 = sb("m4", [128, KC])
    wg_bf, rwg = sb("wg_bf", [128, KC, 16], BF16)
    wra_bf, rwra = sb("wra_bf", [128, 8, 2, 256], BF16)
    wri_bf, rwri = sb("wri_bf", [128, 8, 2, 256], BF16)

    x_sb, rx = sb("x_sb", [128, KC, T])
    hT, rh = sb("hT", [128, KC, T], BF16)
    Cf, rCf = [], []
    for h in range(8):
        a, r = sb(f"Cf{h}", [128, 384])
        Cf.append(a)
        rCf.append(r)
    Cb, rCb = [], []
    for h in range(8):
        a, r = sb(f"Cb{h}", [128, 384], BF16)
        Cb.append(a)
        rCb.append(r)
    eL, reL = sb("eL", [128, 2, 8])
    hst, rhst = sb("hst", [128, KC])
    halo, rhalo = sb("halo", [128, KC, 3])

    wslot = []
    for i in range(WDEPTH):
        wslot.append(sb(f"wslot{i}", [128, KC, 512], BF16))

    SCR_BYTES = 80 * 1024
    scr_used = [0]
    scr = nc.alloc_sbuf_tensor("scr", [128, SCR_BYTES // 4], F32)
    rscr_all = Res("scr_all")

    PS = [nc.alloc_psum_tensor(f"ps{i}", [128, 512], F32) for i in range(8)]
    rPS = [Res(f"ps{i}") for i in range(8)]

    plan = []

    def wtile(src2d, r0, c0, ncols):
        return src2d[r0:r0 + D, c0:c0 + ncols].rearrange("(kc p) n -> p kc n", p=128), ncols

    plan = []
    for i in range(6):
        plan.append(wtile(ada_sl, 0, i * 512, 512))
    for _ in range(nblk):
        for i in range(2, 8):
            plan.append(wtile(a_w_in, 0, i * 512, 512))
    for _ in range(nblk):
        for i in range(12):
            plan.append(wtile(a_w_in, 0, i * 512, 512))
        for i in range(4):
            plan.append(wtile(a_w_out, 0, i * 512, 512))
        for i in range(16):
            plan.append(wtile(mlp_w1[0], 0, i * 512, 512))
        for cg in range(4):
            for kg in range(4):
                plan.append(wtile(mlp_w2[0], kg * D, cg * 512, 512))
    for _ in range(nblk):
        for i in range(4):
            plan.append(wtile(b_w_in, 0, i * 512, 512))
            plan.append(wtile(b_w_in, 0, D + i * 512, 512))
    for _ in range(nblk):
        for i in range(4):
            plan.append(wtile(b_w_out, 0, i * 512, 512))
        for i in range(16):
            plan.append(wtile(mlp_w1[1], 0, i * 512, 512))
        for cg in range(4):
            for kg in range(4):
                plan.append(wtile(mlp_w2[1], kg * D, cg * 512, 512))

    wstate = {"next_load": 0, "next_use": 0}

    def _issue_load():
        i = wstate["next_load"]
        if i >= len(plan):
            return
        src, ncols = plan[i]
        t, r = wslot[i % WDEPTH]
        fw.dma(POOL, t[:, :, 0:ncols], src, f"w{i % WDEPTH}", writes=[r])
        wstate["next_load"] = i + 1

    def wnext():
        i = wstate["next_use"]
        if i == 0:
            for _ in range(WDEPTH):
                _issue_load()
        else:
            _issue_load()
        wstate["next_use"] = i + 1
        return wslot[i % WDEPTH]

    def cload(t, r, src):
        fw.dma(SP, t, src, "const", writes=[r])
        SP.h.wait_ge(fw.dsems["const"][0], fw.dsems["const"][1])
        SP.known[id(fw.dsems["const"][0])] = fw.dsems["const"][1]

    cload(U32[:, :], rU32, triu)
    cload(c_sb[:, :, :], rc, cT.rearrange("p (a b) -> p a b", a=KC))
    cload(bsel[:, :], rbsel, bsel_d)
    cload(adab[:, :, :], radab, ada_b.rearrange("p (a b) -> p a b", a=4))
    cload(bg_sb[:, :, :], rbg, a_bg.rearrange("p (a b) -> p a b", a=4))
    cload(ng_sb[:, :], rng_, a_ng)
    cload(cw_sb[:, :, :], rcw, b_cw.rearrange("p (a b) -> p a b", a=KC))
    cload(bv_sb[:, :, :], rbv, b_vec.rearrange("p (a b) -> p a b", a=4))
    cload(fg_sb[:, :], rfg, fin_g)
    cload(flag_sb[:, :], rflag, flag_d)
    fw.dma(POOL, wg_bf[:, :, :], a_w_in[:, 6144:6160].rearrange("(kc p) n -> p kc n", p=128), "cw0", writes=[rwg])
    fw.dma(POOL, wra_bf[:, :, :, :], b_w_ra.rearrange("n (kc p) d -> p n kc d", p=128), "cw1", writes=[rwra])
    fw.dma(POOL, wri_bf[:, :, :, :], b_w_ri.rearrange("n (kc p) d -> p n kc d", p=128), "cw2", writes=[rwri])

    for j in range(4):
        fw.op(DVE, lambda j=j: nc.vector.tensor_copy(U4[:, j, :], U32[:, :]), reads=[rU32], writes=[rU4])
    fw.op(DVE, lambda: nc.vector.memset(ones_bf[:, :], 1.0), writes=[rones])
    fw.op(DVE, lambda: nc.vector.memset(eps_sb[:, :], EPS), writes=[reps])
    fw.op(DVE, lambda: nc.vector.memset(zero_sb[:, :], 0.0), writes=[rzero])
    fw.op(DVE, lambda: nc.vector.memset(ones32[:, :], 1.0), writes=[rones32])
    fw.op(DVE, lambda: nc.vector.memset(zeros_t[:, :], 0.0), writes=[rzeros])
    fw.op(DVE, lambda: nc.vector.memset(Ast[:, :], 1.0), writes=[rAst])
    fw.op(DVE, lambda: nc.vector.memset(one_sb[:, :], 1.0), writes=[rone])
    fw.op(DVE, lambda: nc.vector.tensor_scalar(hb_sb[:, :, :], bv_sb[:, 1:3, :], 0.5, None, ALU.mult), reads=[rbv], writes=[rhb])
    for h in range(8):
        fw.op(DVE, lambda h=h: nc.vector.memset(Cf[h][:, :], 0.0), writes=[rCf[h]])
        fw.op(DVE, lambda h=h: nc.vector.memset(Cb[h][:, :], 0.0), writes=[rCb[h]])
    fw.op(DVE, lambda: nc.vector.memset(eL[:, :, :], 1.0), writes=[reL])
    fw.op(DVE, lambda: nc.vector.memset(hst[:, :], 0.0), writes=[rhst])
    fw.op(DVE, lambda: nc.vector.memset(halo[:, :, :], 0.0), writes=[rhalo])
    fw.op(ACT, lambda: act_(sc_bf[:, :, :], c_sb[:, :, :], AF.Silu), reads=[rc, rzero, rone, reps], writes=[rsc])
    fw.op(ACT, lambda: act_(m8[:, :], bv_sb[:, 3, :], AF.Exp, scale=-1.0), reads=[rbv], writes=[rm8])
    fw.op(ACT, lambda: act_(m16[:, :], m8[:, :], AF.Ln, bias=one_sb[:, 0:1]), reads=[rm8], writes=[rm16])
    fw.op(DVE, lambda: nc.vector.tensor_scalar(m8[:, :], m16[:, :], -8.0, None, ALU.mult), reads=[rm16], writes=[rm8])
    fw.op(DVE, lambda: nc.vector.tensor_scalar(m4[:, :], m16[:, :], -4.0, None, ALU.mult), reads=[rm16], writes=[rm4])
    fw.op(DVE, lambda: nc.vector.tensor_scalar(m16[:, :], m16[:, :], -16.0, None, ALU.mult), reads=[rm16], writes=[rm16])

    if dbg == "mdbg":
        dbt, rdbt = sb("dbt", [128, 96])
        fw.op(ACT, lambda: act_(dbt[:, 0:16], bv_sb[:, 3, :], AF.Exp, scale=-1.0), reads=[rbv], writes=[rdbt])
        fw.op(ACT, lambda: act_(dbt[:, 16:32], dbt[:, 0:16], AF.Ln, bias=one_sb[:, 0:1]), reads=[rdbt], writes=[rdbt])
        fw.op(ACT, lambda: act_(dbt[:, 32:48], dbt[:, 0:16], AF.Ln, bias=one_sb[:, 0:1]), reads=[rdbt, rone], writes=[rdbt])
        fw.op(DVE, lambda: nc.vector.tensor_scalar(dbt[:, 48:64], dbt[:, 0:16], 1.0, None, ALU.add), reads=[rdbt], writes=[rdbt])
        fw.op(ACT, lambda: act_(dbt[:, 64:80], dbt[:, 48:64], AF.Ln), reads=[rdbt], writes=[rdbt])
        fw.op(DVE, lambda: nc.vector.tensor_copy(dbt[:, 80:96], m16[:, :]), reads=[rm16], writes=[rdbt])
        ev = fw.dma(SP, outT[0:128, 0:96], dbt[:, :], "out", reads=[rdbt])
        fw._wait(SP, [ev])
        return nc
    class Scr:
        def __init__(self):
            self.off = 0

        def take(self, name, shape, dt=F32):
            n = int(np.prod(shape[1:]))
            nbytes = n * (4 if dt == F32 else 2)
            nbytes = (nbytes + 31) // 32 * 32
            w0 = self.off // 4
            self.off += nbytes
            assert self.off <= SCR_BYTES, (name, self.off)
            v = scr[:, w0:w0 + nbytes // 4]
            if dt != F32:
                v = v.bitcast(dt)
            v = v[:, 0:n]
            if len(shape) == 3:
                v = v.rearrange("p (a b) -> p a b", a=shape[1])
            elif len(shape) == 4:
                v = v.rearrange("p (a b c) -> p a b c", a=shape[1], b=shape[2])
            return v

    phase_res = []

    def new_phase():
        prev = list(phase_res)
        phase_res.clear()
        return prev

    def adaln(cmb, x_sb=x_sb, rx=rx, hT=hT, rh=rh, n=T):
        sq = s.take_tmp("sq", [128, 2, T], BF16, nres=2)
        for kc in range(KC):
            fw.op(ACT, lambda kc=kc: act_(sq[0][:, kc % 2, 0:n], x_sb[:, kc, 0:n], AF.Square),
                  reads=[rx], writes=[sq[1][kc % 2]])
            fw.mm([lambda kc=kc: mm(PS[7][:, 0:n], ones_bf[:, :], sq[0][:, kc % 2, 0:n], start=(kc == 0), stop=(kc == KC - 1))],
                  reads=[sq[1][kc % 2], rones], writes=[rPS[7]])
        rstd = s.take_tmp("rstd", [128, T])
        fw.op(ACT, lambda: act_(rstd[0][:, 0:n], PS[7][:, 0:n], AF.Sqrt, bias=eps_sb[:, 0:1], scale=1.0 / D),
              reads=[rPS[7], reps], writes=[rstd[1]])
        fw.op(DVE, lambda: nc.vector.reciprocal(rstd[0][:, 0:n], rstd[0][:, 0:n]),
              reads=[rstd[1]], writes=[rstd[1]])
        tmp = s.take_tmp("adatmp", [128, 2, T], nres=2)
        for kc in range(KC):
            fw.op(DVE, lambda kc=kc: nc.vector.scalar_tensor_tensor(
                tmp[0][:, kc % 2, 0:n], x_sb[:, kc, 0:n], modv[:, cmb, 16 + kc:17 + kc], rstd[0][:, 0:n], ALU.mult, ALU.mult),
                reads=[rx, rmod, rstd[1]], writes=[tmp[1][kc % 2]])
            fw.op(ACT, lambda kc=kc: act_(hT[:, kc, 0:n], tmp[0][:, kc % 2, 0:n], AF.Identity,
                                                          bias=modv[:, cmb, kc:kc + 1]),
                  reads=[tmp[1][kc % 2], rmod], writes=[rh])

    class S:
        def __init__(self):
            self.scr = Scr()
            self.bufs = {}
            self.haz = {}

        def reset(self):
            for (_, res) in self.bufs.values():
                for o in (res if isinstance(res, list) else [res]):
                    evs = list(o.r.values())
                    if o.w is not None:
                        evs.append(o.w)
                    for ev in evs:
                        k = id(ev[0])
                        if k not in self.haz or self.haz[k][1] < ev[1]:
                            self.haz[k] = ev
            self.scr = Scr()
            self.bufs = {}

        def take_tmp(self, name, shape, dt=F32, nres=0):
            if name in self.bufs:
                return self.bufs[name]
            ap = self.scr.take(name, shape, dt)
            res = [Res(name + str(i)) for i in range(nres)] if nres else Res(name)
            for rr in (res if isinstance(res, list) else [res]):
                rr.r = dict(self.haz)
            self.bufs[name] = (ap, res)
            return self.bufs[name]

    s = S()

    def proj_fm(ntiles, evac):
        for i in range(ntiles):
            wt, rw = wnext()
            for fc in range(4):
                bank = (i * 4 + fc) % 4
                fw.mm([lambda kc=kc, fc=fc, wt=wt, bank=bank: mm(
                    PS[bank][:, :], wt[:, kc, fc * 128:(fc + 1) * 128], hT[:, kc, :],
                    start=(kc == 0), stop=(kc == KC - 1)) for kc in range(KC)],
                    reads=[rw, rh], writes=[rPS[bank]])
                evac(i, fc, bank)

    def resid_evac(gate_col0):
        def ev(i, fc, bank):
            f = i * 4 + fc
            fw.op(DVE, lambda: nc.vector.scalar_tensor_tensor(
                x_sb[:, f, :], PS[bank][:, :], modv[:, gate_col0[0], gate_col0[1] + f:gate_col0[1] + f + 1],
                x_sb[:, f, :], ALU.mult, ALU.add), reads=[rPS[bank], rmod, rx], writes=[rx])
        return ev

    def mlp(lyr, cmb):
        s.reset()
        adaln(cmb)
        s.reset()
        aT, raT = s.take_tmp("aT", [128, 64, T], BF16)
        rl = s.take_tmp("relu", [128, 2, T], nres=2)
        rrl = rl[1]
        cnt = [0]

        def ev1(i, fc, bank):
            f = i * 4 + fc
            k = cnt[0] % 2
            cnt[0] += 1
            fw.op(ACT, lambda: act_(rl[0][:, k, :], PS[bank][:, :], AF.Relu),
                  reads=[rPS[bank]], writes=[rrl[k]])
            fw.op(DVE, lambda: nc.vector.tensor_tensor(aT[:, f, :], rl[0][:, k, :], rl[0][:, k, :], ALU.mult),
                  reads=[rrl[k]], writes=[raT])
        proj_fm(16, ev1)
        for cg in range(4):
            for kg in range(4):
                wt, rw = wnext()
                for fc in range(4):
                    fw.mm([lambda kc=kc, fc=fc, wt=wt, kg=kg: mm(
                        PS[fc][:, :], wt[:, kc, fc * 128:(fc + 1) * 128], aT[:, kg * 16 + kc, :],
                        start=(kg == 0 and kc == 0), stop=(kg == 3 and kc == KC - 1)) for kc in range(KC)],
                        reads=[rw, raT], writes=[rPS[fc]])
            for fc in range(4):
                f = cg * 4 + fc
                fw.op(DVE, lambda f=f, fc=fc: nc.vector.scalar_tensor_tensor(
                    x_sb[:, f, :], PS[fc][:, :], modv[:, cmb, 32 + f:33 + f], x_sb[:, f, :], ALU.mult, ALU.add),
                    reads=[rPS[fc], rmod, rx], writes=[rx])

    def mlstm_layer(chunk_base, prepass=False):
        cmb = 0
        s.reset()
        adaln(cmb)
        s.reset()
        qT, rq = s.take_tmp("qT", [128, 8, T], BF16)
        kT, rk = s.take_tmp("kT", [128, 8, T], BF16)
        ktok, rkt = s.take_tmp("ktok", [128, 4, 1024], BF16)
        vaug, rv = s.take_tmp("vaug", [128, 4, 8, 256], BF16)
        yT, ry = s.take_tmp("yT", [128, KC, T], BF16)
        zs, rzs = s.take_tmp("zs", [128, 4, 16])
        nlf, rnlf = s.take_tmp("nlf", [128, 4, 8])
        ecol, recol = s.take_tmp("ecol", [128, 4, 8])
        nlfrep, rnr = s.take_tmp("nlfrep", [128, 8, 128])
        rb, rrb = s.take_tmp("rb", [128, 4, 128])
        vs, rvs = s.take_tmp("vs", [128, 4, 384], BF16)
        MT, rMT = s.take_tmp("MT", [128, 4, 128], BF16)
        den, rden = s.take_tmp("den", [128, 4, 128])
        hh, rhh = s.take_tmp("hh", [128, 2, 512])
        sqh, rsqh = s.take_tmp("sqh", [128, 2, 512], BF16)
        rsd, rrsd = s.take_tmp("rsd", [128, 512])

        for i in (range(2, 8) if prepass else range(8)):
            wt, rw = wnext()
            if i < 4 and not prepass:
                for fc in range(4):
                    bank = fc
                    fw.mm([lambda kc=kc, fc=fc, wt=wt, bank=bank: mm(
                        PS[bank][:, :], wt[:, kc, fc * 128:(fc + 1) * 128], hT[:, kc, :],
                        start=(kc == 0), stop=(kc == KC - 1)) for kc in range(KC)],
                        reads=[rw, rh], writes=[rPS[bank]])
                    hd = (i % 2) * 4 + fc
                    if i < 2:
                        fw.op(ACT, lambda hd=hd, bank=bank: act_(
                            qT[:, hd, :], PS[bank][:, :], AF.Copy, scale=float(128 ** -0.5)),
                            reads=[rPS[bank]], writes=[rq])
                    else:
                        fw.op(ACT, lambda hd=hd, bank=bank: act_(
                            kT[:, hd, :], PS[bank][:, :], AF.Copy), reads=[rPS[bank]], writes=[rk])
            if i >= 2:
                for tl in range(4):
                    bank = 4 + (tl % 2)
                    fw.mm([lambda kc=kc, tl=tl, wt=wt, bank=bank: mm(
                        PS[bank][:, :], hT[:, kc, tl * 128:(tl + 1) * 128], wt[:, kc, :],
                        start=(kc == 0), stop=(kc == KC - 1)) for kc in range(KC)],
                        reads=[rw, rh], writes=[rPS[bank]])
                    if i < 4:
                        fw.op(DVE, lambda tl=tl, bank=bank, i=i: nc.vector.tensor_copy(
                            ktok[:, tl, (i - 2) * 512:(i - 1) * 512], PS[bank][:, :]),
                            reads=[rPS[bank]], writes=[rkt])
                    else:
                        h0 = (i - 4) * 2
                        fw.op(DVE, lambda tl=tl, bank=bank, h0=h0: nc.vector.tensor_copy(
                            vaug[:, tl, h0:h0 + 2, 0:256], PS[bank][:, :].rearrange("p (a b) -> p a b", a=2)),
                            reads=[rPS[bank]], writes=[rv])
        fw.mm([lambda kc=kc, tl=tl: mm(PS[6][:, tl * 16:(tl + 1) * 16], hT[:, kc, tl * 128:(tl + 1) * 128], wg_bf[:, kc, :],
                                       start=(kc == 0), stop=(kc == KC - 1)) for tl in range(4) for kc in range(KC)],
              reads=[rh, rwg], writes=[rPS[6]])
        fw.op(DVE, lambda: nc.vector.tensor_tensor(zs[:, :, :], PS[6][:, 0:64].rearrange("p (a b) -> p a b", a=4),
                                                   bg_sb[:, :, :], ALU.add), reads=[rPS[6], rbg], writes=[rzs])
        fw.op(ACT, lambda: act_(nlf[:, :, :], zs[:, :, 8:16], AF.Exp, scale=-1.0), reads=[rzs], writes=[rnlf])
        fw.op(ACT, lambda: act_(nlf[:, :, :], nlf[:, :, :], AF.Ln, bias=one_sb[:, 0:1]), reads=[rnlf], writes=[rnlf])
        fw.mm([lambda tl=tl: mm(PS[6][:, 64 + tl * 8:72 + tl * 8], U32[:, :], nlf[:, tl, :], start=True, stop=True)
               for tl in range(4)] +
              [lambda tl=tl: mm(PS[6][:, 96 + tl * 8:104 + tl * 8], ones32[:, :], nlf[:, tl, :], start=True, stop=True)
               for tl in range(4)], reads=[rU32, rones32, rnlf], writes=[rPS[6]])
        fw.op(DVE, lambda: nc.vector.tensor_tensor(ecol[:, :, :], PS[6][:, 64:96].rearrange("p (a b) -> p a b", a=4),
                                                   zs[:, :, 0:8], ALU.add), reads=[rPS[6], rzs], writes=[recol])
        fw.op(ACT, lambda: act_(ecol[:, :, :], ecol[:, :, :], AF.Exp), reads=[recol], writes=[recol])

        for tl in range(4):
            ch = chunk_base + tl
            par = ch % 2
            tsl = slice(tl * 128, (tl + 1) * 128)
            if prepass:
                fw.op(ACT, lambda tl=tl, par=par: act_(eL[:, par, :], PS[6][:, 96 + tl * 8:104 + tl * 8], AF.Exp, scale=-1.0),
                      reads=[rPS[6]], writes=[reL])
                for hg in range(2):
                    hs = [hg * 4 + j for j in range(4)]
                    fw.op(DVE, lambda tl=tl, hg=hg: nc.vector.tensor_tensor(
                        vs[:, :, 0:256], vaug[:, tl, hg * 4:hg * 4 + 4, :],
                        ecol[:, tl, hg * 4:hg * 4 + 4].unsqueeze(2).broadcast_to([128, 4, 256]), ALU.mult),
                        reads=[rv, recol], writes=[rvs])
                    fw.op(DVE, lambda tl=tl, hg=hg: nc.vector.tensor_copy(
                        vs[:, :, 256:384], ecol[:, tl, hg * 4:hg * 4 + 4].unsqueeze(2).broadcast_to([128, 4, 128])),
                        reads=[recol], writes=[rvs])
                    for j, h in enumerate(hs):
                        bank = 4 + (j % 2)
                        fw.mm([lambda j=j, h=h, tl=tl, bank=bank: mm(
                            PS[bank][:, 0:384], ktok[:, tl, h * 128:(h + 1) * 128], vs[:, j, :], start=True, stop=True)],
                            reads=[rkt, rvs], writes=[rPS[bank]])
                        fw.op(DVE, lambda h=h, bank=bank, par=par: nc.vector.scalar_tensor_tensor(
                            Cf[h][:, :], Cf[h][:, :], eL[:, 1 - par, h:h + 1], PS[bank][:, 0:384], ALU.mult, ALU.add),
                            reads=[rCf[h], reL, rPS[bank]], writes=[rCf[h]])
                        fw.op(ACT, lambda h=h, par=par: act_(
                            Cb[h][:, :], Cf[h][:, :], AF.Copy, scale=eL[:, par, h:h + 1]),
                            reads=[rCf[h], reL], writes=[rCb[h]])
                continue
            fw.op(DVE, lambda tl=tl: nc.vector.tensor_copy(
                nlfrep[:, :, :], nlf[:, tl, :].unsqueeze(2).broadcast_to([128, 8, 128])), reads=[rnlf], writes=[rnr])
            for hg in range(2):
                hs = [hg * 4 + j for j in range(4)]
                fw.mm([lambda j=j, h=h: mm(PS[7][:, j * 128:(j + 1) * 128], nlfrep[:, h, :], U32[:, :], start=True, stop=True)
                       for j, h in enumerate(hs)], reads=[rnr, rU32], writes=[rPS[7]])
                fw.op(ACT, lambda: act_(rb[:, :, :], PS[7][:, :].rearrange("p (a b) -> p a b", a=4), AF.Exp),
                      reads=[rPS[7]], writes=[rrb])
                fw.op(ACT, lambda hg=hg, par=par: act_(
                    eL[:, par, hg * 4:hg * 4 + 4], PS[7][:, :].rearrange("p (a b) -> p a b", a=4)[:, :, 127], AF.Exp, scale=-1.0),
                    reads=[rPS[7]], writes=[reL])
                fw.op(DVE, lambda tl=tl, hg=hg: nc.vector.tensor_tensor(
                    vs[:, :, 0:256], vaug[:, tl, hg * 4:hg * 4 + 4, :],
                    ecol[:, tl, hg * 4:hg * 4 + 4].unsqueeze(2).broadcast_to([128, 4, 256]), ALU.mult),
                    reads=[rv, recol], writes=[rvs])
                fw.op(DVE, lambda tl=tl, hg=hg: nc.vector.tensor_copy(
                    vs[:, :, 256:384], ecol[:, tl, hg * 4:hg * 4 + 4].unsqueeze(2).broadcast_to([128, 4, 128])),
                    reads=[recol], writes=[rvs])
                fw.mm([lambda j=j, h=h, tsl=tsl: mm(PS[0][:, j * 128:(j + 1) * 128], kT[:, h, tsl], qT[:, h, tsl], start=True, stop=True)
                       for j, h in enumerate(hs)], reads=[rk, rq], writes=[rPS[0]])
                fw.op(DVE, lambda: nc.vector.tensor_tensor(MT[:, :, :], PS[0][:, :].rearrange("p (a b) -> p a b", a=4),
                                                           U4[:, :, :], ALU.mult), reads=[rPS[0], rU4], writes=[rMT])
                for ec in range(3):
                    fns = []
                    for j, h in enumerate(hs):
                        fns.append(lambda j=j, h=h, ec=ec, tsl=tsl: mm(
                            PS[1 + ec][:, j * 128:(j + 1) * 128], Cb[h][:, ec * 128:(ec + 1) * 128], qT[:, h, tsl],
                            start=True, stop=False))
                        fns.append(lambda j=j, h=h, ec=ec: mm(
                            PS[1 + ec][:, j * 128:(j + 1) * 128], vs[:, j, ec * 128:(ec + 1) * 128], MT[:, j, :],
                            start=False, stop=True))
                    fw.mm(fns, reads=[rCb[h] for h in hs] + [rq, rvs, rMT], writes=[rPS[1 + ec]])
                fw.op(ACT, lambda: act_(den[:, :, :], PS[3][:, :].rearrange("p (a b) -> p a b", a=4), AF.Abs),
                      reads=[rPS[3]], writes=[rden])
                fw.op(DVE, lambda: nc.vector.tensor_tensor(den[:, :, :], den[:, :, :], rb[:, :, :], ALU.max),
                      reads=[rden, rrb], writes=[rden])
                fw.op(DVE, lambda: nc.vector.reciprocal(den[:, :, :], den[:, :, :]), reads=[rden], writes=[rden])
                for ec in range(2):
                    fw.op(DVE, lambda ec=ec: nc.vector.tensor_tensor(
                        hh[:, ec, :], PS[1 + ec][:, :], den[:, :, :].rearrange("p a b -> p (a b)"), ALU.mult),
                        reads=[rPS[1 + ec], rden], writes=[rhh])
                    fw.op(ACT, lambda ec=ec: act_(sqh[:, ec, :], hh[:, ec, :], AF.Square),
                          reads=[rhh], writes=[rsqh])
                fw.mm([lambda ec=ec: mm(PS[4][:, :], ones_bf[:, :], sqh[:, ec, :], start=(ec == 0), stop=(ec == 1))
                       for ec in range(2)], reads=[rones, rsqh], writes=[rPS[4]])
                fw.op(ACT, lambda: act_(rsd[:, :], PS[4][:, :], AF.Sqrt, bias=eps_sb[:, 0:1], scale=1.0 / 256),
                      reads=[rPS[4], reps], writes=[rrsd])
                fw.op(DVE, lambda: nc.vector.reciprocal(rsd[:, :], rsd[:, :]),
                      reads=[rrsd], writes=[rrsd])
                for ec in range(2):
                    fw.op(DVE, lambda ec=ec, hg=hg, tsl=tsl: nc.vector.tensor_tensor(
                        yT[:, hg * 8 + ec:hg * 8 + 8:2, tsl], hh[:, ec, :].rearrange("p (a b) -> p a b", a=4),
                        rsd[:, :].rearrange("p (a b) -> p a b", a=4), ALU.mult), reads=[rhh, rrsd], writes=[ry])
                for j, h in enumerate(hs):
                    bank = 5 + (j % 2)
                    fw.mm([lambda j=j, h=h, tl=tl, bank=bank: mm(
                        PS[bank][:, 0:384], ktok[:, tl, h * 128:(h + 1) * 128], vs[:, j, :], start=True, stop=True)],
                        reads=[rkt, rvs], writes=[rPS[bank]])
                    fw.op(DVE, lambda h=h, bank=bank, par=par: nc.vector.scalar_tensor_tensor(
                        Cf[h][:, :], Cf[h][:, :], eL[:, 1 - par, h:h + 1], PS[bank][:, 0:384], ALU.mult, ALU.add),
                        reads=[rCf[h], reL, rPS[bank]], writes=[rCf[h]])
                    fw.op(ACT, lambda h=h, par=par: act_(
                        Cb[h][:, :], Cf[h][:, :], AF.Copy, scale=eL[:, par, h:h + 1]),
                        reads=[rCf[h], reL], writes=[rCb[h]])

        if prepass:
            return
        cnt = [0]

        def evo(i, fc, bank):
            f = i * 4 + fc
            k = cnt[0] % 2
            cnt[0] += 1
            fw.op(ACT, lambda: act_(hh[:, k, :], PS[bank][:, :], AF.Sigmoid),
                  reads=[rPS[bank]], writes=[rhh])
            fw.op(DVE, lambda: nc.vector.scalar_tensor_tensor(
                yT[:, f, :], hh[:, k, :], ng_sb[:, f:f + 1], yT[:, f, :], ALU.mult, ALU.mult),
                reads=[rhh, rng_, ry], writes=[ry])
        proj_fm(4, evo)
        for i in range(4):
            wt, rw = wnext()
            for fc in range(4):
                bank = fc
                f = i * 4 + fc
                fw.mm([lambda kc=kc, fc=fc, wt=wt, bank=bank: mm(
                    PS[bank][:, :], wt[:, kc, fc * 128:(fc + 1) * 128], yT[:, kc, :],
                    start=(kc == 0), stop=(kc == KC - 1)) for kc in range(KC)],
                    reads=[rw, ry], writes=[rPS[bank]])
                fw.op(DVE, lambda f=f, bank=bank: nc.vector.scalar_tensor_tensor(
                    x_sb[:, f, :], PS[bank][:, :], modv[:, cmb, 32 + f:33 + f], x_sb[:, f, :], ALU.mult, ALU.add),
                    reads=[rPS[bank], rmod, rx], writes=[rx])

    def rglru_front(blk):
        cmb = 2
        s.reset()
        adaln(cmb)
        if blk == 0:
            adaln(cmb, x_sb=x_halo, rx=rxhalo, hT=hT_halo, rh=rhhalo, n=4)
        s.reset()
        ysg = s.take_tmp("ysg", [128, 2, T], nres=2)
        asg = s.take_tmp("asg", [128, 2, T], nres=2)
        acum, racum = s.take_tmp("acum", [128, T])
        xbp, rxbp = s.take_tmp("xbp", [128, 4, T + 3])
        xc, rxc = s.take_tmp("xc", [128, 4, T])
        xcb, rxcb = s.take_tmp("xcb", [128, 4, T], BF16)
        gg, rgg = s.take_tmp("gg", [128, 4, T], BF16)
        tm = [s.take_tmp(f"t{i}", [128, T]) for i in range(4)]
        ta = [s.take_tmp(f"ta{i}", [128, T]) for i in range(4)]
        ta2 = [s.take_tmp(f"tb{i}", [128, T]) for i in range(4)]
        tix = [s.take_tmp(f"tc{i}", [128, T]) for i in range(4)]
        for i in range(4):
            wt, rw = wnext()
            for fc in range(4):
                c = i * 4 + fc
                bank = fc
                if blk == 0:
                    fw.mm([lambda kc=kc, fc=fc, wt=wt, bank=bank: mm(
                        PS[bank][:, 0:4], wt[:, kc, fc * 128:(fc + 1) * 128], hT_halo[:, kc, :],
                        start=(kc == 0), stop=(kc == KC - 1)) for kc in range(KC)],
                        reads=[rw, rhhalo], writes=[rPS[bank]])
                    fw.op(ACT, lambda c=c, bank=bank: act_(halo[:, c, :], PS[bank][:, 1:4], AF.Copy, scale=flag_sb[:, 0:1]),
                          reads=[rPS[bank], rflag], writes=[rhalo])
                fw.mm([lambda kc=kc, fc=fc, wt=wt, bank=bank: mm(
                    PS[bank][:, :], wt[:, kc, fc * 128:(fc + 1) * 128], hT[:, kc, :],
                    start=(kc == 0), stop=(kc == KC - 1)) for kc in range(KC)],
                    reads=[rw, rh], writes=[rPS[bank]])
                fw.op(ACT, lambda fc=fc, c=c: act_(xbp[:, fc, 0:3], halo[:, c, :], AF.Copy),
                      reads=[rhalo], writes=[rxbp])
                fw.op(ACT, lambda fc=fc, bank=bank: act_(xbp[:, fc, 3:T + 3], PS[bank][:, :], AF.Copy),
                      reads=[rPS[bank]], writes=[rxbp])
                fw.op(ACT, lambda fc=fc, c=c: act_(halo[:, c, :], xbp[:, fc, T:T + 3], AF.Copy),
                      reads=[rxbp], writes=[rhalo])
                fw.op(ACT, lambda fc=fc, c=c: act_(
                    xc[:, fc, :], xbp[:, fc, 3:T + 3], AF.Identity, bias=bv_sb[:, 0, c:c + 1], scale=cw_sb[:, c, 3:4]),
                    reads=[rxbp, rbv, rcw], writes=[rxc])
                for k in range(3):
                    fw.op(DVE, lambda fc=fc, c=c, k=k: nc.vector.scalar_tensor_tensor(
                        xc[:, fc, :], xbp[:, fc, k:k + T], cw_sb[:, c, k:k + 1], xc[:, fc, :], ALU.mult, ALU.add),
                        reads=[rxbp, rcw, rxc], writes=[rxc])
                fw.op(ACT, lambda fc=fc: act_(xcb[:, fc, :], xc[:, fc, :], AF.Copy),
                      reads=[rxc], writes=[rxcb])
            wt, rw = wnext()
            for fc in range(4):
                bank = 4 + (fc % 2)
                fw.mm([lambda kc=kc, fc=fc, wt=wt, bank=bank: mm(
                    PS[bank][:, :], wt[:, kc, fc * 128:(fc + 1) * 128], hT[:, kc, :],
                    start=(kc == 0), stop=(kc == KC - 1)) for kc in range(KC)],
                    reads=[rw, rh], writes=[rPS[bank]])
                z2, rz2 = tm[0]
                fw.op(ACT, lambda bank=bank, z2=z2: act_(z2[:, :], PS[bank][:, :], AF.Square),
                      reads=[rPS[bank]], writes=[rz2])
                fw.op(DVE, lambda z2=z2: nc.vector.tensor_scalar(z2[:, :], z2[:, :], 0.044715, 1.0, ALU.mult, ALU.add),
                      reads=[rz2], writes=[rz2])
                fw.op(DVE, lambda bank=bank, z2=z2: nc.vector.tensor_tensor(z2[:, :], z2[:, :], PS[bank][:, :], ALU.mult),
                      reads=[rz2, rPS[bank]], writes=[rz2])
                fw.op(ACT, lambda z2=z2: act_(z2[:, :], z2[:, :], AF.Tanh, scale=0.7978845608028654),
                      reads=[rz2], writes=[rz2])
                fw.op(DVE, lambda fc=fc, bank=bank, z2=z2: nc.vector.scalar_tensor_tensor(
                    gg[:, fc, :], z2[:, :], 1.0, PS[bank][:, :], ALU.add, ALU.mult),
                    reads=[rz2, rPS[bank]], writes=[rgg])
            for fc in range(4):
                c = i * 4 + fc
                n = c // 2
                m = c % 2
                kl = (fc // 2) * 2
                fw.mm([lambda kk=kk, n=n, m=m, kl=kl: mm(PS[6][:, :], wra_bf[:, n, kk, m * 128:(m + 1) * 128], xcb[:, kl + kk, :],
                                                         start=(kk == 0), stop=(kk == 1)) for kk in range(2)],
                      reads=[rwra, rxcb], writes=[rPS[6]])
                fw.mm([lambda kk=kk, n=n, m=m, kl=kl: mm(PS[7][:, :], wri_bf[:, n, kk, m * 128:(m + 1) * 128], xcb[:, kl + kk, :],
                                                         start=(kk == 0), stop=(kk == 1)) for kk in range(2)],
                      reads=[rwri, rxcb], writes=[rPS[7]])
                (r_, rr_), (i_, ri_) = tm[1], tm[2]
                a_, ra_ = ta[fc]
                a2, ra2 = ta2[fc]
                ix, rix = tix[fc]
                fw.op(ACT, lambda c=c: act_(r_[:, :], PS[6][:, :], AF.Tanh, bias=hb_sb[:, 0, c:c + 1], scale=0.5),
                      reads=[rPS[6], rhb], writes=[rr_])
                fw.op(ACT, lambda c=c: act_(i_[:, :], PS[7][:, :], AF.Tanh, bias=hb_sb[:, 1, c:c + 1], scale=0.5),
                      reads=[rPS[7], rhb], writes=[ri_])
                fw.op(ACT, lambda c=c, a_=a_: act_(a_[:, :], r_[:, :], AF.Exp, bias=m4[:, c:c + 1], scale=m4[:, c:c + 1]),
                      reads=[rr_, rm4], writes=[ra_])
                fw.op(ACT, lambda c=c, a2=a2: act_(a2[:, :], r_[:, :], AF.Exp, bias=m8[:, c:c + 1], scale=m8[:, c:c + 1]),
                      reads=[rr_, rm8], writes=[ra2])
                fw.op(DVE, lambda fc=fc, ix=ix: nc.vector.scalar_tensor_tensor(ix[:, :], i_[:, :], 1.0, xc[:, fc, :], ALU.add, ALU.mult),
                      reads=[ri_, rxc], writes=[rix])
            for fc in range(4):
                a2, ra2 = ta2[fc]
                fw.op(ACT, lambda a2=a2: act_(a2[:, :], a2[:, :], AF.Sqrt, bias=one_sb[:, 0:1], scale=-1.0),
                      reads=[ra2, rone], writes=[ra2])
            for fc in range(4):
                c = i * 4 + fc
                a_, ra_ = ta[fc]
                a2, ra2 = ta2[fc]
                ix, rix = tix[fc]
                hs_, rhs_ = tm[3]
                fw.op(DVE, lambda a2=a2, ix=ix: nc.vector.scalar_tensor_tensor(ix[:, :], a2[:, :], 0.5, ix[:, :], ALU.mult, ALU.mult),
                      reads=[ra2, rix], writes=[rix])
                fw.op(DVE, lambda c=c, a_=a_, ix=ix: nc.vector.tensor_tensor_scan(hs_[:, :], a_[:, :], ix[:, :], hst[:, c:c + 1], ALU.mult, ALU.add),
                      reads=[ra_, rix, rhst], writes=[rhs_])
                fw.op(ACT, lambda c=c: act_(hst[:, c:c + 1], hs_[:, T - 1:T], AF.Copy),
                      reads=[rhs_], writes=[rhst])
                fw.op(DVE, lambda c=c, a_=a_: nc.vector.tensor_tensor_scan(acum[:, :], a_[:, :], zeros_t[:, :], Ast[:, c:c + 1], ALU.mult, ALU.add),
                      reads=[ra_, rzeros, rAst], writes=[racum])
                fw.op(ACT, lambda c=c: act_(Ast[:, c:c + 1], acum[:, T - 1:T], AF.Copy), reads=[racum], writes=[rAst])
                k = c % 2
                fw.op(DVE, lambda k=k, fc=fc: nc.vector.scalar_tensor_tensor(ysg[0][:, k, :], hs_[:, :], 0.5, gg[:, fc, :], ALU.mult, ALU.mult),
                      reads=[rhs_, rgg], writes=[ysg[1][k]])
                fw.op(DVE, lambda k=k, fc=fc: nc.vector.scalar_tensor_tensor(asg[0][:, k, :], acum[:, :], 0.5, gg[:, fc, :], ALU.mult, ALU.mult),
                      reads=[racum, rgg], writes=[asg[1][k]])
                fw.dma(SP, y0T[c * 128:(c + 1) * 128, blk * T:(blk + 1) * T], ysg[0][:, k, :], f"ys{k}", reads=[ysg[1][k]], writes=[ry0[blk]])
                fw.dma(SP, AgT[c * 128:(c + 1) * 128, blk * T:(blk + 1) * T], asg[0][:, k, :], f"as{k}", reads=[asg[1][k]], writes=[rAg[blk]])

    def rglru_back(blk):
        cmb = 2
        s.reset()
        yT, ry = s.take_tmp("yT", [128, KC, T], BF16)
        ly = s.take_tmp("ly", [128, 2, 4, T], nres=2)
        la = s.take_tmp("la", [128, 2, 4, T], nres=2)
        for q in range(4):
            k = q % 2
            fw.dma(SP, ly[0][:, k, :, :], y0T[q * 512:(q + 1) * 512, blk * T:(blk + 1) * T].rearrange("(c p) t -> p c t", p=128),
                   f"ly{k}", reads=[ry0[blk]], writes=[ly[1][k]])
            fw.dma(SP, la[0][:, k, :, :], AgT[q * 512:(q + 1) * 512, blk * T:(blk + 1) * T].rearrange("(c p) t -> p c t", p=128),
                   f"la{k}", reads=[rAg[blk]], writes=[la[1][k]])
            for cc in range(4):
                c = q * 4 + cc
                fw.op(DVE, lambda k=k, cc=cc, c=c: nc.vector.scalar_tensor_tensor(
                    yT[:, c, :], la[0][:, k, cc, :], h_init[:, c:c + 1], ly[0][:, k, cc, :], ALU.mult, ALU.add),
                    reads=[la[1][k], ly[1][k], rhinit], writes=[ry])
        for i in range(4):
            wt, rw = wnext()
            for fc in range(4):
                bank = fc
                f = i * 4 + fc
                fw.mm([lambda kc=kc, fc=fc, wt=wt, bank=bank: mm(
                    PS[bank][:, :], wt[:, kc, fc * 128:(fc + 1) * 128], yT[:, kc, :],
                    start=(kc == 0), stop=(kc == KC - 1)) for kc in range(KC)],
                    reads=[rw, ry], writes=[rPS[bank]])
                fw.op(DVE, lambda f=f, bank=bank: nc.vector.scalar_tensor_tensor(
                    x_sb[:, f, :], PS[bank][:, :], modv[:, cmb, 32 + f:33 + f], x_sb[:, f, :], ALU.mult, ALU.add),
                    reads=[rPS[bank], rmod, rx], writes=[rx])

    out_evs = []

    def final_norm_store(blk):
        s.reset()
        sq = s.take_tmp("sq", [128, 2, T], BF16, nres=2)
        for kc in range(KC):
            fw.op(ACT, lambda kc=kc: act_(sq[0][:, kc % 2, :], x_sb[:, kc, :], AF.Square),
                  reads=[rx], writes=[sq[1][kc % 2]])
            fw.mm([lambda kc=kc: mm(PS[7][:, :], ones_bf[:, :], sq[0][:, kc % 2, :], start=(kc == 0), stop=(kc == KC - 1))],
                  reads=[sq[1][kc % 2], rones], writes=[rPS[7]])
        rstd = s.take_tmp("rstd", [128, T])
        fw.op(ACT, lambda: act_(rstd[0][:, :], PS[7][:, :], AF.Sqrt, bias=eps_sb[:, 0:1], scale=1.0 / D),
              reads=[rPS[7], reps], writes=[rstd[1]])
        fw.op(DVE, lambda: nc.vector.reciprocal(rstd[0][:, :], rstd[0][:, :]),
              reads=[rstd[1]], writes=[rstd[1]])
        ob, rob = s.take_tmp("ob", [128, KC, T])
        for kc in range(KC):
            fw.op(DVE, lambda kc=kc: nc.vector.scalar_tensor_tensor(
                ob[:, kc, :], x_sb[:, kc, :], fg_sb[:, kc:kc + 1], rstd[0][:, :], ALU.mult, ALU.mult),
                reads=[rx, rfg, rstd[1]], writes=[rob])
        ev = fw.dma(SP, outT[:, blk * T:(blk + 1) * T].rearrange("(kc p) t -> p kc t", p=128), ob[:, :, :], "out", reads=[rob])
        out_evs.append(ev)

    def dbg_store():
        ev = fw.dma(SP, outT[:, 0:T].rearrange("(kc p) t -> p kc t", p=128), x_sb[:, :, :], "out", reads=[rx])
        out_evs.append(ev)

    groups = [[2 * i, 2 * i + 1] for i in range(ncores // 2)]
    cc_n = [0]

    def allgather(src_t, rsrc, dst_t, rdst, groups=groups):
        if stub_cc:
            fw.dma(POOL, dst_t[0:128, :], src_t[:, :], "ccstub", reads=[rsrc], writes=[rdst])
            return
        sem = nc.alloc_semaphore(name=f"cc{cc_n[0]}")
        cc_n[0] += 1
        fw._wait(POOL, fw._deps([rsrc], [rdst]))
        nc.gpsimd.collective_compute("AllGather", ALU.bypass, replica_groups=groups,
                                     ins=[src_t.ap()], outs=[dst_t.ap()]).then_inc(sem, 1)
        fw._record((sem, 1, None), [rsrc], [rdst])

    def xload(src, blk):
        rd = [rx2[blk]] if src is x2T else []
        fw.dma(SP, x_sb[:, :, :], src[:, blk * T:(blk + 1) * T].rearrange("(kc p) t -> p kc t", p=128), "xin",
               reads=rd, writes=[rx])

    s.reset()
    modloc, rmodloc = s.take_tmp("modloc", [128, 96])
    modall, rmodall = s.take_tmp("modall", [128, 192, NB])
    for i in range(6):
        wt, rw = wnext()
        fns = []
        for fc in range(4):
            j = i * 4 + fc
            for kc in range(KC):
                fns.append(lambda fc=fc, kc=kc, j=j, wt=wt: mm(
                    PS[0][:, j * NB:(j + 1) * NB], wt[:, kc, fc * 128:(fc + 1) * 128], sc_bf[:, kc, :],
                    start=(kc == 0), stop=(kc == KC - 1)))
        fw.mm(fns, reads=[rw, rsc], writes=[rPS[0]])
    fw.op(DVE, lambda: nc.vector.tensor_copy(modloc[:, :], PS[0][:, 0:96]), reads=[rPS[0]], writes=[rmodloc])
    fw.dma(SP, mod_src[:, :], modloc[:, :], "modout", reads=[rmodloc], writes=[rmod_src])
    allgather(mod_src, rmod_src, mod_q, rmod_q, groups=[[0, 1, 2, 3], [4, 5, 6, 7]])
    allgather(mod_q, rmod_q, mod_all, rmod_all, groups=[[0, 4], [1, 5], [2, 6], [3, 7]])
    fw.dma(SP, modall[:, :, :].rearrange("p (r j) b -> p r (j b)", r=8), mod_all[:, :].rearrange("(r p) q -> p r q", p=128),
           "modin", reads=[rmod_all], writes=[rmodall])
    mv = modv[:, :, :].rearrange("p a b -> p (a b)")
    fw.op(DVE, lambda: nc.vector.tensor_scalar(mv, modall[:, :, 0], bsel[:, 0:1], None, ALU.mult),
          reads=[rmodall, rbsel], writes=[rmod])
    for bb in range(1, NB):
        fw.op(DVE, lambda bb=bb: nc.vector.scalar_tensor_tensor(mv, modall[:, :, bb], bsel[:, bb:bb + 1], mv, ALU.mult, ALU.add),
              reads=[rmodall, rbsel, rmod], writes=[rmod])
    fw.op(DVE, lambda: nc.vector.tensor_tensor(mv, mv, adab[:, :, :].rearrange("p a b -> p (a b)"), ALU.add),
          reads=[rmod, radab], writes=[rmod])
    for cmb in range(4):
        fw.op(DVE, lambda cmb=cmb: nc.vector.tensor_scalar(modv[:, cmb, 16:32], modv[:, cmb, 16:32], 1.0, None, ALU.add),
              reads=[rmod], writes=[rmod])
    for pj in range(nblk):
        xload(xpre, pj)
        mlstm_layer(pj * 4, prepass=True)
    for h in range(8):
        fw.op(DVE, lambda h=h: nc.vector.tensor_scalar(Cf[h][:, :], Cf[h][:, :], flag_sb[:, 0:1], None, ALU.mult),
              reads=[rCf[h], rflag], writes=[rCf[h]])
        fw.op(DVE, lambda h=h: nc.vector.tensor_scalar(Cb[h][:, :], Cb[h][:, :], flag_sb[:, 0:1], None, ALU.mult),
              reads=[rCb[h], rflag], writes=[rCb[h]])
    for blk in range(nblk):
        xload(xT, blk)
        mlstm_layer(16 + blk * 4)
        mlp(0, 1)
        fw.dma(SP, x2T[:, blk * T:(blk + 1) * T].rearrange("(kc p) t -> p kc t", p=128), x_sb[:, :, :], "x2s",
               reads=[rx], writes=[rx2[blk]])
        if blk == nblk - 1:
            fw.dma(SP, halo_src[:, :].rearrange("p (kc t) -> p kc t", kc=KC), x_sb[:, :, T - 4:T], "halo",
                   reads=[rx], writes=[rhalo_src])
    allgather(halo_src, rhalo_src, halo_all, rhalo_all)
    fw.dma(SP, x_halo[:, :, :], halo_all[0:128, :].rearrange("p (kc t) -> p kc t", kc=KC), "haloin",
           reads=[rhalo_all], writes=[rxhalo])
    for blk in range(nblk):
        xload(x2T, blk)
        rglru_front(blk)
    fw.dma(SP, hend_src[:, :], hst[:, :], "hend", reads=[rhst], writes=[rhend_src])
    allgather(hend_src, rhend_src, hend_all, rhend_all)
    fw.dma(SP, h_init[:, :], hend_all[0:128, :], "hendin", reads=[rhend_all], writes=[rhinit])
    fw.op(DVE, lambda: nc.vector.tensor_scalar(h_init[:, :], h_init[:, :], flag_sb[:, 0:1], None, ALU.mult),
          reads=[rhinit, rflag], writes=[rhinit])
    for blk in range(nblk):
        xload(x2T, blk)
        rglru_back(blk)
        mlp(1, 3)
        final_norm_store(blk)

    if dbg is None:
        assert wstate["next_use"] == len(plan), (wstate, len(plan))
    fw._wait(SP, out_evs)
    return nc


def _prep_inputs(inp, b):
    f32 = np.float32
    x = inp["x"]

    def fm(v):
        return np.ascontiguousarray(np.asarray(v, f32).reshape(KC, 128).T)

    m = {}
    m["cT"] = np.ascontiguousarray(np.stack([fm(inp["c"][bb]) for bb in range(NB)], axis=2).reshape(128, KC * NB))
    bs = np.zeros((128, NB), f32)
    bs[:, b] = 1.0
    m["bsel"] = bs
    m["ada_b"] = np.ascontiguousarray(inp["ada_b"].reshape(4, 48, 128).transpose(2, 0, 1).reshape(128, 4 * 48))
    m["a_w_in"] = np.ascontiguousarray(inp["a_w_in"][0])
    bg = inp["a_b_gate"][0].reshape(16)
    m["a_bg"] = np.ascontiguousarray(np.broadcast_to(np.tile(bg, 4)[None, :], (128, 64))).astype(f32)
    m["a_ng"] = fm(inp["a_norm_g"][0])
    m["a_w_out"] = np.ascontiguousarray(inp["a_w_out"][0])
    m["b_w_in"] = np.ascontiguousarray(inp["b_w_in"][0])
    cw = inp["b_conv_w"][0]
    m["b_cw"] = np.ascontiguousarray(cw.T.reshape(KC, 128, 4).transpose(1, 0, 2).reshape(128, KC * 4))
    m["b_vec"] = np.ascontiguousarray(np.concatenate(
        [fm(inp["b_conv_b"][0]), fm(inp["b_b_ra"][0]), fm(inp["b_b_ri"][0]), fm(inp["b_lam"][0])], axis=1))
    m["b_w_ra"] = np.ascontiguousarray(inp["b_w_ra"][0])
    m["b_w_ri"] = np.ascontiguousarray(inp["b_w_ri"][0])
    m["b_w_out"] = np.ascontiguousarray(inp["b_w_out"][0])
    m["mlp_w1"] = np.ascontiguousarray(inp["mlp_w1"])
    m["mlp_w2"] = np.ascontiguousarray(inp["mlp_w2"])
    m["fin_g"] = fm(inp["final_g"])
    m["triu"] = np.triu(np.ones((128, 128), f32))
    return m


_NC_CACHE = {}


def kernel(**inputs):
    inp = {k: np.asarray(v, dtype=np.float32) for k, v in inputs.items()}
    if "full" not in _NC_CACHE:
        _NC_CACHE["full"] = build()
    nc = _NC_CACHE["full"]
    in_maps = []
    for core in range(NCORES):
        b, half = core // 2, core % 2
        m = _prep_inputs(inp, b)
        xb = inp["x"][b]
        m["xT"] = np.ascontiguousarray(xb[half * TOK:(half + 1) * TOK].T)
        m["xpre"] = np.ascontiguousarray(xb[0:TOK].T)
        m["flag"] = np.full((128, 1), float(half), np.float32)
        aw = inp["ada_w"].reshape(4, D, 3 * D)
        m["ada_sl"] = np.ascontiguousarray(aw[core // 2][:, (core % 2) * 3072:(core % 2 + 1) * 3072])
        in_maps.append(m)
    res = run_bass_kernel_spmd(nc, in_maps, core_ids=list(range(NCORES)))
    out = np.empty((NB, SEQ, D), np.float32)
    for core in range(NCORES):
        b, half = core // 2, core % 2
        out[b, half * TOK:(half + 1) * TOK, :] = res.results[core]["outT"].T
    return out
```

```python
import numpy as np
import concourse.bass as bass
import concourse.mybir as mybir
from concourse.bass_utils import run_bass_kernel_spmd

F32 = mybir.dt.float32
BF16 = mybir.dt.bfloat16
ALU = mybir.AluOpType
AF = mybir.ActivationFunctionType

D = 2048
SEQ = 4096
NB = 4
KC = 16
T = 512
DFF = 8192
IN_A = 6160
EPS = 1e-6
NCORES = 8
TOK = 2048
NBLK = TOK // 512
WDEPTH = 2


class Res:
    __slots__ = ("name", "w", "r")

    def __init__(self, name):
        self.name = name
        self.w = None
        self.r = {}

    def add_r(self, ev):
        k = id(ev[0])
        if k not in self.r or self.r[k][1] < ev[1]:
            self.r[k] = ev


class Eng:
    def __init__(self, nc, name, h):
        self.name = name
        self.h = h
        self.sem = nc.alloc_semaphore(name="s_" + name)
        self.count = 0
        self.known = {}


class FW:
    def __init__(self, nc):
        self.nc = nc
        self.pe = Eng(nc, "pe", nc.tensor)
        self.act = Eng(nc, "act", nc.scalar)
        self.dve = Eng(nc, "dve", nc.vector)
        self.pool = Eng(nc, "pool", nc.gpsimd)
        self.sp = Eng(nc, "sp", nc.sync)
        self.dsems = {}

    def _wait(self, eng, evs):
        best = {}
        for ev in evs:
            if ev is None:
                continue
            sem, val, owner = ev
            if owner is eng:
                continue
            k = id(sem)
            if k not in best or best[k][1] < val:
                best[k] = ev
        for k, (sem, val, owner) in best.items():
            if eng.known.get(k, 0) >= val:
                continue
            eng.h.wait_ge(sem, val)
            eng.known[k] = val

    @staticmethod
    def _deps(reads, writes):
        evs = []
        for r in reads:
            evs.append(r.w)
        for w in writes:
            evs.append(w.w)
            evs.extend(w.r.values())
        return evs

    @staticmethod
    def _record(ev, reads, writes):
        for r in reads:
            r.add_r(ev)
        for w in writes:
            w.w = ev
            w.r = {}

    def op(self, eng, fn, reads=(), writes=()):
        self._wait(eng, self._deps(reads, writes))
        ins = fn()
        eng.count += 1
        ins.then_inc(eng.sem, 1)
        self._record((eng.sem, eng.count, eng), reads, writes)

    def mm(self, fns, reads=(), writes=()):
        eng = self.pe
        self._wait(eng, self._deps(reads, writes))
        ins = None
        for f in fns:
            ins = f()
        eng.count += 1
        ins.then_inc(eng.sem, 1)
        self._record((eng.sem, eng.count, eng), reads, writes)

    def dma(self, q, out, in_, key, reads=(), writes=()):
        if key not in self.dsems:
            self.dsems[key] = [self.nc.alloc_semaphore(name="d_" + key), 0]
        ds = self.dsems[key]
        self._wait(q, self._deps(reads, writes))
        q.h.dma_start(out=out, in_=in_).then_inc(ds[0], 16)
        ds[1] += 16
        ev = (ds[0], ds[1], None)
        self._record(ev, reads, writes)
        return ev


def _prune(res_list):
    pass


def build(ncores=NCORES, dbg=None, stub_cc=False):
    nblk = NBLK
    nc = bass.Bass("TRN2", target_bir_lowering=False)
    fw = FW(nc)
    PE, ACT, DVE, POOL, SP = fw.pe, fw.act, fw.dve, fw.pool, fw.sp
    mm = nc.tensor.matmul
    _zb = []

    def act_(out, in_, func, bias=None, scale=1.0):
        if func == AF.Copy:
            assert bias is None
            return nc.scalar.activation(out, in_, func, scale=scale)
        if bias is None:
            bias = _zb[0][:, 0:1]
        elif isinstance(bias, float):
            raise ValueError("float bias")
        return nc.scalar.activation(out, in_, func, bias=bias, scale=scale)

    def din(name, shape):
        return nc.dram_tensor(name, shape, F32, kind="ExternalInput").ap()

    xT = din("xT", [D, TOK])
    xpre = din("xpre", [D, TOK])
    flag_d = din("flag", [128, 1])
    cT = din("cT", [128, KC * NB])
    bsel_d = din("bsel", [128, NB])
    ada_sl = din("ada_sl", [D, 3072])
    ada_b = din("ada_b", [128, 4 * 48])
    a_w_in = din("a_w_in", [D, IN_A])
    a_bg = din("a_bg", [128, 64])
    a_ng = din("a_ng", [128, KC])
    a_w_out = din("a_w_out", [D, D])
    b_w_in = din("b_w_in", [D, 2 * D])
    b_cw = din("b_cw", [128, KC * 4])
    b_vec = din("b_vec", [128, 4 * KC])
    b_w_ra = din("b_w_ra", [8, 256, 256])
    b_w_ri = din("b_w_ri", [8, 256, 256])
    b_w_out = din("b_w_out", [D, D])
    mlp_w1 = din("mlp_w1", [2, D, DFF])
    mlp_w2 = din("mlp_w2", [2, DFF, D])
    fin_g = din("fin_g", [128, KC])
    triu = din("triu", [128, 128])
    outT = nc.dram_tensor("outT", [D, TOK], F32, kind="ExternalOutput").ap()
    x2T = nc.dram_tensor("x2T", [D, TOK], F32).ap()
    y0T = nc.dram_tensor("y0T", [D, TOK], F32).ap()
    AgT = nc.dram_tensor("AgT", [D, TOK], F32).ap()
    halo_src = nc.dram_tensor("halo_src", [128, 64], F32)
    halo_all = nc.dram_tensor("halo_all", [256, 64], F32)
    hend_src = nc.dram_tensor("hend_src", [128, KC], F32)
    hend_all = nc.dram_tensor("hend_all", [256, KC], F32)
    mod_src = nc.dram_tensor("mod_src", [128, 96], F32)
    mod_q = nc.dram_tensor("mod_q", [512, 96], F32)
    mod_all = nc.dram_tensor("mod_all", [1024, 96], F32)
    rmod_src, rmod_q, rmod_all = Res("ms"), Res("mq"), Res("ma")
    rx2 = [Res(f"x2_{j}") for j in range(4)]
    ry0 = [Res(f"y0_{j}") for j in range(4)]
    rAg = [Res(f"Ag_{j}") for j in range(4)]
    rhalo_src, rhalo_all, rhend_src, rhend_all = Res("hs"), Res("ha"), Res("es"), Res("ea")

    def sb(name, shape, dt=F32):
        return nc.alloc_sbuf_tensor(name, shape, dt), Res(name)

    U32, rU32 = sb("U32", [128, 128])
    U4, rU4 = sb("U4", [128, 4, 128])
    ones_bf, rones = sb("ones_bf", [128, 128], BF16)
    c_sb, rc = sb("c_sb", [128, KC, NB])
    sc_bf, rsc = sb("sc_bf", [128, KC, NB], BF16)
    bsel, rbsel = sb("bsel_sb", [128, NB])
    modv, rmod = sb("modv", [128, 4, 48])
    adab, radab = sb("adab", [128, 4, 48])
    bg_sb, rbg = sb("bg_sb", [128, 4, 16])
    ng_sb, rng_ = sb("ng_sb", [128, KC])
    cw_sb, rcw = sb("cw_sb", [128, KC, 4])
    bv_sb, rbv = sb("bv_sb", [128, 4, KC])
    m8, rm8 = sb("m8", [128, KC])
    m16, rm16 = sb("m16", [128, KC])
    fg_sb, rfg = sb("fg_sb", [128, KC])
    eps_sb, reps = sb("eps_sb", [128, 1])
    flag_sb, rflag = sb("flag_sb", [128, 1])
    ones32, rones32 = sb("ones32", [128, 128])
    zeros_t, rzeros = sb("zeros_t", [128, T])
    Ast, rAst = sb("Ast", [128, KC])
    h_init, rhinit = sb("h_init", [128, KC])
    x_halo, rxhalo = sb("x_halo", [128, KC, 4])
    hT_halo, rhhalo = sb("hT_halo", [128, KC, 4], BF16)
    zero_sb, rzero = sb("zero_sb", [128, 1])
    _zb.append(zero_sb)
    one_sb, rone = sb("one_sb", [128, 1])
    hb_sb, rhb = sb("hb_sb", [128, 2, KC])
    m4, rm4 = sb("m4", [128, KC])
    wg_bf, rwg = sb("wg_bf", [128, KC, 16], BF16)
    wra_bf, rwra = sb("wra_bf", [128, 8, 2, 256], BF16)
    wri_bf, rwri = sb("wri_bf", [128, 8, 2, 256], BF16)

    x_sb, rx = sb("x_sb", [128, KC, T])
    rxs = [Res(f"x{k}") for k in range(KC)]
    hT, rh = sb("hT", [128, KC, T], BF16)
    Cf, rCf = [], []
    for h in range(8):
        a, r = sb(f"Cf{h}", [128, 384])
        Cf.append(a)
        rCf.append(r)
    Cb, rCb = [], []
    for h in range(8):
        a, r = sb(f"Cb{h}", [128, 384], BF16)
        Cb.append(a)
        rCb.append(r)
    eL, reL = sb("eL", [128, 2, 8])
    hst, rhst = sb("hst", [128, KC])
    halo, rhalo = sb("halo", [128, KC, 3])

    wslot = []
    for i in range(WDEPTH):
        wslot.append(sb(f"wslot{i}", [128, KC, 512], BF16))

    SCR_BYTES = 80 * 1024
    scr_used = [0]
    scr = nc.alloc_sbuf_tensor("scr", [128, SCR_BYTES // 4], F32)
    rscr_all = Res("scr_all")

    PS = [nc.alloc_psum_tensor(f"ps{i}", [128, 512], F32) for i in range(8)]
    rPS = [Res(f"ps{i}") for i in range(8)]

    plan = []

    def wtile(src2d, r0, c0, ncols):
        return src2d[r0:r0 + D, c0:c0 + ncols].rearrange("(kc p) n -> p kc n", p=128), ncols

    plan = []
    for i in range(6):
        plan.append(wtile(ada_sl, 0, i * 512, 512))
    for _ in range(nblk):
        for i in range(2, 8):
            plan.append(wtile(a_w_in, 0, i * 512, 512))
    for _ in range(nblk):
        for i in range(12):
            plan.append(wtile(a_w_in, 0, i * 512, 512))
        for i in range(4):
            plan.append(wtile(a_w_out, 0, i * 512, 512))
        for i in range(16):
            plan.append(wtile(mlp_w1[0], 0, i * 512, 512))
        for cg in range(4):
            for kg in range(4):
                plan.append(wtile(mlp_w2[0], kg * D, cg * 512, 512))
    for _ in range(nblk):
        for i in range(4):
            plan.append(wtile(b_w_in, 0, i * 512, 512))
            plan.append(wtile(b_w_in, 0, D + i * 512, 512))
    for _ in range(nblk):
        for i in range(4):
            plan.append(wtile(b_w_out, 0, i * 512, 512))
        for i in range(16):
            plan.append(wtile(mlp_w1[1], 0, i * 512, 512))
        for cg in range(4):
            for kg in range(4):
                plan.append(wtile(mlp_w2[1], kg * D, cg * 512, 512))

    wstate = {"next_load": 0, "next_use": 0}

    def _issue_load():
        i = wstate["next_load"]
        if i >= len(plan):
            return
        src, ncols = plan[i]
        t, r = wslot[i % WDEPTH]
        fw.dma(POOL, t[:, :, 0:ncols], src, f"w{i % WDEPTH}", writes=[r])
        wstate["next_load"] = i + 1

    def wnext():
        i = wstate["next_use"]
        if i == 0:
            for _ in range(WDEPTH):
                _issue_load()
        else:
            _issue_load()
        wstate["next_use"] = i + 1
        return wslot[i % WDEPTH]

    def cload(t, r, src):
        fw.dma(SP, t, src, "const", writes=[r])
        SP.h.wait_ge(fw.dsems["const"][0], fw.dsems["const"][1])
        SP.known[id(fw.dsems["const"][0])] = fw.dsems["const"][1]

    cload(U32[:, :], rU32, triu)
    cload(c_sb[:, :, :], rc, cT.rearrange("p (a b) -> p a b", a=KC))
    cload(bsel[:, :], rbsel, bsel_d)
    cload(adab[:, :, :], radab, ada_b.rearrange("p (a b) -> p a b", a=4))
    cload(bg_sb[:, :, :], rbg, a_bg.rearrange("p (a b) -> p a b", a=4))
    cload(ng_sb[:, :], rng_, a_ng)
    cload(cw_sb[:, :, :], rcw, b_cw.rearrange("p (a b) -> p a b", a=KC))
    cload(bv_sb[:, :, :], rbv, b_vec.rearrange("p (a b) -> p a b", a=4))
    cload(fg_sb[:, :], rfg, fin_g)
    cload(flag_sb[:, :], rflag, flag_d)
    fw.dma(POOL, wg_bf[:, :, :], a_w_in[:, 6144:6160].rearrange("(kc p) n -> p kc n", p=128), "cw0", writes=[rwg])
    fw.dma(POOL, wra_bf[:, :, :, :], b_w_ra.rearrange("n (kc p) d -> p n kc d", p=128), "cw1", writes=[rwra])
    fw.dma(POOL, wri_bf[:, :, :, :], b_w_ri.rearrange("n (kc p) d -> p n kc d", p=128), "cw2", writes=[rwri])

    for j in range(4):
        fw.op(DVE, lambda j=j: nc.vector.tensor_copy(U4[:, j, :], U32[:, :]), reads=[rU32], writes=[rU4])
    fw.op(DVE, lambda: nc.vector.memset(ones_bf[:, :], 1.0), writes=[rones])
    fw.op(DVE, lambda: nc.vector.memset(eps_sb[:, :], EPS), writes=[reps])
    fw.op(DVE, lambda: nc.vector.memset(zero_sb[:, :], 0.0), writes=[rzero])
    fw.op(DVE, lambda: nc.vector.memset(ones32[:, :], 1.0), writes=[rones32])
    fw.op(DVE, lambda: nc.vector.memset(zeros_t[:, :], 0.0), writes=[rzeros])
    fw.op(DVE, lambda: nc.vector.memset(Ast[:, :], 1.0), writes=[rAst])
    fw.op(DVE, lambda: nc.vector.memset(one_sb[:, :], 1.0), writes=[rone])
    fw.op(DVE, lambda: nc.vector.tensor_scalar(hb_sb[:, :, :], bv_sb[:, 1:3, :], 0.5, None, ALU.mult), reads=[rbv], writes=[rhb])
    for h in range(8):
        fw.op(DVE, lambda h=h: nc.vector.memset(Cf[h][:, :], 0.0), writes=[rCf[h]])
        fw.op(DVE, lambda h=h: nc.vector.memset(Cb[h][:, :], 0.0), writes=[rCb[h]])
    fw.op(DVE, lambda: nc.vector.memset(eL[:, :, :], 1.0), writes=[reL])
    fw.op(DVE, lambda: nc.vector.memset(hst[:, :], 0.0), writes=[rhst])
    fw.op(DVE, lambda: nc.vector.memset(halo[:, :, :], 0.0), writes=[rhalo])
    fw.op(ACT, lambda: act_(sc_bf[:, :, :], c_sb[:, :, :], AF.Silu), reads=[rc, rzero, rone, reps], writes=[rsc])
    fw.op(ACT, lambda: act_(m8[:, :], bv_sb[:, 3, :], AF.Exp, scale=-1.0), reads=[rbv], writes=[rm8])
    fw.op(ACT, lambda: act_(m16[:, :], m8[:, :], AF.Ln, bias=one_sb[:, 0:1]), reads=[rm8], writes=[rm16])
    fw.op(DVE, lambda: nc.vector.tensor_scalar(m8[:, :], m16[:, :], -8.0, None, ALU.mult), reads=[rm16], writes=[rm8])
    fw.op(DVE, lambda: nc.vector.tensor_scalar(m4[:, :], m16[:, :], -4.0, None, ALU.mult), reads=[rm16], writes=[rm4])
    fw.op(DVE, lambda: nc.vector.tensor_scalar(m16[:, :], m16[:, :], -16.0, None, ALU.mult), reads=[rm16], writes=[rm16])

    if dbg == "mdbg":
        dbt, rdbt = sb("dbt", [128, 96])
        fw.op(ACT, lambda: act_(dbt[:, 0:16], bv_sb[:, 3, :], AF.Exp, scale=-1.0), reads=[rbv], writes=[rdbt])
        fw.op(ACT, lambda: act_(dbt[:, 16:32], dbt[:, 0:16], AF.Ln, bias=one_sb[:, 0:1]), reads=[rdbt], writes=[rdbt])
        fw.op(ACT, lambda: act_(dbt[:, 32:48], dbt[:, 0:16], AF.Ln, bias=one_sb[:, 0:1]), reads=[rdbt, rone], writes=[rdbt])
        fw.op(DVE, lambda: nc.vector.tensor_scalar(dbt[:, 48:64], dbt[:, 0:16], 1.0, None, ALU.add), reads=[rdbt], writes=[rdbt])
        fw.op(ACT, lambda: act_(dbt[:, 64:80], dbt[:, 48:64], AF.Ln), reads=[rdbt], writes=[rdbt])
        fw.op(DVE, lambda: nc.vector.tensor_copy(dbt[:, 80:96], m16[:, :]), reads=[rm16], writes=[rdbt])
        ev = fw.dma(SP, outT[0:128, 0:96], dbt[:, :], "out", reads=[rdbt])
        fw._wait(SP, [ev])
        return nc
    class Scr:
        def __init__(self):
            self.off = 0

        def take(self, name, shape, dt=F32):
            n = int(np.prod(shape[1:]))
            nbytes = n * (4 if dt == F32 else 2)
            nbytes = (nbytes + 31) // 32 * 32
            w0 = self.off // 4
            self.off += nbytes
            assert self.off <= SCR_BYTES, (name, self.off)
            v = scr[:, w0:w0 + nbytes // 4]
            if dt != F32:
                v = v.bitcast(dt)
            v = v[:, 0:n]
            if len(shape) == 3:
                v = v.rearrange("p (a b) -> p a b", a=shape[1])
            elif len(shape) == 4:
                v = v.rearrange("p (a b c) -> p a b c", a=shape[1], b=shape[2])
            return v

    phase_res = []

    def new_phase():
        prev = list(phase_res)
        phase_res.clear()
        return prev

    def adaln(cmb, x_sb=x_sb, rx=None, hT=hT, rh=rh, n=T):
        sq = s.take_tmp("sq", [128, 2, T], BF16, nres=2)
        for kc in range(KC):
            if kc % 2 == 0:
                fw.op(ACT, lambda kc=kc: act_(sq[0][:, kc % 2, 0:n], x_sb[:, kc, 0:n], AF.Square),
                      reads=[rx if rx is not None else rxs[kc]], writes=[sq[1][kc % 2]])
            else:
                fw.op(DVE, lambda kc=kc: nc.vector.tensor_tensor(sq[0][:, kc % 2, 0:n], x_sb[:, kc, 0:n], x_sb[:, kc, 0:n], ALU.mult),
                      reads=[rx if rx is not None else rxs[kc]], writes=[sq[1][kc % 2]])
            fw.mm([lambda kc=kc: mm(PS[7][:, 0:n], ones_bf[:, :], sq[0][:, kc % 2, 0:n], start=(kc == 0), stop=(kc == KC - 1))],
                  reads=[sq[1][kc % 2], rones], writes=[rPS[7]])
        rstd = s.take_tmp("rstd", [128, T])
        fw.op(ACT, lambda: act_(rstd[0][:, 0:n], PS[7][:, 0:n], AF.Sqrt, bias=eps_sb[:, 0:1], scale=1.0 / D),
              reads=[rPS[7], reps], writes=[rstd[1]])
        fw.op(DVE, lambda: nc.vector.reciprocal(rstd[0][:, 0:n], rstd[0][:, 0:n]),
              reads=[rstd[1]], writes=[rstd[1]])
        tmp = s.take_tmp("adatmp", [128, 2, T], nres=2)
        for kc in range(KC):
            fw.op(DVE, lambda kc=kc: nc.vector.scalar_tensor_tensor(
                tmp[0][:, kc % 2, 0:n], x_sb[:, kc, 0:n], modv[:, cmb, 16 + kc:17 + kc], rstd[0][:, 0:n], ALU.mult, ALU.mult),
                reads=[rx if rx is not None else rxs[kc], rmod, rstd[1]], writes=[tmp[1][kc % 2]])
            fw.op(ACT, lambda kc=kc: act_(hT[:, kc, 0:n], tmp[0][:, kc % 2, 0:n], AF.Identity,
                                                          bias=modv[:, cmb, kc:kc + 1]),
                  reads=[tmp[1][kc % 2], rmod], writes=[rh])

    class S:
        def __init__(self):
            self.scr = Scr()
            self.bufs = {}
            self.haz = {}

        def reset(self):
            for (_, res) in self.bufs.values():
                for o in (res if isinstance(res, list) else [res]):
                    evs = list(o.r.values())
                    if o.w is not None:
                        evs.append(o.w)
                    for ev in evs:
                        k = id(ev[0])
                        if k not in self.haz or self.haz[k][1] < ev[1]:
                            self.haz[k] = ev
            self.scr = Scr()
            self.bufs = {}

        def take_tmp(self, name, shape, dt=F32, nres=0):
            if name in self.bufs:
                return self.bufs[name]
            ap = self.scr.take(name, shape, dt)
            res = [Res(name + str(i)) for i in range(nres)] if nres else Res(name)
            for rr in (res if isinstance(res, list) else [res]):
                rr.r = dict(self.haz)
            self.bufs[name] = (ap, res)
            return self.bufs[name]

    s = S()

    def proj_fm(ntiles, evac):
        for i in range(ntiles):
            wt, rw = wnext()
            for fc in range(4):
                bank = (i * 4 + fc) % 4
                fw.mm([lambda kc=kc, fc=fc, wt=wt, bank=bank: mm(
                    PS[bank][:, :], wt[:, kc, fc * 128:(fc + 1) * 128], hT[:, kc, :],
                    start=(kc == 0), stop=(kc == KC - 1)) for kc in range(KC)],
                    reads=[rw, rh], writes=[rPS[bank]])
                evac(i, fc, bank)

    def resid_evac(gate_col0):
        def ev(i, fc, bank):
            f = i * 4 + fc
            fw.op(DVE, lambda: nc.vector.scalar_tensor_tensor(
                x_sb[:, f, :], PS[bank][:, :], modv[:, gate_col0[0], gate_col0[1] + f:gate_col0[1] + f + 1],
                x_sb[:, f, :], ALU.mult, ALU.add), reads=[rPS[bank], rmod, rxs[f]], writes=[rxs[f]])
        return ev

    def mlp(lyr, cmb):
        s.reset()
        adaln(cmb)
        s.reset()
        aT, raT = s.take_tmp("aT", [128, 64, T], BF16)
        rl = s.take_tmp("relu", [128, 2, T], nres=2)
        rrl = rl[1]
        cnt = [0]

        def ev1(i, fc, bank):
            f = i * 4 + fc
            k = cnt[0] % 2
            cnt[0] += 1
            fw.op(ACT, lambda: act_(rl[0][:, k, :], PS[bank][:, :], AF.Relu),
                  reads=[rPS[bank]], writes=[rrl[k]])
            fw.op(DVE, lambda: nc.vector.tensor_tensor(aT[:, f, :], rl[0][:, k, :], rl[0][:, k, :], ALU.mult),
                  reads=[rrl[k]], writes=[raT])
        proj_fm(16, ev1)
        for cg in range(4):
            for kg in range(4):
                wt, rw = wnext()
                for fc in range(4):
                    fw.mm([lambda kc=kc, fc=fc, wt=wt, kg=kg: mm(
                        PS[fc][:, :], wt[:, kc, fc * 128:(fc + 1) * 128], aT[:, kg * 16 + kc, :],
                        start=(kg == 0 and kc == 0), stop=(kg == 3 and kc == KC - 1)) for kc in range(KC)],
                        reads=[rw, raT], writes=[rPS[fc]])
            for fc in range(4):
                f = cg * 4 + fc
                fw.op(DVE, lambda f=f, fc=fc: nc.vector.scalar_tensor_tensor(
                    x_sb[:, f, :], PS[fc][:, :], modv[:, cmb, 32 + f:33 + f], x_sb[:, f, :], ALU.mult, ALU.add),
                    reads=[rPS[fc], rmod, rxs[f]], writes=[rxs[f]])

    def mlstm_layer(chunk_base, prepass=False):
        cmb = 0
        s.reset()
        adaln(cmb)
        s.reset()
        qT, rq = s.take_tmp("qT", [128, 8, T], BF16)
        kT, rk = s.take_tmp("kT", [128, 8, T], BF16)
        ktok, rkt = s.take_tmp("ktok", [128, 4, 1024], BF16)
        vaug, rv = s.take_tmp("vaug", [128, 4, 8, 256], BF16)
        yT, ry = s.take_tmp("yT", [128, KC, T], BF16)
        zs, rzs = s.take_tmp("zs", [128, 4, 16])
        nlf, rnlf = s.take_tmp("nlf", [128, 4, 8])
        ecol, recol = s.take_tmp("ecol", [128, 4, 8])
        nlfrep, rnr = s.take_tmp("nlfrep", [128, 8, 128])
        rb, rrb = s.take_tmp("rb", [128, 4, 128])
        vs, rvs = s.take_tmp("vs", [128, 4, 384], BF16)
        MT, rMT = s.take_tmp("MT", [128, 4, 128], BF16)
        den, rden = s.take_tmp("den", [128, 4, 128])
        hh, rhh = s.take_tmp("hh", [128, 2, 512])
        sqh, rsqh = s.take_tmp("sqh", [128, 2, 512], BF16)
        rsd, rrsd = s.take_tmp("rsd", [128, 512])

        for i in (range(2, 8) if prepass else range(8)):
            wt, rw = wnext()
            if i < 4 and not prepass:
                for fc in range(4):
                    bank = fc
                    fw.mm([lambda kc=kc, fc=fc, wt=wt, bank=bank: mm(
                        PS[bank][:, :], wt[:, kc, fc * 128:(fc + 1) * 128], hT[:, kc, :],
                        start=(kc == 0), stop=(kc == KC - 1)) for kc in range(KC)],
                        reads=[rw, rh], writes=[rPS[bank]])
                    hd = (i % 2) * 4 + fc
                    if i < 2:
                        fw.op(ACT, lambda hd=hd, bank=bank: act_(
                            qT[:, hd, :], PS[bank][:, :], AF.Copy, scale=float(128 ** -0.5)),
                            reads=[rPS[bank]], writes=[rq])
                    else:
                        fw.op(ACT, lambda hd=hd, bank=bank: act_(
                            kT[:, hd, :], PS[bank][:, :], AF.Copy), reads=[rPS[bank]], writes=[rk])
            if i >= 2:
                for tl in range(4):
                    bank = 4 + (tl % 2)
                    fw.mm([lambda kc=kc, tl=tl, wt=wt, bank=bank: mm(
                        PS[bank][:, :], hT[:, kc, tl * 128:(tl + 1) * 128], wt[:, kc, :],
                        start=(kc == 0), stop=(kc == KC - 1)) for kc in range(KC)],
                        reads=[rw, rh], writes=[rPS[bank]])
                    if i < 4:
                        fw.op(DVE, lambda tl=tl, bank=bank, i=i: nc.vector.tensor_copy(
                            ktok[:, tl, (i - 2) * 512:(i - 1) * 512], PS[bank][:, :]),
                            reads=[rPS[bank]], writes=[rkt])
                    else:
                        h0 = (i - 4) * 2
                        fw.op(DVE, lambda tl=tl, bank=bank, h0=h0: nc.vector.tensor_copy(
                            vaug[:, tl, h0:h0 + 2, 0:256], PS[bank][:, :].rearrange("p (a b) -> p a b", a=2)),
                            reads=[rPS[bank]], writes=[rv])
        fw.mm([lambda kc=kc, tl=tl: mm(PS[6][:, tl * 16:(tl + 1) * 16], hT[:, kc, tl * 128:(tl + 1) * 128], wg_bf[:, kc, :],
                                       start=(kc == 0), stop=(kc == KC - 1)) for tl in range(4) for kc in range(KC)],
              reads=[rh, rwg], writes=[rPS[6]])
        fw.op(DVE, lambda: nc.vector.tensor_tensor(zs[:, :, :], PS[6][:, 0:64].rearrange("p (a b) -> p a b", a=4),
                                                   bg_sb[:, :, :], ALU.add), reads=[rPS[6], rbg], writes=[rzs])
        fw.op(ACT, lambda: act_(nlf[:, :, :], zs[:, :, 8:16], AF.Exp, scale=-1.0), reads=[rzs], writes=[rnlf])
        fw.op(ACT, lambda: act_(nlf[:, :, :], nlf[:, :, :], AF.Ln, bias=one_sb[:, 0:1]), reads=[rnlf], writes=[rnlf])
        fw.mm([lambda tl=tl: mm(PS[6][:, 64 + tl * 8:72 + tl * 8], U32[:, :], nlf[:, tl, :], start=True, stop=True)
               for tl in range(4)] +
              [lambda tl=tl: mm(PS[6][:, 96 + tl * 8:104 + tl * 8], ones32[:, :], nlf[:, tl, :], start=True, stop=True)
               for tl in range(4)], reads=[rU32, rones32, rnlf], writes=[rPS[6]])
        fw.op(DVE, lambda: nc.vector.tensor_tensor(ecol[:, :, :], PS[6][:, 64:96].rearrange("p (a b) -> p a b", a=4),
                                                   zs[:, :, 0:8], ALU.add), reads=[rPS[6], rzs], writes=[recol])
        fw.op(ACT, lambda: act_(ecol[:, :, :], ecol[:, :, :], AF.Exp), reads=[recol], writes=[recol])

        for tl in range(4):
            ch = chunk_base + tl
            par = ch % 2
            tsl = slice(tl * 128, (tl + 1) * 128)
            if prepass:
                fw.op(ACT, lambda tl=tl, par=par: act_(eL[:, par, :], PS[6][:, 96 + tl * 8:104 + tl * 8], AF.Exp, scale=-1.0),
                      reads=[rPS[6]], writes=[reL])
                for hg in range(2):
                    hs = [hg * 4 + j for j in range(4)]
                    fw.op(DVE, lambda tl=tl, hg=hg: nc.vector.tensor_tensor(
                        vs[:, :, 0:256], vaug[:, tl, hg * 4:hg * 4 + 4, :],
                        ecol[:, tl, hg * 4:hg * 4 + 4].unsqueeze(2).broadcast_to([128, 4, 256]), ALU.mult),
                        reads=[rv, recol], writes=[rvs])
                    fw.op(DVE, lambda tl=tl, hg=hg: nc.vector.tensor_copy(
                        vs[:, :, 256:384], ecol[:, tl, hg * 4:hg * 4 + 4].unsqueeze(2).broadcast_to([128, 4, 128])),
                        reads=[recol], writes=[rvs])
                    for j, h in enumerate(hs):
                        bank = 4 + (j % 2)
                        fw.mm([lambda j=j, h=h, tl=tl, bank=bank: mm(
                            PS[bank][:, 0:384], ktok[:, tl, h * 128:(h + 1) * 128], vs[:, j, :], start=True, stop=True)],
                            reads=[rkt, rvs], writes=[rPS[bank]])
                        fw.op(DVE, lambda h=h, bank=bank, par=par: nc.vector.scalar_tensor_tensor(
                            Cf[h][:, :], Cf[h][:, :], eL[:, 1 - par, h:h + 1], PS[bank][:, 0:384], ALU.mult, ALU.add),
                            reads=[rCf[h], reL, rPS[bank]], writes=[rCf[h]])
                        fw.op(ACT, lambda h=h, par=par: act_(
                            Cb[h][:, :], Cf[h][:, :], AF.Copy, scale=eL[:, par, h:h + 1]),
                            reads=[rCf[h], reL], writes=[rCb[h]])
                continue
            fw.op(DVE, lambda tl=tl: nc.vector.tensor_copy(
                nlfrep[:, :, :], nlf[:, tl, :].unsqueeze(2).broadcast_to([128, 8, 128])), reads=[rnlf], writes=[rnr])
            for hg in range(2):
                hs = [hg * 4 + j for j in range(4)]
                fw.mm([lambda j=j, h=h: mm(PS[7][:, j * 128:(j + 1) * 128], nlfrep[:, h, :], U32[:, :], start=True, stop=True)
                       for j, h in enumerate(hs)], reads=[rnr, rU32], writes=[rPS[7]])
                fw.op(ACT, lambda: act_(rb[:, :, :], PS[7][:, :].rearrange("p (a b) -> p a b", a=4), AF.Exp),
                      reads=[rPS[7]], writes=[rrb])
                fw.op(ACT, lambda hg=hg, par=par: act_(
                    eL[:, par, hg * 4:hg * 4 + 4], PS[7][:, :].rearrange("p (a b) -> p a b", a=4)[:, :, 127], AF.Exp, scale=-1.0),
                    reads=[rPS[7]], writes=[reL])
                fw.op(DVE, lambda tl=tl, hg=hg: nc.vector.tensor_tensor(
                    vs[:, :, 0:256], vaug[:, tl, hg * 4:hg * 4 + 4, :],
                    ecol[:, tl, hg * 4:hg * 4 + 4].unsqueeze(2).broadcast_to([128, 4, 256]), ALU.mult),
                    reads=[rv, recol], writes=[rvs])
                fw.op(DVE, lambda tl=tl, hg=hg: nc.vector.tensor_copy(
                    vs[:, :, 256:384], ecol[:, tl, hg * 4:hg * 4 + 4].unsqueeze(2).broadcast_to([128, 4, 128])),
                    reads=[recol], writes=[rvs])
                fw.mm([lambda j=j, h=h, tsl=tsl: mm(PS[0][:, j * 128:(j + 1) * 128], kT[:, h, tsl], qT[:, h, tsl], start=True, stop=True)
                       for j, h in enumerate(hs)], reads=[rk, rq], writes=[rPS[0]])
                fw.op(DVE, lambda: nc.vector.tensor_tensor(MT[:, :, :], PS[0][:, :].rearrange("p (a b) -> p a b", a=4),
                                                           U4[:, :, :], ALU.mult), reads=[rPS[0], rU4], writes=[rMT])
                for ec in range(3):
                    fns = []
                    for j, h in enumerate(hs):
                        fns.append(lambda j=j, h=h, ec=ec, tsl=tsl: mm(
                            PS[1 + ec][:, j * 128:(j + 1) * 128], Cb[h][:, ec * 128:(ec + 1) * 128], qT[:, h, tsl],
                            start=True, stop=False))
                        fns.append(lambda j=j, h=h, ec=ec: mm(
                            PS[1 + ec][:, j * 128:(j + 1) * 128], vs[:, j, ec * 128:(ec + 1) * 128], MT[:, j, :],
                            start=False, stop=True))
                    fw.mm(fns, reads=[rCb[h] for h in hs] + [rq, rvs, rMT], writes=[rPS[1 + ec]])
                fw.op(ACT, lambda: act_(den[:, :, :], PS[3][:, :].rearrange("p (a b) -> p a b", a=4), AF.Abs),
                      reads=[rPS[3]], writes=[rden])
                fw.op(DVE, lambda: nc.vector.tensor_tensor(den[:, :, :], den[:, :, :], rb[:, :, :], ALU.max),
                      reads=[rden, rrb], writes=[rden])
                fw.op(DVE, lambda: nc.vector.reciprocal(den[:, :, :], den[:, :, :]), reads=[rden], writes=[rden])
                for ec in range(2):
                    fw.op(DVE, lambda ec=ec: nc.vector.tensor_tensor(
                        hh[:, ec, :], PS[1 + ec][:, :], den[:, :, :].rearrange("p a b -> p (a b)"), ALU.mult),
                        reads=[rPS[1 + ec], rden], writes=[rhh])
                    fw.op(ACT, lambda ec=ec: act_(sqh[:, ec, :], hh[:, ec, :], AF.Square),
                          reads=[rhh], writes=[rsqh])
                fw.mm([lambda ec=ec: mm(PS[4][:, :], ones_bf[:, :], sqh[:, ec, :], start=(ec == 0), stop=(ec == 1))
                       for ec in range(2)], reads=[rones, rsqh], writes=[rPS[4]])
                fw.op(ACT, lambda: act_(rsd[:, :], PS[4][:, :], AF.Sqrt, bias=eps_sb[:, 0:1], scale=1.0 / 256),
                      reads=[rPS[4], reps], writes=[rrsd])
                fw.op(DVE, lambda: nc.vector.reciprocal(rsd[:, :], rsd[:, :]),
                      reads=[rrsd], writes=[rrsd])
                for ec in range(2):
                    fw.op(DVE, lambda ec=ec, hg=hg, tsl=tsl: nc.vector.tensor_tensor(
                        yT[:, hg * 8 + ec:hg * 8 + 8:2, tsl], hh[:, ec, :].rearrange("p (a b) -> p a b", a=4),
                        rsd[:, :].rearrange("p (a b) -> p a b", a=4), ALU.mult), reads=[rhh, rrsd], writes=[ry])
                for j, h in enumerate(hs):
                    bank = 5 + (j % 2)
                    fw.mm([lambda j=j, h=h, tl=tl, bank=bank: mm(
                        PS[bank][:, 0:384], ktok[:, tl, h * 128:(h + 1) * 128], vs[:, j, :], start=True, stop=True)],
                        reads=[rkt, rvs], writes=[rPS[bank]])
                    fw.op(DVE, lambda h=h, bank=bank, par=par: nc.vector.scalar_tensor_tensor(
                        Cf[h][:, :], Cf[h][:, :], eL[:, 1 - par, h:h + 1], PS[bank][:, 0:384], ALU.mult, ALU.add),
                        reads=[rCf[h], reL, rPS[bank]], writes=[rCf[h]])
                    fw.op(ACT, lambda h=h, par=par: act_(
                        Cb[h][:, :], Cf[h][:, :], AF.Copy, scale=eL[:, par, h:h + 1]),
                        reads=[rCf[h], reL], writes=[rCb[h]])

        if prepass:
            return
        cnt = [0]

        def evo(i, fc, bank):
            f = i * 4 + fc
            k = cnt[0] % 2
            cnt[0] += 1
            fw.op(ACT, lambda: act_(hh[:, k, :], PS[bank][:, :], AF.Sigmoid),
                  reads=[rPS[bank]], writes=[rhh])
            fw.op(DVE, lambda: nc.vector.scalar_tensor_tensor(
                yT[:, f, :], hh[:, k, :], ng_sb[:, f:f + 1], yT[:, f, :], ALU.mult, ALU.mult),
                reads=[rhh, rng_, ry], writes=[ry])
        proj_fm(4, evo)
        for i in range(4):
            wt, rw = wnext()
            for fc in range(4):
                bank = fc
                f = i * 4 + fc
                fw.mm([lambda kc=kc, fc=fc, wt=wt, bank=bank: mm(
                    PS[bank][:, :], wt[:, kc, fc * 128:(fc + 1) * 128], yT[:, kc, :],
                    start=(kc == 0), stop=(kc == KC - 1)) for kc in range(KC)],
                    reads=[rw, ry], writes=[rPS[bank]])
                fw.op(DVE, lambda f=f, bank=bank: nc.vector.scalar_tensor_tensor(
                    x_sb[:, f, :], PS[bank][:, :], modv[:, cmb, 32 + f:33 + f], x_sb[:, f, :], ALU.mult, ALU.add),
                    reads=[rPS[bank], rmod, rxs[f]], writes=[rxs[f]])

    def rglru_front(blk):
        cmb = 2
        s.reset()
        adaln(cmb)
        if blk == 0:
            adaln(cmb, x_sb=x_halo, rx=rxhalo, hT=hT_halo, rh=rhhalo, n=4)
        s.reset()
        ysg = s.take_tmp("ysg", [128, 2, T], nres=2)
        asg = s.take_tmp("asg", [128, 2, T], nres=2)
        acum, racum = s.take_tmp("acum", [128, T])
        xbp, rxbp = s.take_tmp("xbp", [128, 4, T + 3])
        xc, rxc = s.take_tmp("xc", [128, 4, T])
        xcb, rxcb = s.take_tmp("xcb", [128, 4, T], BF16)
        gg2 = [s.take_tmp(f"gg{i}", [128, 4, T], BF16) for i in range(2)]
        tm = [s.take_tmp(f"t{i}", [128, T]) for i in range(4)]
        ta = [s.take_tmp(f"ta{i}", [128, T]) for i in range(4)]
        ta2 = [s.take_tmp(f"tb{i}", [128, T]) for i in range(4)]
        tix = [s.take_tmp(f"tc{i}", [128, T]) for i in range(4)]
        for i in range(4):
            gg, rgg = gg2[i % 2]
            wt, rw = wnext()
            for fc in range(4):
                c = i * 4 + fc
                bank = fc
                if blk == 0:
                    fw.mm([lambda kc=kc, fc=fc, wt=wt, bank=bank: mm(
                        PS[bank][:, 0:4], wt[:, kc, fc * 128:(fc + 1) * 128], hT_halo[:, kc, :],
                        start=(kc == 0), stop=(kc == KC - 1)) for kc in range(KC)],
                        reads=[rw, rhhalo], writes=[rPS[bank]])
                    fw.op(ACT, lambda c=c, bank=bank: act_(halo[:, c, :], PS[bank][:, 1:4], AF.Copy, scale=flag_sb[:, 0:1]),
                          reads=[rPS[bank], rflag], writes=[rhalo])
                fw.mm([lambda kc=kc, fc=fc, wt=wt, bank=bank: mm(
                    PS[bank][:, :], wt[:, kc, fc * 128:(fc + 1) * 128], hT[:, kc, :],
                    start=(kc == 0), stop=(kc == KC - 1)) for kc in range(KC)],
                    reads=[rw, rh], writes=[rPS[bank]])
                fw.op(ACT, lambda fc=fc, c=c: act_(xbp[:, fc, 0:3], halo[:, c, :], AF.Copy),
                      reads=[rhalo], writes=[rxbp])
                fw.op(ACT, lambda fc=fc, bank=bank: act_(xbp[:, fc, 3:T + 3], PS[bank][:, :], AF.Copy),
                      reads=[rPS[bank]], writes=[rxbp])
                fw.op(ACT, lambda fc=fc, c=c: act_(halo[:, c, :], xbp[:, fc, T:T + 3], AF.Copy),
                      reads=[rxbp], writes=[rhalo])
                fw.op(ACT, lambda fc=fc, c=c: act_(
                    xc[:, fc, :], xbp[:, fc, 3:T + 3], AF.Identity, bias=bv_sb[:, 0, c:c + 1], scale=cw_sb[:, c, 3:4]),
                    reads=[rxbp, rbv, rcw], writes=[rxc])
                for k in range(3):
                    fw.op(DVE, lambda fc=fc, c=c, k=k: nc.vector.scalar_tensor_tensor(
                        xc[:, fc, :], xbp[:, fc, k:k + T], cw_sb[:, c, k:k + 1], xc[:, fc, :], ALU.mult, ALU.add),
                        reads=[rxbp, rcw, rxc], writes=[rxc])
                fw.op(ACT, lambda fc=fc: act_(xcb[:, fc, :], xc[:, fc, :], AF.Copy),
                      reads=[rxc], writes=[rxcb])
            wt, rw = wnext()
            for fc in range(4):
                bank = 4 + (fc % 2)
                fw.mm([lambda kc=kc, fc=fc, wt=wt, bank=bank: mm(
                    PS[bank][:, :], wt[:, kc, fc * 128:(fc + 1) * 128], hT[:, kc, :],
                    start=(kc == 0), stop=(kc == KC - 1)) for kc in range(KC)],
                    reads=[rw, rh], writes=[rPS[bank]])
                z2, rz2 = tm[0]
                fw.op(ACT, lambda bank=bank, z2=z2: act_(z2[:, :], PS[bank][:, :], AF.Square),
                      reads=[rPS[bank]], writes=[rz2])
                fw.op(DVE, lambda z2=z2: nc.vector.tensor_scalar(z2[:, :], z2[:, :], 0.044715, 1.0, ALU.mult, ALU.add),
                      reads=[rz2], writes=[rz2])
                fw.op(DVE, lambda bank=bank, z2=z2: nc.vector.tensor_tensor(z2[:, :], z2[:, :], PS[bank][:, :], ALU.mult),
                      reads=[rz2, rPS[bank]], writes=[rz2])
                fw.op(ACT, lambda z2=z2: act_(z2[:, :], z2[:, :], AF.Tanh, scale=0.7978845608028654),
                      reads=[rz2], writes=[rz2])
                fw.op(DVE, lambda fc=fc, bank=bank, z2=z2: nc.vector.scalar_tensor_tensor(
                    gg[:, fc, :], z2[:, :], 1.0, PS[bank][:, :], ALU.add, ALU.mult),
                    reads=[rz2, rPS[bank]], writes=[rgg])
            for fc in range(4):
                c = i * 4 + fc
                n = c // 2
                m = c % 2
                kl = (fc // 2) * 2
                fw.mm([lambda kk=kk, n=n, m=m, kl=kl: mm(PS[6][:, :], wra_bf[:, n, kk, m * 128:(m + 1) * 128], xcb[:, kl + kk, :],
                                                         start=(kk == 0), stop=(kk == 1)) for kk in range(2)],
                      reads=[rwra, rxcb], writes=[rPS[6]])
                fw.mm([lambda kk=kk, n=n, m=m, kl=kl: mm(PS[7][:, :], wri_bf[:, n, kk, m * 128:(m + 1) * 128], xcb[:, kl + kk, :],
                                                         start=(kk == 0), stop=(kk == 1)) for kk in range(2)],
                      reads=[rwri, rxcb], writes=[rPS[7]])
                (r_, rr_), (i_, ri_) = tm[1], tm[2]
                a_, ra_ = ta[fc]
                a2, ra2 = ta2[fc]
                ix, rix = tix[fc]
                fw.op(ACT, lambda c=c: act_(r_[:, :], PS[6][:, :], AF.Tanh, bias=hb_sb[:, 0, c:c + 1], scale=0.5),
                      reads=[rPS[6], rhb], writes=[rr_])
                fw.op(ACT, lambda c=c: act_(i_[:, :], PS[7][:, :], AF.Tanh, bias=hb_sb[:, 1, c:c + 1], scale=0.5),
                      reads=[rPS[7], rhb], writes=[ri_])
                fw.op(ACT, lambda c=c, a_=a_: act_(a_[:, :], r_[:, :], AF.Exp, bias=m4[:, c:c + 1], scale=m4[:, c:c + 1]),
                      reads=[rr_, rm4], writes=[ra_])
                fw.op(ACT, lambda c=c, a2=a2: act_(a2[:, :], r_[:, :], AF.Exp, bias=m8[:, c:c + 1], scale=m8[:, c:c + 1]),
                      reads=[rr_, rm8], writes=[ra2])
                fw.op(DVE, lambda fc=fc, ix=ix: nc.vector.scalar_tensor_tensor(ix[:, :], i_[:, :], 1.0, xc[:, fc, :], ALU.add, ALU.mult),
                      reads=[ri_, rxc], writes=[rix])
            for fc in range(4):
                a2, ra2 = ta2[fc]
                fw.op(ACT, lambda a2=a2: act_(a2[:, :], a2[:, :], AF.Sqrt, bias=one_sb[:, 0:1], scale=-1.0),
                      reads=[ra2, rone], writes=[ra2])
            for fc in range(4):
                c = i * 4 + fc
                a_, ra_ = ta[fc]
                a2, ra2 = ta2[fc]
                ix, rix = tix[fc]
                hs_, rhs_ = tm[3]
                fw.op(DVE, lambda a2=a2, ix=ix: nc.vector.scalar_tensor_tensor(ix[:, :], a2[:, :], 0.5, ix[:, :], ALU.mult, ALU.mult),
                      reads=[ra2, rix], writes=[rix])
                fw.op(DVE, lambda c=c, a_=a_, ix=ix: nc.vector.tensor_tensor_scan(hs_[:, :], a_[:, :], ix[:, :], hst[:, c:c + 1], ALU.mult, ALU.add),
                      reads=[ra_, rix, rhst], writes=[rhs_])
                fw.op(ACT, lambda c=c: act_(hst[:, c:c + 1], hs_[:, T - 1:T], AF.Copy),
                      reads=[rhs_], writes=[rhst])
                fw.op(DVE, lambda c=c, a_=a_: nc.vector.tensor_tensor_scan(acum[:, :], a_[:, :], zeros_t[:, :], Ast[:, c:c + 1], ALU.mult, ALU.add),
                      reads=[ra_, rzeros, rAst], writes=[racum])
                fw.op(ACT, lambda c=c: act_(Ast[:, c:c + 1], acum[:, T - 1:T], AF.Copy), reads=[racum], writes=[rAst])
                k = c % 2
                fw.op(DVE, lambda k=k, fc=fc: nc.vector.scalar_tensor_tensor(ysg[0][:, k, :], hs_[:, :], 0.5, gg[:, fc, :], ALU.mult, ALU.mult),
                      reads=[rhs_, rgg], writes=[ysg[1][k]])
                fw.op(DVE, lambda k=k, fc=fc: nc.vector.scalar_tensor_tensor(asg[0][:, k, :], acum[:, :], 0.5, gg[:, fc, :], ALU.mult, ALU.mult),
                      reads=[racum, rgg], writes=[asg[1][k]])
                fw.dma(SP, y0T[c * 128:(c + 1) * 128, blk * T:(blk + 1) * T], ysg[0][:, k, :], f"ys{k}", reads=[ysg[1][k]], writes=[ry0[blk]])
                fw.dma(SP, AgT[c * 128:(c + 1) * 128, blk * T:(blk + 1) * T], asg[0][:, k, :], f"as{k}", reads=[asg[1][k]], writes=[rAg[blk]])

    def rglru_back(blk):
        cmb = 2
        s.reset()
        yT, ry = s.take_tmp("yT", [128, KC, T], BF16)
        ly = s.take_tmp("ly", [128, 2, 4, T], nres=2)
        la = s.take_tmp("la", [128, 2, 4, T], nres=2)
        for q in range(4):
            k = q % 2
            fw.dma(SP, ly[0][:, k, :, :], y0T[q * 512:(q + 1) * 512, blk * T:(blk + 1) * T].rearrange("(c p) t -> p c t", p=128),
                   f"ly{k}", reads=[ry0[blk]], writes=[ly[1][k]])
            fw.dma(SP, la[0][:, k, :, :], AgT[q * 512:(q + 1) * 512, blk * T:(blk + 1) * T].rearrange("(c p) t -> p c t", p=128),
                   f"la{k}", reads=[rAg[blk]], writes=[la[1][k]])
            for cc in range(4):
                c = q * 4 + cc
                fw.op(DVE, lambda k=k, cc=cc, c=c: nc.vector.scalar_tensor_tensor(
                    yT[:, c, :], la[0][:, k, cc, :], h_init[:, c:c + 1], ly[0][:, k, cc, :], ALU.mult, ALU.add),
                    reads=[la[1][k], ly[1][k], rhinit], writes=[ry])
        for i in range(4):
            wt, rw = wnext()
            for fc in range(4):
                bank = fc
                f = i * 4 + fc
                fw.mm([lambda kc=kc, fc=fc, wt=wt, bank=bank: mm(
                    PS[bank][:, :], wt[:, kc, fc * 128:(fc + 1) * 128], yT[:, kc, :],
                    start=(kc == 0), stop=(kc == KC - 1)) for kc in range(KC)],
                    reads=[rw, ry], writes=[rPS[bank]])
                fw.op(DVE, lambda f=f, bank=bank: nc.vector.scalar_tensor_tensor(
                    x_sb[:, f, :], PS[bank][:, :], modv[:, cmb, 32 + f:33 + f], x_sb[:, f, :], ALU.mult, ALU.add),
                    reads=[rPS[bank], rmod, rxs[f]], writes=[rxs[f]])

    out_evs = []

    def final_norm_store(blk):
        s.reset()
        sq = s.take_tmp("sq", [128, 2, T], BF16, nres=2)
        for kc in range(KC):
            fw.op(ACT, lambda kc=kc: act_(sq[0][:, kc % 2, :], x_sb[:, kc, :], AF.Square),
                  reads=[rxs[kc]], writes=[sq[1][kc % 2]])
            fw.mm([lambda kc=kc: mm(PS[7][:, :], ones_bf[:, :], sq[0][:, kc % 2, :], start=(kc == 0), stop=(kc == KC - 1))],
                  reads=[sq[1][kc % 2], rones], writes=[rPS[7]])
        rstd = s.take_tmp("rstd", [128, T])
        fw.op(ACT, lambda: act_(rstd[0][:, :], PS[7][:, :], AF.Sqrt, bias=eps_sb[:, 0:1], scale=1.0 / D),
              reads=[rPS[7], reps], writes=[rstd[1]])
        fw.op(DVE, lambda: nc.vector.reciprocal(rstd[0][:, :], rstd[0][:, :]),
              reads=[rstd[1]], writes=[rstd[1]])
        ob, rob = s.take_tmp("ob", [128, KC, T])
        for kc in range(KC):
            fw.op(DVE, lambda kc=kc: nc.vector.scalar_tensor_tensor(
                ob[:, kc, :], x_sb[:, kc, :], fg_sb[:, kc:kc + 1], rstd[0][:, :], ALU.mult, ALU.mult),
                reads=[rxs[kc], rfg, rstd[1]], writes=[rob])
        ev = fw.dma(SP, outT[:, blk * T:(blk + 1) * T].rearrange("(kc p) t -> p kc t", p=128), ob[:, :, :], "out", reads=[rob])
        out_evs.append(ev)

    def dbg_store():
        ev = fw.dma(SP, outT[:, 0:T].rearrange("(kc p) t -> p kc t", p=128), x_sb[:, :, :], "out", reads=rxs)
        out_evs.append(ev)

    groups = [[2 * i, 2 * i + 1] for i in range(ncores // 2)]
    cc_n = [0]

    def allgather(src_t, rsrc, dst_t, rdst, groups=groups):
        if stub_cc:
            fw.dma(POOL, dst_t[0:128, :], src_t[:, :], "ccstub", reads=[rsrc], writes=[rdst])
            return
        sem = nc.alloc_semaphore(name=f"cc{cc_n[0]}")
        cc_n[0] += 1
        fw._wait(POOL, fw._deps([rsrc], [rdst]))
        nc.gpsimd.collective_compute("AllGather", ALU.bypass, replica_groups=groups,
                                     ins=[src_t.ap()], outs=[dst_t.ap()]).then_inc(sem, 1)
        fw._record((sem, 1, None), [rsrc], [rdst])

    def xload(src, blk):
        rd = [rx2[blk]] if src is x2T else []
        fw.dma(SP, x_sb[:, :, :], src[:, blk * T:(blk + 1) * T].rearrange("(kc p) t -> p kc t", p=128), "xin",
               reads=rd, writes=rxs)

    s.reset()
    modloc, rmodloc = s.take_tmp("modloc", [128, 96])
    modall, rmodall = s.take_tmp("modall", [128, 192, NB])
    for i in range(6):
        wt, rw = wnext()
        fns = []
        for fc in range(4):
            j = i * 4 + fc
            for kc in range(KC):
                fns.append(lambda fc=fc, kc=kc, j=j, wt=wt: mm(
                    PS[0][:, j * NB:(j + 1) * NB], wt[:, kc, fc * 128:(fc + 1) * 128], sc_bf[:, kc, :],
                    start=(kc == 0), stop=(kc == KC - 1)))
        fw.mm(fns, reads=[rw, rsc], writes=[rPS[0]])
    fw.op(DVE, lambda: nc.vector.tensor_copy(modloc[:, :], PS[0][:, 0:96]), reads=[rPS[0]], writes=[rmodloc])
    fw.dma(SP, mod_src[:, :], modloc[:, :], "modout", reads=[rmodloc], writes=[rmod_src])
    allgather(mod_src, rmod_src, mod_q, rmod_q, groups=[[0, 1, 2, 3], [4, 5, 6, 7]])
    allgather(mod_q, rmod_q, mod_all, rmod_all, groups=[[0, 4], [1, 5], [2, 6], [3, 7]])
    fw.dma(SP, modall[:, :, :].rearrange("p (r j) b -> p r (j b)", r=8), mod_all[:, :].rearrange("(r p) q -> p r q", p=128),
           "modin", reads=[rmod_all], writes=[rmodall])
    mv = modv[:, :, :].rearrange("p a b -> p (a b)")
    fw.op(DVE, lambda: nc.vector.tensor_scalar(mv, modall[:, :, 0], bsel[:, 0:1], None, ALU.mult),
          reads=[rmodall, rbsel], writes=[rmod])
    for bb in range(1, NB):
        fw.op(DVE, lambda bb=bb: nc.vector.scalar_tensor_tensor(mv, modall[:, :, bb], bsel[:, bb:bb + 1], mv, ALU.mult, ALU.add),
              reads=[rmodall, rbsel, rmod], writes=[rmod])
    fw.op(DVE, lambda: nc.vector.tensor_tensor(mv, mv, adab[:, :, :].rearrange("p a b -> p (a b)"), ALU.add),
          reads=[rmod, radab], writes=[rmod])
    for cmb in range(4):
        fw.op(DVE, lambda cmb=cmb: nc.vector.tensor_scalar(modv[:, cmb, 16:32], modv[:, cmb, 16:32], 1.0, None, ALU.add),
              reads=[rmod], writes=[rmod])
    for pj in range(nblk):
        xload(xpre, pj)
        mlstm_layer(pj * 4, prepass=True)
    for h in range(8):
        fw.op(DVE, lambda h=h: nc.vector.tensor_scalar(Cf[h][:, :], Cf[h][:, :], flag_sb[:, 0:1], None, ALU.mult),
              reads=[rCf[h], rflag], writes=[rCf[h]])
        fw.op(DVE, lambda h=h: nc.vector.tensor_scalar(Cb[h][:, :], Cb[h][:, :], flag_sb[:, 0:1], None, ALU.mult),
              reads=[rCb[h], rflag], writes=[rCb[h]])
    for blk in range(nblk):
        xload(xT, blk)
        mlstm_layer(16 + blk * 4)
        mlp(0, 1)
        fw.dma(SP, x2T[:, blk * T:(blk + 1) * T].rearrange("(kc p) t -> p kc t", p=128), x_sb[:, :, :], "x2s",
               reads=rxs, writes=[rx2[blk]])
        if blk == nblk - 1:
            fw.dma(SP, halo_src[:, :].rearrange("p (kc t) -> p kc t", kc=KC), x_sb[:, :, T - 4:T], "halo",
                   reads=rxs, writes=[rhalo_src])
    allgather(halo_src, rhalo_src, halo_all, rhalo_all)
    fw.dma(SP, x_halo[:, :, :], halo_all[0:128, :].rearrange("p (kc t) -> p kc t", kc=KC), "haloin",
           reads=[rhalo_all], writes=[rxhalo])
    for blk in range(nblk):
        xload(x2T, blk)
        rglru_front(blk)
    fw.dma(SP, hend_src[:, :], hst[:, :], "hend", reads=[rhst], writes=[rhend_src])
    allgather(hend_src, rhend_src, hend_all, rhend_all)
    fw.dma(SP, h_init[:, :], hend_all[0:128, :], "hendin", reads=[rhend_all], writes=[rhinit])
    fw.op(DVE, lambda: nc.vector.tensor_scalar(h_init[:, :], h_init[:, :], flag_sb[:, 0:1], None, ALU.mult),
          reads=[rhinit, rflag], writes=[rhinit])
    for blk in range(nblk):
        xload(x2T, blk)
        rglru_back(blk)
        mlp(1, 3)
        final_norm_store(blk)

    if dbg is None:
        assert wstate["next_use"] == len(plan), (wstate, len(plan))
    fw._wait(SP, out_evs)
    return nc


def _prep_inputs(inp, b):
    f32 = np.float32
    x = inp["x"]

    def fm(v):
        return np.ascontiguousarray(np.asarray(v, f32).reshape(KC, 128).T)

    m = {}
    m["cT"] = np.ascontiguousarray(np.stack([fm(inp["c"][bb]) for bb in range(NB)], axis=2).reshape(128, KC * NB))
    bs = np.zeros((128, NB), f32)
    bs[:, b] = 1.0
    m["bsel"] = bs
    m["ada_b"] = np.ascontiguousarray(inp["ada_b"].reshape(4, 48, 128).transpose(2, 0, 1).reshape(128, 4 * 48))
    m["a_w_in"] = np.ascontiguousarray(inp["a_w_in"][0])
    bg = inp["a_b_gate"][0].reshape(16)
    m["a_bg"] = np.ascontiguousarray(np.broadcast_to(np.tile(bg, 4)[None, :], (128, 64))).astype(f32)
    m["a_ng"] = fm(inp["a_norm_g"][0])
    m["a_w_out"] = np.ascontiguousarray(inp["a_w_out"][0])
    m["b_w_in"] = np.ascontiguousarray(inp["b_w_in"][0])
    cw = inp["b_conv_w"][0]
    m["b_cw"] = np.ascontiguousarray(cw.T.reshape(KC, 128, 4).transpose(1, 0, 2).reshape(128, KC * 4))
    m["b_vec"] = np.ascontiguousarray(np.concatenate(
        [fm(inp["b_conv_b"][0]), fm(inp["b_b_ra"][0]), fm(inp["b_b_ri"][0]), fm(inp["b_lam"][0])], axis=1))
    m["b_w_ra"] = np.ascontiguousarray(inp["b_w_ra"][0])
    m["b_w_ri"] = np.ascontiguousarray(inp["b_w_ri"][0])
    m["b_w_out"] = np.ascontiguousarray(inp["b_w_out"][0])
    m["mlp_w1"] = np.ascontiguousarray(inp["mlp_w1"])
    m["mlp_w2"] = np.ascontiguousarray(inp["mlp_w2"])
    m["fin_g"] = fm(inp["final_g"])
    m["triu"] = np.triu(np.ones((128, 128), f32))
    return m


_NC_CACHE = {}


def kernel(**inputs):
    inp = {k: np.asarray(v, dtype=np.float32) for k, v in inputs.items()}
    if "full" not in _NC_CACHE:
        _NC_CACHE["full"] = build()
    nc = _NC_CACHE["full"]
    in_maps = []
    for core in range(NCORES):
        b, half = core // 2, core % 2
        m = _prep_inputs(inp, b)
        xb = inp["x"][b]
        m["xT"] = np.ascontiguousarray(xb[half * TOK:(half + 1) * TOK].T)
        m["xpre"] = np.ascontiguousarray(xb[0:TOK].T)
        m["flag"] = np.full((128, 1), float(half), np.float32)
        aw = inp["ada_w"].reshape(4, D, 3 * D)
        m["ada_sl"] = np.ascontiguousarray(aw[core // 2][:, (core % 2) * 3072:(core % 2 + 1) * 3072])
        in_maps.append(m)
    res = run_bass_kernel_spmd(nc, in_maps, core_ids=list(range(NCORES)))
    out = np.empty((NB, SEQ, D), np.float32)
    for core in range(NCORES):
        b, half = core // 2, core % 2
        out[b, half * TOK:(half + 1) * TOK, :] = res.results[core]["outT"].T
    return out
```
